# Optimizing a Trainium2 kernel written in Bass

```python
import math, functools
import jax, jax.numpy as jnp
from jax import lax
import numpy as np

D_MODEL = 2048
BATCH = 4
SEQ = 8192
DEPTH = 2
DEC_BATCH = 32
DEC_SEQ = 16
PAST_LEN = 1024

CHUNK = 64
WINDOW = 128
WIN_CHUNKS = WINDOW // CHUNK
BAND = (WIN_CHUNKS + 1) * CHUNK
SWA_KEEP = min(WINDOW, PAST_LEN)
HEAD_DIM = 64
Q_HEADS = 16
KV_HEADS = 4
Q_PER_KV = Q_HEADS // KV_HEADS
ATTN_WIDTH = Q_HEADS * HEAD_DIM
KV_WIDTH = KV_HEADS * HEAD_DIM
ATTN_SCALE = HEAD_DIM ** -0.5
ROPE_THETA = 10000.0
SSM_HEADS = 16
SSM_HEAD_DIM = 64
SSM_INNER = SSM_HEADS * SSM_HEAD_DIM
SSM_GROUPS = 4
SSM_HEADS_PER_GROUP = SSM_HEADS // SSM_GROUPS
SSM_STATE = 128
SSM_GN = SSM_GROUPS * SSM_STATE
SSM_CONV = 4
SSM_CONV_DIM = SSM_INNER + 2 * SSM_GN
SSD_CHUNK = 128
Q_END = ATTN_WIDTH
K_END = Q_END + KV_WIDTH
V_END = K_END + KV_WIDTH
Z_END = V_END + SSM_INNER
XBC_END = Z_END + SSM_CONV_DIM
IN_WIDTH = XBC_END + SSM_HEADS
MIX_WIDTH = ATTN_WIDTH + SSM_INNER
CONF_DIM = D_MODEL
CONF_KERNEL = 31
D_FF = 4 * D_MODEL
N_EVEN = (DEPTH + 1) // 2
N_ODD = DEPTH // 2
EPS = 1e-6
F32 = jnp.float32

kernel_name = 'hybrid_stream_swa_ssd_conformer'


def rms_norm(x, g):
    xf = x.astype(F32)
    y = xf * lax.rsqrt(jnp.mean(xf * xf, axis=-1, keepdims=True) + EPS)
    return (y * g.astype(F32)).astype(x.dtype)


def layer_norm(x, g, b):
    xf = x.astype(F32)
    mu = jnp.mean(xf, axis=-1, keepdims=True)
    xc = xf - mu
    var = jnp.mean(xc * xc, axis=-1, keepdims=True)
    return (xc * lax.rsqrt(var + EPS) * g.astype(F32) + b.astype(F32)).astype(x.dtype)


def adaln(c, w, b):
    m = (jax.nn.silu(c) @ w + b)[:, None, :]
    return jnp.split(m, 6, axis=-1)


def rope(x, pos):
    half = HEAD_DIM // 2
    inv = ROPE_THETA ** (-jnp.arange(half, dtype=F32) / half)
    ang = pos.astype(F32)[:, None] * inv[None, :]
    shp = (1, x.shape[1]) + (1,) * (x.ndim - 3) + (half,)
    cos = jnp.cos(ang).reshape(shp)
    sin = jnp.sin(ang).reshape(shp)
    xf = x.astype(F32)
    x1, x2 = xf[..., :half], xf[..., half:]
    return jnp.concatenate([x1 * cos - x2 * sin, x2 * cos + x1 * sin], axis=-1).astype(x.dtype)


def sink_probs(s, sinks):
    sk = sinks.astype(F32)[:, :, None]
    m = jnp.maximum(jnp.max(s, axis=-1), sk)
    p = jnp.exp(s - m[..., None])
    return p / (jnp.sum(p, axis=-1) + jnp.exp(sk - m))[..., None]


def swa_prompt(q, k, v, sinks):
    bsz, T = q.shape[0], q.shape[1]
    nb = T // CHUNK
    pad = WIN_CHUNKS * CHUNK

    def band(t):
        tp = jnp.pad(t, ((0, 0), (pad, 0), (0, 0), (0, 0))).reshape(bsz, nb + WIN_CHUNKS, CHUNK, KV_HEADS, HEAD_DIM)
        return jnp.concatenate([tp[:, j:j + nb] for j in range(WIN_CHUNKS + 1)], axis=2)

    kb, vb = band(k), band(v)
    qb = q.reshape(bsz, nb, CHUNK, KV_HEADS, Q_PER_KV, HEAD_DIM)
    s = jnp.einsum('bnqkgd,bnskd->bnkgqs', qb, kb, preferred_element_type=F32) * ATTN_SCALE
    key_pos = (jnp.arange(nb) * CHUNK)[:, None] + jnp.arange(BAND)[None, :] - pad
    s = jnp.where((key_pos >= 0)[None, :, None, None, None, :], s, -jnp.inf)
    pr = sink_probs(s, sinks).astype(v.dtype)
    o = jnp.einsum('bnkgqs,bnskd->bnqkgd', pr, vb)
    return o.reshape(bsz, T, KV_HEADS, Q_PER_KV, HEAD_DIM)


def swa_sample(q, k, v, sinks, cache_k, cache_v):
    kall = jnp.concatenate([cache_k.astype(k.dtype), k], axis=1)
    vall = jnp.concatenate([cache_v.astype(v.dtype), v], axis=1)
    s = jnp.einsum('bqkgd,bskd->bkgqs', q, kall, preferred_element_type=F32) * ATTN_SCALE
    pr = sink_probs(s, sinks).astype(v.dtype)
    return jnp.einsum('bkgqs,bskd->bqkgd', pr, vall)


def causal_dwconv(x, ctx, w, b):
    xp = jnp.concatenate([ctx.astype(x.dtype), x], axis=1)
    y = lax.conv_general_dilated(xp, w.astype(x.dtype)[:, None, :], window_strides=(1,), padding='VALID',
                                 dimension_numbers=('NWC', 'WIO', 'NWC'), feature_group_count=x.shape[-1])
    return y + b.astype(x.dtype), xp[:, xp.shape[1] - (w.shape[0] - 1):]


def ssd_scan(x, dt, a, bm, cm, h0, chunk):
    bsz, T, G, R, P = x.shape
    N = bm.shape[-1]
    nc = T // chunk
    xc = x.astype(F32).reshape(bsz, nc, chunk, G, R, P)
    bc = bm.astype(F32).reshape(bsz, nc, chunk, G, N)
    cc = cm.astype(F32).reshape(bsz, nc, chunk, G, N)
    dtc = dt.reshape(bsz, nc, chunk, G, R)
    acum = jnp.cumsum(dtc * a, axis=2)
    causal = jnp.tril(jnp.ones((chunk, chunk), bool))[:, :, None, None]
    seg = acum[:, :, :, None] - acum[:, :, None, :]
    decay = jnp.exp(jnp.where(causal, seg, -jnp.inf))
    cb = jnp.einsum('bclgn,bcsgn->bclsg', cc, bc)
    xdt = xc * dtc[..., None]
    y_diag = jnp.einsum('bclsg,bclsgr,bcsgrp->bclgrp', cb, decay, xdt)
    to_end = jnp.exp(acum[:, :, -1:] - acum)
    states = jnp.einsum('bclgn,bclgrp->bcgrpn', bc, xdt * to_end[..., None])
    chunk_decay = jnp.exp(acum[:, :, -1])

    def step(h, inp):
        st, dec = inp
        return h * dec[..., None, None] + st, h

    h_t, h_in = lax.scan(step, h0.astype(F32), (jnp.moveaxis(states, 1, 0), jnp.moveaxis(chunk_decay, 1, 0)))
    h_in = jnp.moveaxis(h_in, 0, 1)
    y_off = jnp.einsum('bclgn,bcgrpn->bclgrp', cc, h_in) * jnp.exp(acum)[..., None]
    return (y_diag + y_off).reshape(bsz, T, G, R, P), h_t


def gated_group_rms(y, z, g):
    bsz, T, _ = y.shape
    u = (y.astype(F32) * jax.nn.silu(z.astype(F32))).reshape(bsz, T, SSM_GROUPS, SSM_INNER // SSM_GROUPS)
    u = u * lax.rsqrt(jnp.mean(u * u, axis=-1, keepdims=True) + EPS)
    return u.reshape(bsz, T, SSM_INNER) * g.astype(F32)


def attn_ssm_mixer(h, pos, attend, conv_ctx, h0, ssd_chunk, w_in, w_out, sinks, a_log, dt_bias, d_skip, conv_w, conv_b, norm_g):
    bsz, T, _ = h.shape
    proj = h @ w_in
    q = rope(proj[..., :Q_END].reshape(bsz, T, KV_HEADS, Q_PER_KV, HEAD_DIM), pos)
    k = rope(proj[..., Q_END:K_END].reshape(bsz, T, KV_HEADS, HEAD_DIM), pos)
    v = proj[..., K_END:V_END].reshape(bsz, T, KV_HEADS, HEAD_DIM)
    z = proj[..., V_END:Z_END]
    xbc = proj[..., Z_END:XBC_END]
    dt = proj[..., XBC_END:]
    o_attn = attend(q, k, v, sinks.reshape(KV_HEADS, Q_PER_KV)).reshape(bsz, T, ATTN_WIDTH)
    xbc, conv_state = causal_dwconv(xbc, conv_ctx, conv_w, conv_b)
    xbc = jax.nn.silu(xbc)
    xs = xbc[..., :SSM_INNER].reshape(bsz, T, SSM_GROUPS, SSM_HEADS_PER_GROUP, SSM_HEAD_DIM)
    bm = xbc[..., SSM_INNER:SSM_INNER + SSM_GN].reshape(bsz, T, SSM_GROUPS, SSM_STATE)
    cm = xbc[..., SSM_INNER + SSM_GN:].reshape(bsz, T, SSM_GROUPS, SSM_STATE)
    dt = jax.nn.softplus(dt.astype(F32) + dt_bias.astype(F32)).reshape(bsz, T, SSM_GROUPS, SSM_HEADS_PER_GROUP)
    a = -jnp.exp(a_log.astype(F32)).reshape(SSM_GROUPS, SSM_HEADS_PER_GROUP)
    y, h_t = ssd_scan(xs, dt, a, bm, cm, h0, ssd_chunk)
    y = y + d_skip.astype(F32).reshape(SSM_GROUPS, SSM_HEADS_PER_GROUP)[:, :, None] * xs.astype(F32)
    y = gated_group_rms(y.reshape(bsz, T, SSM_INNER), z, norm_g).astype(h.dtype)
    out = jnp.concatenate([o_attn, y], axis=-1) @ w_out
    return out, k, v, h_t.reshape(bsz, SSM_HEADS, SSM_HEAD_DIM, SSM_STATE), conv_state


def conformer_conv(h, ctx, w1, b1, dw_w, dw_b, ln_g, ln_b, w2, b2):
    u = h @ w1 + b1
    u = u[..., :CONF_DIM] * jax.nn.sigmoid(u[..., CONF_DIM:])
    u, new_ctx = causal_dwconv(u, ctx, dw_w, dw_b)
    u = jax.nn.silu(layer_norm(u, ln_g, ln_b))
    return u @ w2 + b2, new_ctx


def sq_relu_mlp(h, w_up, w_down):
    return jnp.square(jax.nn.relu(h @ w_up)) @ w_down


def trunk(x, c, pos, p, attend, ssm_ctx, ssm_h0, conf_ctx, ssd_chunk):
    ks, vs, hs, scs, ccs = [], [], [], [], []
    for i in range(DEPTH):
        sh1, sc1, gt1, sh2, sc2, gt2 = adaln(c, p['w_mod'][i], p['b_mod'][i])
        h = rms_norm(x, p['g_mix'][i]) * (1.0 + sc1) + sh1
        if i % 2 == 0:
            e = i // 2
            out, k, v, h_t, sc = attn_ssm_mixer(
                h, pos, functools.partial(attend, e), ssm_ctx[e], ssm_h0[e], ssd_chunk,
                p['w_in'][e], p['w_out'][e], p['attn_sinks'][e], p['ssm_a_log'][e], p['ssm_dt_bias'][e],
                p['ssm_d'][e], p['ssm_conv_w'][e], p['ssm_conv_b'][e], p['ssm_norm_g'][e])
            ks.append(k); vs.append(v); hs.append(h_t); scs.append(sc)
        else:
            o = i // 2
            out, cc = conformer_conv(h, conf_ctx[o], p['conf_w1'][o], p['conf_b1'][o], p['conf_dw_w'][o],
                                     p['conf_dw_b'][o], p['conf_ln_g'][o], p['conf_ln_b'][o],
                                     p['conf_w2'][o], p['conf_b2'][o])
            ccs.append(cc)
        x = x + gt1 * out
        h = rms_norm(x, p['g_mlp'][i]) * (1.0 + sc2) + sh2
        x = x + gt2 * sq_relu_mlp(h, p['mlp_w_up'][i], p['mlp_w_down'][i])
    return rms_norm(x, p['g_final']), ks, vs, hs, scs, ccs


def setup_inputs(seed: int = 0) -> dict:
    key = jax.random.key(seed)
    keys = iter(jax.random.split(key, 48))

    def nrm(shape, scale=1.0):
        return jax.random.normal(next(keys), shape, F32) * scale

    dt0 = jnp.exp(jax.random.uniform(next(keys), (N_EVEN, SSM_HEADS), F32) * (math.log(0.1) - math.log(0.001)) + math.log(0.001))
    return {
        'x_prompt': nrm((BATCH, SEQ, D_MODEL)),
        'x_sample': nrm((DEC_BATCH, DEC_SEQ, D_MODEL)),
        'cache_swa_k': nrm((N_EVEN, DEC_BATCH, SWA_KEEP, KV_HEADS, HEAD_DIM)),
        'cache_swa_v': nrm((N_EVEN, DEC_BATCH, SWA_KEEP, KV_HEADS, HEAD_DIM)),
        'state_ssm': nrm((N_EVEN, DEC_BATCH, SSM_HEADS, SSM_HEAD_DIM, SSM_STATE), 0.1),
        'state_ssm_conv': nrm((N_EVEN, DEC_BATCH, SSM_CONV - 1, SSM_CONV_DIM)),
        'state_conf_conv': nrm((N_ODD, DEC_BATCH, CONF_KERNEL - 1, CONF_DIM), 0.5),
        'c_prompt': nrm((BATCH, D_MODEL)),
        'c_sample': nrm((DEC_BATCH, D_MODEL)),
        'w_mod': nrm((DEPTH, D_MODEL, 6 * D_MODEL), 0.5 * D_MODEL ** -0.5),
        'b_mod': nrm((DEPTH, 6 * D_MODEL), 0.02),
        'g_mix': 1.0 + nrm((DEPTH, D_MODEL), 0.02),
        'g_mlp': 1.0 + nrm((DEPTH, D_MODEL), 0.02),
        'w_in': nrm((N_EVEN, D_MODEL, IN_WIDTH), D_MODEL ** -0.5),
        'w_out': nrm((N_EVEN, MIX_WIDTH, D_MODEL), MIX_WIDTH ** -0.5),
        'attn_sinks': nrm((N_EVEN, Q_HEADS), 0.5),
        'ssm_a_log': jnp.log(jax.random.uniform(next(keys), (N_EVEN, SSM_HEADS), F32, 1.0, 16.0)),
        'ssm_dt_bias': dt0 + jnp.log(-jnp.expm1(-dt0)),
        'ssm_d': 1.0 + nrm((N_EVEN, SSM_HEADS), 0.1),
        'ssm_conv_w': nrm((N_EVEN, SSM_CONV, SSM_CONV_DIM), SSM_CONV ** -0.5),
        'ssm_conv_b': nrm((N_EVEN, SSM_CONV_DIM), 0.02),
        'ssm_norm_g': 1.0 + nrm((N_EVEN, SSM_INNER), 0.02),
        'conf_w1': nrm((N_ODD, D_MODEL, 2 * CONF_DIM), D_MODEL ** -0.5),
        'conf_b1': nrm((N_ODD, 2 * CONF_DIM), 0.02),
        'conf_dw_w': nrm((N_ODD, CONF_KERNEL, CONF_DIM), CONF_KERNEL ** -0.5),
        'conf_dw_b': nrm((N_ODD, CONF_DIM), 0.02),
        'conf_ln_g': 1.0 + nrm((N_ODD, CONF_DIM), 0.02),
        'conf_ln_b': nrm((N_ODD, CONF_DIM), 0.02),
        'conf_w2': nrm((N_ODD, CONF_DIM, D_MODEL), CONF_DIM ** -0.5),
        'conf_b2': nrm((N_ODD, D_MODEL), 0.02),
        'mlp_w_up': nrm((DEPTH, D_MODEL, D_FF), D_MODEL ** -0.5),
        'mlp_w_down': nrm((DEPTH, D_FF, D_MODEL), D_FF ** -0.5),
        'g_final': 1.0 + nrm((D_MODEL,), 0.02),
    }


def reference(x_prompt, x_sample, cache_swa_k, cache_swa_v, state_ssm, state_ssm_conv, state_conf_conv,
              c_prompt, c_sample, w_mod, b_mod, g_mix, g_mlp, w_in, w_out, attn_sinks, ssm_a_log, ssm_dt_bias,
              ssm_d, ssm_conv_w, ssm_conv_b, ssm_norm_g, conf_w1, conf_b1, conf_dw_w, conf_dw_b, conf_ln_g,
              conf_ln_b, conf_w2, conf_b2, mlp_w_up, mlp_w_down, g_final):
    p = dict(w_mod=w_mod, b_mod=b_mod, g_mix=g_mix, g_mlp=g_mlp, w_in=w_in, w_out=w_out, attn_sinks=attn_sinks,
             ssm_a_log=ssm_a_log, ssm_dt_bias=ssm_dt_bias, ssm_d=ssm_d, ssm_conv_w=ssm_conv_w,
             ssm_conv_b=ssm_conv_b, ssm_norm_g=ssm_norm_g, conf_w1=conf_w1, conf_b1=conf_b1,
             conf_dw_w=conf_dw_w, conf_dw_b=conf_dw_b, conf_ln_g=conf_ln_g, conf_ln_b=conf_ln_b,
             conf_w2=conf_w2, conf_b2=conf_b2, mlp_w_up=mlp_w_up, mlp_w_down=mlp_w_down, g_final=g_final)

    bp, tp = x_prompt.shape[0], x_prompt.shape[1]
    zero_sc = [jnp.zeros((bp, SSM_CONV - 1, SSM_CONV_DIM), x_prompt.dtype) for _ in range(N_EVEN)]
    zero_h = [jnp.zeros((bp, SSM_GROUPS, SSM_HEADS_PER_GROUP, SSM_HEAD_DIM, SSM_STATE), F32) for _ in range(N_EVEN)]
    zero_cc = [jnp.zeros((bp, CONF_KERNEL - 1, CONF_DIM), x_prompt.dtype) for _ in range(N_ODD)]
    y_prompt, kp, vp, hp, scp, ccp = trunk(
        x_prompt, c_prompt, jnp.arange(tp), p, lambda e, q, k, v, s: swa_prompt(q, k, v, s),
        zero_sc, zero_h, zero_cc, min(SSD_CHUNK, tp))

    bs, ts = x_sample.shape[0], x_sample.shape[1]
    h0_s = [state_ssm[e].reshape(bs, SSM_GROUPS, SSM_HEADS_PER_GROUP, SSM_HEAD_DIM, SSM_STATE).astype(F32)
            for e in range(N_EVEN)]
    y_sample, ksm, vsm, hsm, scsm, ccsm = trunk(
        x_sample, c_sample, PAST_LEN + jnp.arange(ts), p,
        lambda e, q, k, v, s: swa_sample(q, k, v, s, cache_swa_k[e], cache_swa_v[e]),
        [state_ssm_conv[e] for e in range(N_EVEN)], h0_s, [state_conf_conv[o] for o in range(N_ODD)], ts)

    swa_k_prompt = jnp.stack([k[:, k.shape[1] - WINDOW:] for k in kp])
    swa_v_prompt = jnp.stack([v[:, v.shape[1] - WINDOW:] for v in vp])
    swa_k_sample = jnp.stack(ksm)
    swa_v_sample = jnp.stack(vsm)
    ssm_state_prompt = jnp.stack(hp)
    ssm_state_sample = jnp.stack(hsm)
    ssm_conv_prompt = jnp.stack(scp)
    ssm_conv_sample = jnp.stack(scsm)
    conf_conv_prompt = jnp.stack(ccp)
    conf_conv_sample = jnp.stack(ccsm)
    return (y_prompt, y_sample, swa_k_prompt, swa_v_prompt, swa_k_sample, swa_v_sample,
            ssm_state_prompt, ssm_state_sample, ssm_conv_prompt, ssm_conv_sample,
            conf_conv_prompt, conf_conv_sample)
```

```python
import math, os, sys
from contextlib import ExitStack
KDEBUG = bool(os.environ.get("KDEBUG"))
import numpy as np
import concourse.bass as bass
import concourse.mybir as mybir
from concourse.bass_utils import run_bass_kernel_spmd

F32 = mybir.dt.float32
BF16 = mybir.dt.bfloat16
ALU = mybir.AluOpType
AF = mybir.ActivationFunctionType

D = 2048
KC = 16
DFF = 8192
EPS = 1e-6
SEQ = 8192
NCORES = 8
HALO = 256
PAST_LEN = 1024
WEXT = 3072 + 256 + 1024 + 2048 + 16
QK0, V0, Z0, XBC0, DT0 = 0, 3072, 3328, 4352, 6400
NEG = -30000.0


class Res:
    __slots__ = ("name", "w", "rs", "excl")

    def __init__(self, name="", excl=False):
        self.name = name
        self.w = None
        self.rs = []
        self.excl = excl


class Engine:
    def __init__(self, name):
        self.name = name
        self.ops = []
        self.count = 0
        self.known = {}
        self.is_pe = name == "pe"


class Prog:
    def __init__(self, dry=False):
        self.dry = dry
        self.sems = {}
        self.sem_names = []
        self.E = {n: Engine(n) for n in ("pe", "act", "dve", "pool", "sp")}
        for n in self.E:
            self.sem_names.append("eng_" + n)
        self.dma_pools = {}
        for q in ("sp", "pool"):
            keys = [f"dq_{q}_{i}" for i in range(8)]
            self.sem_names += keys
            self.dma_pools[q] = {"keys": keys, "cnt": [0] * len(keys), "i": 0}
        self.n_instr = 0

    def _need(self, eng, ev, same_ok=False):
        if ev is None:
            return
        key, val, ename = ev
        if same_ok and ename == eng.name:
            return
        if eng.known.get(key, 0) >= val:
            return
        eng.known[key] = val
        eng.ops.append(lambda e, key=key, val=val: e.wait_ge(self.sems[key], val))

    def _deps(self, eng, reads, writes):
        for r in reads:
            self._need(eng, r.w, same_ok=False)
            if r.excl:
                for ev in r.rs:
                    self._need(eng, ev, same_ok=True)
        for w in writes:
            self._need(eng, w.w, same_ok=eng.is_pe)
            for ev in w.rs:
                self._need(eng, ev, same_ok=(eng.name != "pool"))

    def _commit(self, ev, reads, writes):
        for r in reads:
            r.rs.append(ev)
            if len(r.rs) > 48:
                best = {}
                for e in r.rs:
                    if e[0] not in best or best[e[0]][1] < e[1]:
                        best[e[0]] = e
                r.rs = list(best.values())
        for w in writes:
            w.w = ev
            w.rs = []

    def op(self, engname, fn, reads=(), writes=()):
        if self.dry:
            return None
        if KDEBUG:
            fr = sys._getframe(2)
            lab = f"{fr.f_code.co_name}:{fr.f_lineno}"
            fn0 = fn
            fn = lambda e, fn0=fn0, lab=lab: fn0(e).annotate(lab)
        eng = self.E[engname]
        self._deps(eng, reads, writes)
        eng.count += 1
        key = "eng_" + engname
        ev = (key, eng.count, engname)
        eng.ops.append(lambda e, fn=fn, key=key: fn(e).then_inc(self.sems[key], 1))
        self._commit(ev, reads, writes)
        self.n_instr += 1
        return ev

    def dma(self, q, out_ap, in_ap, reads=(), writes=(), **kw):
        if self.dry:
            return None
        eng = self.E[q]
        pool = self.dma_pools[q]
        i = pool["i"]
        pool["i"] = (i + 1) % len(pool["keys"])
        key = pool["keys"][i]
        prev = pool["cnt"][i]
        pool["cnt"][i] = prev + 16
        if prev > 0:
            self._need(eng, (key, prev, "dma"))
        self._deps(eng, reads, writes)
        ev = (key, prev + 16, "dma")
        eng.ops.append(lambda e, key=key, o=out_ap, i_=in_ap, kw=kw:
                       e.dma_start(out=o, in_=i_, **kw).then_inc(self.sems[key], 16))
        self._commit(ev, reads, writes)
        self.n_instr += 1
        return ev

    def finish(self):
        for q, pool in self.dma_pools.items():
            for key, cnt in zip(pool["keys"], pool["cnt"]):
                if cnt:
                    self._need(self.E["sp"], (key, cnt, "dma"))

    def build(self, nc, stack):
        for key in self.sem_names:
            self.sems[key] = stack.enter_context(nc.semaphore(key))
        block = stack.enter_context(nc.Block())
        E = self.E

        @block.tensor
        def _(e):
            for f in E["pe"].ops:
                f(e)

        @block.scalar
        def _(e):
            for f in E["act"].ops:
                f(e)

        @block.vector
        def _(e):
            for f in E["dve"].ops:
                f(e)

        @block.gpsimd
        def _(e):
            for f in E["pool"].ops:
                f(e)

        @block.sync
        def _(e):
            for f in E["sp"].ops:
                f(e)


class Kern:
    def __init__(self, nc, st, npre, nmain, dbg=()):
        self.nc, self.st = nc, st
        self.npre, self.nmain = npre, nmain
        self.dbg = set(dbg)
        self.dbg_out = {}
        self.ncols = 320 + 512 * nmain
        self._decl()
        self._alloc()

    def _decl(self):
        nc = self.nc
        di = lambda n, s: nc.dram_tensor(n, list(s), F32, kind="ExternalInput").ap()
        do = lambda n, s: nc.dram_tensor(n, list(s), F32, kind="ExternalOutput").ap()
        I = self.I = {}
        O = self.O = {}
        I["x_pre"] = di("x_pre", (max(self.npre, 1) * 128, D))
        I["x_main"] = di("x_main", (HALO + 512 * self.nmain, D))
        I["x_smp"] = di("x_smp", (64, D))
        I["c5"] = di("c5", (5, D))
        I["ck"] = di("ck", (4, 128, 512))
        I["cv"] = di("cv", (4, 128, 256))
        I["st_ssm"] = di("st_ssm", (4, 1024, 128))
        I["st_sconv"] = di("st_sconv", (4, 3, D))
        I["st_cconv"] = di("st_cconv", (4, 30, D))
        I["rope_cos"] = di("rope_cos", (128, self.ncols))
        I["rope_sin"] = di("rope_sin", (128, self.ncols))
        I["flags"] = di("flags", (128, 2))
        I["w_mod"] = di("w_mod", (2, D, 6 * D))
        I["b_mod"] = di("b_mod", (2, 6 * D))
        I["g_mix"] = di("g_mix", (2, D))
        I["g_mlp"] = di("g_mlp", (2, D))
        I["w_in_ext"] = di("w_in_ext", (D, WEXT))
        I["w_out"] = di("w_out", (D, D))
        I["attn_sinks"] = di("attn_sinks", (1, 16))
        I["ssm_a_log"] = di("ssm_a_log", (1, 16))
        I["ssm_dt_bias"] = di("ssm_dt_bias", (1, 16))
        I["ssm_d"] = di("ssm_d", (1, 16))
        I["ssm_conv_w"] = di("ssm_conv_w", (4, D))
        I["ssm_conv_b"] = di("ssm_conv_b", (1, D))
        I["ssm_norm_g"] = di("ssm_norm_g", (1, 1024))
        I["w1_ext"] = di("w1_ext", (D, 2 * D))
        I["b1_ext"] = di("b1_ext", (1, 2 * D))
        I["conf_dw_w"] = di("conf_dw_w", (31, D))
        I["conf_dw_b"] = di("conf_dw_b", (1, D))
        I["conf_ln_g"] = di("conf_ln_g", (1, D))
        I["conf_ln_b"] = di("conf_ln_b", (1, D))
        I["conf_w2"] = di("conf_w2", (D, D))
        I["conf_b2"] = di("conf_b2", (1, D))
        I["mlp_w_up"] = di("mlp_w_up", (2, D, DFF))
        I["mlp_w_down"] = di("mlp_w_down", (2, DFF, D))
        I["g_final"] = di("g_final", (1, D))
        O["y_main"] = do("y_main", (512 * self.nmain, D))
        O["y_smp"] = do("y_smp", (64, D))
        O["k_p"] = do("k_p", (128, 256))
        O["v_p"] = do("v_p", (128, 256))
        O["k_s"] = do("k_s", (64, 256))
        O["v_s"] = do("v_s", (64, 256))
        O["hst_p"] = do("hst_p", (1024, 128))
        O["hst_s"] = do("hst_s", (4, 1024, 128))
        O["sconv_p"] = do("sconv_p", (3, D))
        O["sconv_s"] = do("sconv_s", (4, 3, D))
        O["cconv_p"] = do("cconv_p", (30, D))
        O["cconv_s"] = do("cconv_s", (4, 30, D))
        self.scr = {}
        self.scr_res = {}

    def _scratch(self, name, nunits):
        if name not in self.scr:
            self.scr[name] = self.nc.dram_tensor("scr_" + name, [nunits, 128, 4096], BF16, kind="Internal").ap()
        return self.scr[name]

    def _alloc(self):
        nc, st = self.nc, self.st
        self.sb_bytes = 0

        def S(name, shape, dt=F32):
            n = 1
            for s in shape[1:]:
                n *= s
            self.sb_bytes += n * (4 if dt == F32 else 2)
            return st.enter_context(nc.sbuf_tensor("s_" + name, list(shape), dt))

        self.S = S
        self.xres = S("xres", [128, 16, 512])
        self.r_xres = [Res(f"xres{c}") for c in range(16)]
        self.NW = 3
        self.wsl = [S(f"wsl{i}", [128, 4096], BF16) for i in range(self.NW)]
        self.r_wsl = [Res(f"wsl{i}") for i in range(self.NW)]
        self.AR = 72 * 1024
        self.arena = S("arena", [128, self.AR // 2], BF16)
        self.r_ar = [Res(f"ar{i}") for i in range(self.AR // 1024)]
        self.ident = S("ident", [128, 128]); self.r_ident = Res("ident")
        self.identb = S("identb", [128, 128], BF16); self.r_identb = Res("identb")
        self.tri = S("tri", [128, 128]); self.r_tri = Res("tri")
        self.onesf = S("onesf", [128, 128]); self.r_onesf = Res("onesf")
        self.onesb = S("onesb", [128, 128], BF16); self.r_onesb = Res("onesb")
        self.diag = S("diag", [128, 64, 128], BF16); self.r_diag = Res("diag")
        self.modT = S("modT", [128, 2, 96, 5]); self.r_mod = [[Res(f"mod{l}{g}") for g in range(6)] for l in range(2)]
        self.gs = S("gs", [128, 2, 2, 16, 5]); self.r_gsv = [[Res(f"gs{l}{w}") for w in range(2)] for l in range(2)]
        self.bmT = S("bmT", [128, 192])
        self.b2g = S("b2g", [128, 16, 5]); self.r_b2g = Res("b2g")
        self.cst = S("cst", [128, 800]); self.r_cst = Res("cst")
        self.C_GMIX, self.C_GMLP, self.C_GFIN, self.C_CB, self.C_B1 = 0, 32, 64, 80, 96
        self.C_DWB, self.C_LNG, self.C_LNB, self.C_B2, self.C_NG, self.C_CW, self.C_DWW = 128, 144, 160, 176, 192, 200, 264
        self.rowc = S("rowc", [128, 5, 16]); self.r_rowc = Res("rowc")
        self.cbrow = S("cbrow", [1, 1536], BF16); self.r_cbrow = Res("cbrow")
        self.flags = S("flags", [128, 2]); self.r_flags = Res("flags")
        self.scT = S("scT", [128, 16, 5], BF16); self.r_scT = Res("scT")
        self.dgb = S("dgb", [128, 12, 128], BF16); self.r_dgb = [Res(f"dgb{i}") for i in range(12)]
        self.dg_i = 0
        self.Kbuf = S("Kbuf", [128, 4, 640], BF16); self.r_K = Res("Kbuf")
        self.Vbuf = S("Vbuf", [128, 5, 256], BF16); self.r_V = Res("Vbuf")
        self.hT = S("hT", [128, 1024]); self.r_hT = Res("hT")
        self.hTb = S("hTb", [128, 1024], BF16); self.r_hTb = Res("hTb")
        self.xpctx = S("xpctx", [128, 16, 3], BF16); self.r_xpctx = Res("xpctx")
        self.xpctxf = S("xpctxf", [128, 16, 3]); self.r_xpctxf = Res("xpctxf")
        self.uctx = S("uctx", [128, 16, 30]); self.r_uctx = Res("uctx")
        self.rstd = S("rstd", [128, 512]); self.r_rstd = Res("rstd")
        self.tmp = [S(f"tmp{i}", [128, 512]) for i in range(4)]
        self.r_tmp = [Res(f"tmp{i}") for i in range(4)]
        self.tmp_i = 0
        self.sqb = [S(f"sqb{i}", [128, 512], BF16) for i in range(4)]
        self.r_sqb = [Res(f"sqb{i}") for i in range(4)]
        self.sqb_i = 0
        self.sm = S("sm", [128, 4, 64]); self.r_sm = [Res(f"sm{i}") for i in range(4)]
        self.sm_i = 0
        self.ps = [st.enter_context(nc.psum_tensor(f"ps{i}", [128, 512], F32)) for i in range(8)]
        self.r_ps = [Res(f"ps{i}", excl=True) for i in range(8)]
        self.ps_i = 0

    def psum(self):
        i = self.ps_i
        self.ps_i = (i + 1) % 6
        return self.ps[i], self.r_ps[i]

    def gtmp(self):
        i = self.tmp_i
        self.tmp_i = (i + 1) % 4
        return self.tmp[i], self.r_tmp[i]

    def gsqb(self):
        i = self.sqb_i
        self.sqb_i = (i + 1) % 4
        return self.sqb[i], self.r_sqb[i]

    def gdg(self):
        i = self.dg_i
        self.dg_i = (i + 1) % 12
        return self.dgb[:, i, :], self.r_dgb[i]

    def gsm(self):
        i = self.sm_i
        self.sm_i = (i + 1) % 4
        return self.sm[:, i, :], self.r_sm[i]

    def av(self, off, shape, dt=BF16):
        n = 1
        for s in shape:
            n *= s
        esz = 4 if dt == F32 else 2
        assert off % 4 == 0 and off + n * esz <= self.AR, (off, shape)
        ap = self.arena[:, off // 2: off // 2 + n * esz // 2]
        if dt == F32:
            ap = ap.bitcast(F32)
        if len(shape) == 2:
            ap = ap.rearrange("p (a b) -> p a b", a=shape[0])
        elif len(shape) == 3:
            ap = ap.rearrange("p (a b c) -> p a b c", a=shape[0], b=shape[1])
        return ap

    def ares(self, off, nbytes):
        return self.r_ar[off // 1024: (off + nbytes + 1023) // 1024]

    def ACT(self, out, in_, func, rd, wr, **kw):
        self.P.op("act", lambda e: e.activation(out=out, in_=in_, func=func, **kw), rd, wr)

    def TT(self, eng, out, in0, in1, op, rd, wr):
        self.P.op(eng, lambda e: e.tensor_tensor(out=out, in0=in0, in1=in1, op=op), rd, wr)

    def TS(self, eng, out, in0, s1, s2, op0, op1, rd, wr):
        if s2 is None:
            self.P.op(eng, lambda e: e.tensor_scalar(out=out, in0=in0, scalar1=s1, scalar2=None, op0=op0), rd, wr)
        else:
            self.P.op(eng, lambda e: e.tensor_scalar(out=out, in0=in0, scalar1=s1, scalar2=s2, op0=op0, op1=op1), rd, wr)

    def STT(self, out, in0, scalar, in1, op0, op1, rd, wr):
        self.P.op("dve", lambda e: e.scalar_tensor_tensor(out=out, in0=in0, scalar=scalar, in1=in1, op0=op0, op1=op1), rd, wr)

    def CP(self, eng, out, in_, rd, wr):
        if eng == "act":
            self.P.op("act", lambda e: e.activation(out=out, in_=in_, func=AF.Identity), rd, wr)
        else:
            self.P.op(eng, lambda e: e.tensor_copy(out=out, in_=in_), rd, wr)

    def MM(self, out, lhsT, rhs, start, stop, rd, wr):
        self.P.op("pe", lambda e: e.matmul(out, lhsT=lhsT, rhs=rhs, start=start, stop=stop), rd, wr)

    def TR(self, out, in_, ident, rd, wr):
        self.P.op("pe", lambda e: e.transpose(out, in_, ident), rd, wr)

    def RECIP(self, out, in_, rd, wr):
        self.P.op("dve", lambda e: e.reciprocal(out=out, in_=in_), rd, wr)

    def MEMSET(self, eng, ap, val, wr):
        self.P.op(eng, lambda e: e.memset(ap, val), (), wr)

    def DBG(self, name, ap, rd, shape):
        if name not in self.dbg:
            return
        if name not in self.dbg_out:
            self.dbg_out[name] = self.nc.dram_tensor("dbg_" + name, list(shape), F32, kind="ExternalOutput").ap()
        self.P.dma("sp", self.dbg_out[name], ap, reads=rd)

    def wreq(self, name, uidx, nunits, src_ap, kc, ncols, once=False):
        P = self.P
        spec = (name, uidx, nunits, src_ap, kc, ncols, once)
        if P.dry:
            self.wsched.append(spec)
            return None, None
        i = self.w_consumed
        assert self.wsched[i][0] == name and self.wsched[i][1] == uidx, (self.wsched[i][:2], name, uidx)
        while self.w_issued < min(len(self.wsched), i + self.NW):
            self._wissue(self.w_issued)
            self.w_issued += 1
        self.w_consumed += 1
        s = i % self.NW
        view = self.wsl[s][:, 0:kc * ncols].rearrange("p (k n) -> p k n", k=kc)
        return view, self.r_wsl[s]

    def _wissue(self, i):
        name, uidx, nunits, src_ap, kc, ncols, once = self.wsched[i]
        s = i % self.NW
        P = self.P
        flat = self.wsl[s][:, 0:kc * ncols]
        view = flat.rearrange("p (k n) -> p k n", k=kc)
        key = (name, uidx)
        if once:
            P.dma("pool", view, src_ap.rearrange("(k p) n -> p k n", p=128), writes=[self.r_wsl[s]])
        else:
            while key not in self.scr_res:
                assert self.pc_i < len(self.pc_list)
                self.precast_next()
            scr = self._scratch(name, nunits)
            P.dma("sp", flat, scr[uidx, :, 0:kc * ncols], reads=[self.scr_res[key]], writes=[self.r_wsl[s]])

    def load_vec_fm(self, src_rows_ap, nrows, dst_ap):
        t, rt = self.gtmp()
        self.P.dma("sp", t[0:nrows, 0:128], src_rows_ap, writes=[rt])
        ps, rp = self.psum()
        self.TR(ps[:, 0:nrows], t[0:nrows, 0:128], self.ident[0:nrows, 0:nrows], [rt, self.r_ident], [rp])
        self.CP("dve", dst_ap, ps[:, 0:nrows], [rp], [self.r_cst])

    def emit_setup(self):
        P, I = self.P, self.I
        self.MEMSET("pool", self.ident[:], 0.0, [self.r_ident])
        P.op("pool", lambda e: e.affine_select(out=self.ident[:], in_=self.ident[:], pattern=[[-1, 128]],
                                                compare_op=ALU.not_equal, fill=1.0, base=0, channel_multiplier=1),
             [self.r_ident], [self.r_ident])
        self.CP("dve", self.identb[:], self.ident[:], [self.r_ident], [self.r_identb])
        self.MEMSET("pool", self.tri[:], 1.0, [self.r_tri])
        P.op("pool", lambda e: e.affine_select(out=self.tri[:], in_=self.tri[:], pattern=[[1, 128]],
                                                compare_op=ALU.is_ge, fill=0.0, base=0, channel_multiplier=-1),
             [self.r_tri], [self.r_tri])
        self.MEMSET("pool", self.onesf[:], 1.0, [self.r_onesf])
        self.MEMSET("pool", self.onesb[:], 1.0, [self.r_onesb])
        self.MEMSET("pool", self.Kbuf[:], 0.0, [self.r_K])
        self.MEMSET("pool", self.Vbuf[:], 0.0, [self.r_V])
        self.MEMSET("pool", self.hT[:], 0.0, [self.r_hT])
        self.MEMSET("pool", self.hTb[:], 0.0, [self.r_hTb])
        self.MEMSET("pool", self.xpctx[:], 0.0, [self.r_xpctx])
        self.MEMSET("pool", self.xpctxf[:], 0.0, [self.r_xpctxf])
        self.MEMSET("pool", self.uctx[:], 0.0, [self.r_uctx])
        P.dma("sp", self.flags[:], I["flags"], writes=[self.r_flags])
        c = self.cst
        rows = lambda ap, n: ap.rearrange("a (r p) -> (a r) p", p=128)
        self.load_vec_fm(rows(I["g_mix"], 32), 32, c[:, self.C_GMIX:self.C_GMIX + 32])
        self.load_vec_fm(rows(I["g_mlp"], 32), 32, c[:, self.C_GMLP:self.C_GMLP + 32])
        self.load_vec_fm(rows(I["g_final"], 16), 16, c[:, self.C_GFIN:self.C_GFIN + 16])
        self.load_vec_fm(rows(I["ssm_conv_b"], 16), 16, c[:, self.C_CB:self.C_CB + 16])
        self.load_vec_fm(rows(I["b1_ext"], 32), 32, c[:, self.C_B1:self.C_B1 + 32])
        self.load_vec_fm(rows(I["conf_dw_b"], 16), 16, c[:, self.C_DWB:self.C_DWB + 16])
        self.load_vec_fm(rows(I["conf_ln_g"], 16), 16, c[:, self.C_LNG:self.C_LNG + 16])
        self.load_vec_fm(rows(I["conf_ln_b"], 16), 16, c[:, self.C_LNB:self.C_LNB + 16])
        self.load_vec_fm(rows(I["conf_b2"], 16), 16, c[:, self.C_B2:self.C_B2 + 16])
        self.load_vec_fm(rows(I["ssm_norm_g"], 8), 8, c[:, self.C_NG:self.C_NG + 8])
        self.load_vec_fm(rows(I["ssm_conv_w"], 64), 64, c[:, self.C_CW:self.C_CW + 64])
        dww = rows(I["conf_dw_w"], 496)
        for q in range(4):
            self.load_vec_fm(dww[q * 124:(q + 1) * 124, :], 124, c[:, self.C_DWW + q * 124: self.C_DWW + (q + 1) * 124])
        for i, nm in enumerate(["ssm_dt_bias", "ssm_a_log", "ssm_d", "attn_sinks"]):
            P.dma("sp", self.rowc[:, i, :], I[nm].broadcast_to([128, 16]), writes=[self.r_rowc])
        self.ACT(self.rowc[:, 1, :], self.rowc[:, 1, :], AF.Exp, [self.r_rowc], [self.r_rowc])
        self.TS("dve", self.rowc[:, 1, :], self.rowc[:, 1, :], -1.0, None, ALU.mult, None, [self.r_rowc], [self.r_rowc])
        self.ACT(self.rowc[:, 3, :], self.rowc[:, 3, :], AF.Exp, [self.r_rowc], [self.r_rowc])
        t, rt = self.gtmp()
        P.dma("sp", t[0:1, 0:512], I["ssm_conv_b"][:, 0:512], writes=[rt])
        self.CP("dve", self.cbrow[0:1, 0:512], t[0:1, 0:512], [rt], [self.r_cbrow])
        t, rt = self.gtmp()
        P.dma("sp", t[0:1, 0:512], I["ssm_conv_b"][:, 512:1024], writes=[rt])
        self.CP("dve", self.cbrow[0:1, 512:1024], t[0:1, 0:512], [rt], [self.r_cbrow])
        t, rt = self.gtmp()
        P.dma("sp", t[0:1, 0:512], I["ssm_conv_b"][:, 1024:1536], writes=[rt])
        self.CP("dve", self.cbrow[0:1, 1024:1536], t[0:1, 0:512], [rt], [self.r_cbrow])
        for j in range(4):
            for cc in range(16):
                self.TS("dve", self.diag[:, j * 16 + cc, :], self.identb[:], c[:, self.C_CW + j * 16 + cc: self.C_CW + j * 16 + cc + 1],
                        None, ALU.mult, None, [self.r_identb, self.r_cst], [self.r_diag])
        c5t = self.av(0, [2048], F32)
        r_c5 = self.ares(0, 8192)
        P.dma("sp", c5t[0:5, :], I["c5"], writes=r_c5)
        ps, rp = self.psum()
        for k in range(16):
            self.TR(ps[:, k * 5:(k + 1) * 5], c5t[0:5, k * 128:(k + 1) * 128], self.ident[0:5, 0:5], r_c5 + [self.r_ident], [rp])
        self.ACT(self.scT[:].rearrange("p k s -> p (k s)"), ps[:, 0:80], AF.Silu, [rp], [self.r_scT])
        bm = self.bmT
        r_bm = [Res("bm")]
        self.r_bm = r_bm
        for l in range(2):
            t, rt = self.gtmp()
            P.dma("sp", t[0:96, 0:128], I["b_mod"][l:l + 1, :].rearrange("a (r p) -> (a r) p", p=128), writes=[rt])
            ps, rp = self.psum()
            self.TR(ps[:, 0:96], t[0:96, 0:128], self.ident[0:96, 0:96], [rt, self.r_ident], [rp])
            self.CP("dve", bm[:, l * 96:(l + 1) * 96], ps[:, 0:96], [rp], r_bm)
        for u in range(16):
            self.adaln_unit(0, u)
        self.adaln_derive(0, 0)
        self.adaln_rest = [(0, u) for u in range(16, 48)] + [(1, u) for u in range(48)]

    def adaln_unit(self, l, u):
        P, I = self.P, self.I
        wv, rw = self.wreq("w_mod%d" % l, u, 48, I["w_mod"][l, :, u * 256:(u + 1) * 256], 16, 256, once=True)
        if P.dry:
            return
        ps, rp = self.psum()
        for k in range(16):
            self.MM(ps[0:5, 0:256], self.scT[:, k, :], wv[:, k, :], k == 0, k == 15, [self.r_scT, rw], [rp])
        t, rt = self.gtmp()
        self.CP("act", t[0:5, 0:256], ps[0:5, 0:256], [rp], [rt])
        ps2, rp2 = self.psum()
        for h in range(2):
            self.TR(ps2[:, h * 5:(h + 1) * 5], t[0:5, h * 128:(h + 1) * 128], self.ident[0:5, 0:5], [rt, self.r_ident], [rp2])
        for h in range(2):
            ch = u * 2 + h
            self.TS("dve", self.modT[:, l, ch, :], ps2[:, h * 5:(h + 1) * 5], self.bmT[:, l * 96 + ch: l * 96 + ch + 1], None,
                    ALU.add, None, [rp2] + self.r_bm, [self.r_mod[l][ch // 16]])
        for _ in range(4):
            self.precast_next()

    def adaln_flush(self, n):
        for _ in range(min(n, len(self.adaln_rest))):
            l, u = self.adaln_rest.pop(0)
            self.adaln_unit(l, u)

    def adaln_derive(self, l, which):
        c = self.cst
        for cc in range(16):
            if which == 0:
                self.TS("dve", self.gs[:, l, 0, cc, :], self.modT[:, l, 16 + cc, :], 1.0, c[:, self.C_GMIX + l * 16 + cc: self.C_GMIX + l * 16 + cc + 1],
                        ALU.add, ALU.mult, [self.r_mod[l][1], self.r_cst], [self.r_gsv[l][0]])
            else:
                self.TS("dve", self.gs[:, l, 1, cc, :], self.modT[:, l, 64 + cc, :], 1.0, c[:, self.C_GMLP + l * 16 + cc: self.C_GMLP + l * 16 + cc + 1],
                        ALU.add, ALU.mult, [self.r_mod[l][4], self.r_cst], [self.r_gsv[l][1]])

    def adaln_finish(self):
        c = self.cst
        self.adaln_flush(1000)
        self.adaln_derive(0, 1)
        self.adaln_derive(1, 0)
        self.adaln_derive(1, 1)
        for cc in range(16):
            self.TS("dve", self.b2g[:, cc, :], self.modT[:, 1, 32 + cc, :], c[:, self.C_B2 + cc: self.C_B2 + cc + 1], None,
                    ALU.mult, None, [self.r_mod[1][2], self.r_cst], [self.r_b2g])

    def precast_next(self):
        if self.P.dry or self.pc_i >= len(self.pc_list):
            return
        name, uidx, nunits, src_ap, kc, ncols, once = self.pc_list[self.pc_i]
        self.pc_i += 1
        scr = self._scratch(name, nunits)
        r = Res("scr")
        self.scr_res[(name, uidx)] = r
        self.P.dma("pool", scr[uidx, :, 0:kc * ncols].rearrange("p (k n) -> p k n", k=kc),
                   src_ap.rearrange("(k p) n -> p k n", p=128), writes=[r])

    def m_sh(self, l, which, cc, s):
        return self.modT[:, l, (0 if which == 0 else 48) + cc, s:s + 1]

    def m_gt(self, l, which, cc, s):
        return self.modT[:, l, (32 if which == 0 else 80) + cc, s:s + 1]

    def m_gs(self, l, which, cc, s):
        return self.gs[:, l, which, cc, s:s + 1]

    def load_x(self, blocks):
        P = self.P
        for bi, (src, c0, n) in enumerate(blocks):
            off = (bi % 2) * 8192
            xs = self.av(off, [2048], F32)
            rx = self.ares(off, 8192)
            P.dma("sp", xs[0:n, :], src, writes=rx)
            for g in range(4):
                ps, rp = self.psum()
                for q in range(4):
                    cc = g * 4 + q
                    self.TR(ps[:, q * 128: q * 128 + n], xs[0:n, cc * 128:(cc + 1) * 128], self.ident[0:n, 0:n], rx + [self.r_ident], [rp])
                for q in range(4):
                    cc = g * 4 + q
                    self.CP("act" if g % 2 else "dve", self.xres[:, cc, c0:c0 + n], ps[:, q * 128:q * 128 + n], [rp], [self.r_xres[cc]])
                    if bi == len(blocks) - 1:
                        self.stat_chunk(cc, c0 + n)

    def stat_chunk(self, cc, T):
        sq, rsq = self.gsqb()
        self.ACT(sq[:, 0:T], self.xres[:, cc, 0:T], AF.Square, [self.r_xres[cc]], [rsq])
        self.stat_pend.append((sq, rsq))
        if len(self.stat_pend) > 2:
            self._stat_mm(T)

    def _stat_mm(self, T):
        sq, rsq = self.stat_pend.pop(0)
        ps, rp = self.ps[7], self.r_ps[7]
        self.MM(ps[:, 0:T], self.onesb[:], sq[:, 0:T], self.stat_n == 0, self.stat_n == 15, [self.r_onesb, rsq], [rp])
        self.stat_n += 1

    def norm(self, T, segs, l, which, hn, r_hn, final=False):
        while self.stat_pend:
            self._stat_mm(T)
        assert self.P.dry or self.stat_n == 16, self.stat_n
        self.stat_n = 0
        ps, rp = self.ps[7], self.r_ps[7]
        self.ACT(self.rstd[:, 0:T], ps[:, 0:T], AF.Sqrt, [rp], [self.r_rstd], scale=1.0 / D, bias=EPS)
        self.RECIP(self.rstd[:, 0:T], self.rstd[:, 0:T], [self.r_rstd], [self.r_rstd])
        if final:
            return
        for cc in range(16):
            t, rt = self.gtmp()
            for (c0, n, s) in segs:
                self.STT(t[:, c0:c0 + n], self.xres[:, cc, c0:c0 + n], self.m_gs(l, which, cc, s), self.rstd[:, c0:c0 + n],
                         ALU.mult, ALU.mult, [self.r_xres[cc], self.r_gsv[l][which], self.r_rstd], [rt])
                self.ACT(hn[:, cc, c0:c0 + n], t[:, c0:c0 + n], AF.Identity, [rt, self.r_mod[l][0 if which == 0 else 3]], [r_hn[cc]],
                         bias=self.m_sh(l, which, cc, s))

    def proj_fm(self, name, src, ncols_total, kc, in_buf, r_in, T, evac, col0=0, ncols=None, kbase=0, nunits=None, ubase=0,
                unit_cols=256):
        P = self.P
        ncols = ncols_total - col0 if ncols is None else ncols
        nu = (ncols + unit_cols - 1) // unit_cols
        nunits = nu if nunits is None else nunits
        for u in range(nu):
            cw = min(unit_cols, ncols - u * unit_cols)
            sap = src[kbase * 128:(kbase + kc) * 128, col0 + u * unit_cols: col0 + u * unit_cols + cw]
            wv, rw = self.wreq(name, ubase + u, nunits, sap, kc, cw)
            if P.dry:
                continue
            for h in range(cw // 128):
                ps, rp = self.psum()
                for k in range(kc):
                    self.MM(ps[:, 0:T], wv[:, k, h * 128:(h + 1) * 128], in_buf[:, k, 0:T], k == 0, k == kc - 1,
                            [rw, r_in[k]], [rp])
                evac(u * (unit_cols // 128) + h, ps, rp)

    def l0_mixer(self, tile):
        P, I = self.P, self.I
        T, segs, blocks, kind = tile["T"], tile["segs"], tile["blocks"], tile["kind"]
        pre = kind == "pre"
        KB = 1024
        hn = self.av(0, [16, 512]); r_hn = [self.ares(c * KB, KB)[0] for c in range(16)]
        Q = self.av(16 * KB, [8, 512]); r_Q = [self.ares(16 * KB + c * KB, KB)[0] for c in range(8)]
        nslot = len(blocks)
        zb = self.av(24 * KB, [6, 1024]); r_z = [self.ares(24 * KB + b * 2048, 2048) for b in range(6)]
        XW = tile["xpw"]
        xp = self.av(36 * KB, [16, XW]); r_xp = [self.ares(36 * KB + c * XW * 2, XW * 2) for c in range(16)]
        xst = self.av(53 * KB, [6, 1024]); r_xst = [self.ares(53 * KB + b * 2048, 2048) for b in range(6)]
        Bt = self.av(65 * KB, [6, 512]); r_Bt = [self.ares(65 * KB + b * 1024, 1024) for b in range(6)]
        BCT = self.av(0, [8, 512]); r_BCT = [self.ares(c * KB, KB)[0] for c in range(8)]
        mix = self.av(36 * KB, [16, 512]); r_mix = [self.ares(36 * KB + c * KB, KB)[0] for c in range(16)]
        dtb = self.av(71 * KB, [6, 2, 16], F32); r_dtb = self.ares(71 * KB, 768)
        KSUB = int(os.environ.get("KSUB", "99"))
        if KSUB <= 1:
            return
        self.norm(T, segs, 0, 0, hn, r_hn)
        if KSUB <= 2:
            return
        ropec = self.av(63 * KB, [512], F32); ropes = self.av(65 * KB, [512], F32)
        r_rope = self.ares(63 * KB, 4096)
        if not pre:
            P.dma("sp", ropec[:, 0:T], I["rope_cos"][:, tile["rc0"]:tile["rc0"] + T], writes=r_rope)
            P.dma("sp", ropes[:, 0:T], I["rope_sin"][:, tile["rc0"]:tile["rc0"] + T], writes=r_rope)
        W = I["w_in_ext"]
        if not pre:
            kfull = {}

            def evac_qk(ci, ps, rp, st={}):
                u, h = ci // 2, ci % 2
                if h == 0:
                    t1, r1 = self.gtmp()
                    self.TT("dve", t1[:, 0:T], ps[:, 0:T], ropec[:, 0:T], ALU.mult, [rp] + r_rope, [r1])
                    st["t1"] = (t1, r1)
                else:
                    t1, r1 = st["t1"]
                    t2, r2 = self.gtmp()
                    self.TT("dve", t2[:, 0:T], ps[:, 0:T], ropes[:, 0:T], ALU.mult, [rp] + r_rope, [r2])
                    if u < 8:
                        self.TT("pool", Q[:, u, 0:T], t1[:, 0:T], t2[:, 0:T], ALU.add, [r1, r2], [r_Q[u]])
                    else:
                        j = u - 8
                        self.TT("pool", t1[:, 0:T], t1[:, 0:T], t2[:, 0:T], ALU.add, [r1, r2], [r1])
                        self.k_store(tile, j, t1, r1)

            self.proj_fm("w_in", W, WEXT, 16, hn, r_hn, T, evac_qk, col0=QK0, ncols=3072, nunits=29, ubase=0)
            wv, rw = self.wreq("w_in", 12, 29, W[:, V0:V0 + 256], 16, 256)
            if not P.dry:
                for bi, (c0, n, bk) in enumerate(blocks):
                    ps, rp = self.psum()
                    for k in range(16):
                        self.MM(ps[0:n, 0:256], hn[:, k, c0:c0 + n], wv[:, k, :], k == 0, k == 15, [r_hn[k], rw], [rp])
                    self.v_store(tile, bi, ps, rp)
        if not pre:
            for u in range(4):
                wv, rw = self.wreq("w_in", 13 + u, 29, W[:, Z0 + u * 256: Z0 + (u + 1) * 256], 16, 256)
                if P.dry:
                    continue
                for bi, (c0, n, bk) in enumerate(blocks):
                    ps, rp = self.psum()
                    for k in range(16):
                        self.MM(ps[0:n, 0:256], hn[:, k, c0:c0 + n], wv[:, k, :], k == 0, k == 15, [r_hn[k], rw], [rp])
                    self.ACT(zb[0:n, bi, u * 256:(u + 1) * 256], ps[0:n, 0:256], AF.Silu, [rp], r_z[bi])
        so = tile["segoff"]
        self.CP("pool", xp[:, :, so[0]:so[0] + 3], self.xpctx[:], [self.r_xpctx], sum(r_xp, []))

        def evac_xbc(ci, ps, rp):
            for si, (c0, n, s) in enumerate(segs):
                self.CP("act", xp[:, ci, so[si] + 3: so[si] + 3 + n], ps[:, c0:c0 + n], [rp], r_xp[ci])
                if si == 0 and not pre:
                    self.CP("dve", self.xpctxf[:, ci, :], ps[:, c0 + n - 3:c0 + n], [rp], [self.r_xpctxf])
                elif si > 0:
                    self.CP("dve", tile["sconvf"][:, si - 1, ci, :], ps[:, c0 + n - 3:c0 + n], [rp], tile["r_sconvf"])

        ncx = 1536 if pre else 2048
        self.proj_fm("w_in", W, WEXT, 16, hn, r_hn, T, evac_xbc, col0=XBC0, ncols=ncx, nunits=29, ubase=17)
        if KSUB <= 3:
            return
        wv, rw = self.wreq("w_in", 25, 29, W[:, DT0:DT0 + 16], 16, 16)
        if not P.dry:
            for bi, (c0, n, bk) in enumerate(blocks):
                ps, rp = self.psum()
                for k in range(16):
                    self.MM(ps[0:n, 0:16], hn[:, k, c0:c0 + n], wv[:, k, :], k == 0, k == 15, [r_hn[k], rw], [rp])
                self.TT("dve", dtb[0:n, bi, 0, :], ps[0:n, 0:16], self.rowc[0:n, 0, :], ALU.add, [rp, self.r_rowc], r_dtb)
                self.ACT(dtb[0:n, bi, 0, :], dtb[0:n, bi, 0, :], AF.Exp, r_dtb, r_dtb)
                self.ACT(dtb[0:n, bi, 0, :], dtb[0:n, bi, 0, :], AF.Ln, r_dtb, r_dtb, bias=1.0)
                self.TT("dve", dtb[0:n, bi, 1, :], dtb[0:n, bi, 0, :], self.rowc[0:n, 1, :], ALU.mult, r_dtb + [self.r_rowc], r_dtb)
        if P.dry:
            if not pre:
                self.proj_fm("w_out", I["w_out"], D, 16, mix, r_mix, T, None)
            return
        if KSUB <= 4:
            return
        if kind == "halo":
            P.dma("sp", self.O["k_s"].rearrange("(s t) c -> t s c", t=16), tile["kso"][0:16, :, :], reads=tile["r_kso"])
            P.dma("sp", self.O["v_s"].rearrange("(s t) c -> t s c", t=16), tile["vso"][0:16, :, :], reads=tile["r_vso"])
        if tile.get("last"):
            P.dma("sp", self.O["k_p"], tile["kpo"][:, :], reads=tile["r_kpo"])
            P.dma("sp", self.O["v_p"], tile["vpo"][:, :], reads=tile["r_vpo"])
        if kind == "halo":
            for s in range(4):
                self.load_ctx_fm(I["st_sconv"][s], 3, xp, r_xp, so[1 + s])
        c = self.cst
        for si, (c0, n, s) in enumerate(segs):
            if not pre:
                for cc in range(8, 16):
                    ps, rp = self.psum()
                    for j in range(4):
                        self.MM(ps[:, 0:n], self.diag[:, j * 16 + cc, :], xp[:, cc, so[si] + j: so[si] + j + n], j == 0, j == 3,
                                [self.r_diag] + r_xp[cc], [rp])
                    self.ACT(BCT[:, cc - 8, c0:c0 + n], ps[:, 0:n], AF.Silu, [rp, self.r_cst], [r_BCT[cc - 8]],
                             bias=c[:, self.C_CB + cc:self.C_CB + cc + 1])
        for bi, (c0, n, bk) in enumerate(blocks):
            si = bk["seg"]
            o = so[si] + (c0 - segs[si][0])
            for g3 in range(3):
                ps, rp = self.psum()
                for q in range(4):
                    cc = g3 * 4 + q
                    for j in range(4):
                        self.MM(ps[0:n, q * 128:(q + 1) * 128], xp[:, cc, o + j: o + j + n], self.diag[:, j * 16 + cc, :], j == 0, False,
                                r_xp[cc] + [self.r_diag], [rp])
                    self.MM(ps[0:n, q * 128:(q + 1) * 128], self.onesb[0:1, 0:n], self.cbrow[0:1, cc * 128:(cc + 1) * 128], False, True,
                            [self.r_onesb, self.r_cbrow], [rp])
                if g3 < 2:
                    self.ACT(xst[0:n, bi, g3 * 512:(g3 + 1) * 512], ps[0:n, :], AF.Silu, [rp], r_xst[bi])
                else:
                    self.ACT(Bt[0:n, bi, :], ps[0:n, :], AF.Silu, [rp], r_Bt[bi])
        n0 = segs[0][1]
        ncs = 12 if pre else 16
        self.CP("pool", self.xpctx[:, 0:ncs, :], xp[:, 0:ncs, so[0] + n0: so[0] + n0 + 3], sum(r_xp[0:ncs], []), [self.r_xpctx])
        if KSUB <= 5:
            return
        if not pre:
            self.attention(tile, Q, r_Q, mix, r_mix)
        for bi, (c0, n, bk) in enumerate(blocks):
            self.ssd_block(tile, bi, c0, n, bk, xst, r_xst, Bt, r_Bt, BCT, r_BCT, zb, r_z, dtb, r_dtb, mix, r_mix)
        if pre:
            return
        def evac_out(ci, ps, rp):
            for (c0, n, s) in segs:
                self.STT(self.xres[:, ci, c0:c0 + n], ps[:, c0:c0 + n], self.m_gt(0, 0, ci, s), self.xres[:, ci, c0:c0 + n],
                         ALU.mult, ALU.add, [rp, self.r_mod[0][2], self.r_xres[ci]], [self.r_xres[ci]])
            self.stat_chunk(ci, T)

        self.proj_fm("w_out", I["w_out"], D, 16, mix, r_mix, T, evac_out)

    def load_ctx_fm(self, src, nrows, dst, r_dst, coloff):
        for qq in range(4):
            stg, r_stg = self.gtmp()
            self.P.dma("sp", stg[0:nrows, :], src[:, qq * 512:(qq + 1) * 512], writes=[r_stg])
            ps, rp = self.psum()
            for q in range(4):
                self.TR(ps[:, q * 32: q * 32 + nrows], stg[0:nrows, q * 128:(q + 1) * 128], self.ident[0:nrows, 0:nrows],
                        [r_stg, self.r_ident], [rp])
            for q in range(4):
                cc = qq * 4 + q
                self.CP("dve", dst[:, cc, coloff:coloff + nrows], ps[:, q * 32:q * 32 + nrows], [rp], r_dst[cc])

    def k_store(self, tile, j, t1, r1):
        segs = tile["segs"]
        c0, n, s = segs[0]
        self.CP("act", self.Kbuf[:, j, 128:128 + n], t1[:, 0:n], [r1], [self.r_K])
        if tile["kind"] == "halo":
            self.CP("act", tile["Kown"][:, j, :], t1[:, 256:320], [r1], tile["r_Kown"])
            for si in range(1, 5):
                c0, n, s = segs[si]
                ps, rp = self.psum()
                self.TR(ps[0:16, 0:64], t1[0:64, c0:c0 + 16], self.ident[0:64, 0:64], [r1, self.r_ident], [rp])
                self.CP("dve", tile["kso"][0:16, si - 1, j * 64:(j + 1) * 64], ps[0:16, 0:64], [rp], tile["r_kso"])
        if tile.get("last"):
            ps, rp = self.psum()
            self.TR(ps[:, 0:64], t1[0:64, 384:512], self.ident[0:64, 0:64], [r1, self.r_ident], [rp])
            self.CP("dve", tile["kpo"][:, j * 64:(j + 1) * 64], ps[:, 0:64], [rp], tile["r_kpo"])

    def v_store(self, tile, bi, ps, rp):
        c0, n, bk = tile["blocks"][bi]
        if bk["seg"] == 0:
            self.CP("act", self.Vbuf[0:n, 1 + bi, :], ps[0:n, 0:256], [rp], [self.r_V])
            if tile.get("last") and bi == 3:
                self.CP("dve", tile["vpo"][:, :], ps[:, 0:256], [rp], tile["r_vpo"])
        else:
            s = bk["seg"] - 1
            self.CP("act", tile["Vsmp"][0:16, s, :], ps[0:16, 0:256], [rp], tile["r_Vsmp"])
            self.CP("dve", tile["vso"][0:16, s, :], ps[0:16, 0:256], [rp], tile["r_vso"])

    def attn_group(self, Q, r_Q, qc0, nq, ktiles, mix, r_mix, mc0):
        KBY = 1024
        PT = self.av(8 * KBY, [2, 4, 64]); r_PT = self.ares(8 * KBY, 1024)
        nkt = len(ktiles)
        for j in range(4):
            pss = [self.psum(), self.psum()]
            for kt, (Kfn, Vfn, nk, p0, kp, bias, rd) in enumerate(ktiles):
                for g in range(4):
                    h = 4 * j + g
                    half = h % 2
                    ps, rp = pss[half]
                    self.MM(ps[p0:p0 + nk, kt * 128 + (g // 2) * 64: kt * 128 + (g // 2) * 64 + nq], Kfn(j, half),
                            Q[half * 64:(half + 1) * 64, h // 2, qc0:qc0 + nq], True, True, rd + [r_Q[h // 2]], [rp])
            for kt, (Kfn, Vfn, nk, p0, kp, bias, rd) in enumerate(ktiles):
                if nk < kp:
                    self.MEMSET("pool", PT[:, kt, :, :], 0.0, r_PT)
                for half in range(2):
                    ps, rp = pss[half]
                    src = ps[p0:p0 + nk, kt * 128:(kt + 1) * 128].rearrange("p (g q) -> p g q", g=2)[:, :, 0:nq]
                    dst = PT[p0:p0 + nk, kt, :, :].rearrange("p (gg hh) q -> p gg hh q", hh=2)[:, :, half, 0:nq]
                    kw = {"scale": 0.125}
                    rdd = [rp]
                    if bias is not None:
                        kw["bias"] = bias[p0:p0 + nk, :]
                        rdd.append(self.r_flags)
                    self.ACT(dst, src, AF.Exp, rdd, r_PT, **kw)
            psd, rpd = self.psum()
            for kt, (Kfn, Vfn, nk, p0, kp, bias, rd) in enumerate(ktiles):
                self.MM(psd[:, 0:4 * nq].rearrange("p (g q) -> p g q", g=4), self.onesb[0:kp, :], PT[0:kp, kt, :, 0:nq],
                        kt == 0, kt == nkt - 1, [self.r_onesb] + r_PT, [rpd])
            pso, rpo = self.psum()
            for g in range(4):
                ph = (g % 2) * 64
                for kt, (Kfn, Vfn, nk, p0, kp, bias, rd) in enumerate(ktiles):
                    self.MM(pso[ph:ph + 64, (g // 2) * 64:(g // 2) * 64 + nq], Vfn(j), PT[0:kp, kt, g, 0:nq],
                            kt == 0, kt == nkt - 1, rd + r_PT, [rpo])
            den, rden = self.gtmp()
            self.TT("dve", den[:, 0:4 * nq].rearrange("p (g q) -> p g q", g=4), psd[:, 0:4 * nq].rearrange("p (g q) -> p g q", g=4),
                    self.rowc[:, 3, 4 * j:4 * j + 4].unsqueeze(2).to_broadcast([128, 4, nq]), ALU.add, [rpd, self.r_rowc], [rden])
            self.RECIP(den[:, 0:4 * nq], den[:, 0:4 * nq], [rden], [rden])
            for g in range(4):
                ph = (g % 2) * 64
                cc = 2 * j + g // 2
                self.TT("dve", mix[ph:ph + 64, cc, mc0:mc0 + nq], pso[ph:ph + 64, (g // 2) * 64:(g // 2) * 64 + nq],
                        den[ph:ph + 64, g * nq:(g + 1) * nq], ALU.mult, [rpo, rden], [r_mix[cc]])

    def attention(self, tile, Q, r_Q, mix, r_mix):
        P, I = self.P, self.I
        segs = tile["segs"]
        c0, n, s = segs[0]
        first_main = tile.get("first_main", False)
        for qi in range(n // 64):
            qc = qi * 64
            kts = []
            lo = qc - 128
            pieces = [(lo, 128), (lo + 128, 64)] if lo % 128 == 0 else [(lo, 64), (lo + 64, 128)]
            for (k0, nk) in pieces:
                blk = (k0 + 128) // 128
                p0 = (k0 + 128) % 128
                bias = self.flags[:, 1:2] if (first_main and k0 < 0) else None
                Kfn = (lambda j, half, k0=k0, nk=nk: self.Kbuf[half * 64:(half + 1) * 64, j, 128 + k0:128 + k0 + nk])
                Vfn = (lambda j, blk=blk: self.Vbuf[:, blk, j * 64:(j + 1) * 64])
                kts.append((Kfn, Vfn, nk, p0, 128, bias, [self.r_K, self.r_V]))
            self.attn_group(Q, r_Q, c0 + qc, 64, kts, mix, r_mix, c0 + qc)
        if tile["kind"] == "halo":
            Ks, Vc, Vo, Kown = tile["Ksmp"], tile["Vcache"], tile["Vsmp"], tile["Kown"]
            for si in range(1, 5):
                c0, n, s = segs[si]
                sq = si - 1
                stg, r_stg = self.gtmp()
                P.dma("sp", stg[:, :], I["ck"][sq], writes=[r_stg])
                ps, rp = self.psum()
                for j in range(4):
                    self.TR(ps[:, j * 128:(j + 1) * 128], stg[:, j * 128:(j + 1) * 128], self.ident[:], [r_stg, self.r_ident], [rp])
                self.CP("dve", Ks[:, :, 0:128], ps[:, :].rearrange("p (j t) -> p j t", j=4), [rp], tile["r_Ksmp"])
                self.CP("pool", Ks[:, :, 128:144], Kown[:, :, sq * 16:(sq + 1) * 16], tile["r_Kown"], tile["r_Ksmp"])
                stg2, r_stg2 = self.gtmp()
                P.dma("sp", stg2[:, 0:256], I["cv"][sq], writes=[r_stg2])
                self.CP("dve", Vc[:, :], stg2[:, 0:256], [r_stg2], tile["r_Vcache"])
                kts = [
                    ((lambda j, half: Ks[half * 64:(half + 1) * 64, j, 0:128]),
                     (lambda j: Vc[:, j * 64:(j + 1) * 64]), 128, 0, 128, None, tile["r_Ksmp"] + tile["r_Vcache"]),
                    ((lambda j, half: Ks[half * 64:(half + 1) * 64, j, 128:144]),
                     (lambda j, sq=sq: Vo[0:16, sq, j * 64:(j + 1) * 64]), 16, 0, 16, None, tile["r_Ksmp"] + tile["r_Vsmp"]),
                ]
                self.attn_group(Q, r_Q, c0, 16, kts, mix, r_mix, c0)
        c0, n, s = segs[0]
        self.CP("pool", self.Kbuf[:, :, 0:128], self.Kbuf[:, :, n:n + 128], [self.r_K], [self.r_K])
        self.CP("pool", self.Vbuf[:, 0, :], self.Vbuf[:, n // 128, :], [self.r_V], [self.r_V])

    def ssd_block(self, tile, bi, c0, L, bk, xst, r_xst, Bt, r_Bt, BCT, r_BCT, zb, r_z, dtb, r_dtb, mix, r_mix):
        pre = tile["kind"] == "pre"
        KBY = 1024
        base = 8 * KBY
        if bk["seg"] == 0:
            hT, r_hT, hTb, r_hTb = self.hT, [self.r_hT], self.hTb, [self.r_hTb]
        else:
            sq = bk["seg"] - 1
            hT, r_hT = tile["hTs"], tile["r_hTs"]
            hTb, r_hTb = tile["hTsb"], tile["r_hTsb"]
            for g2 in range(2):
                stg3, r_stg3 = self.gtmp()
                self.P.dma("sp", stg3[:, :].rearrange("p (a n) -> p a n", a=4),
                           self.I["st_ssm"][sq].rearrange("(a p) n -> p a n", p=128)[:, g2 * 4:(g2 + 1) * 4, :], writes=[r_stg3])
                ps, rp = self.psum()
                for q in range(4):
                    self.TR(ps[:, q * 128:(q + 1) * 128], stg3[:, q * 128:(q + 1) * 128], self.ident[:], [r_stg3, self.r_ident], [rp])
                self.CP("dve", hT[:, g2 * 512:(g2 + 1) * 512], ps[:, :], [rp], r_hT)
                self.CP("act", hTb[:, g2 * 512:(g2 + 1) * 512], ps[:, :], [rp], r_hTb)
        bufA = self.av(base + 1 * KBY, [4, 128], F32); rA = self.ares(base + 1 * KBY, 2048)
        bufB = self.av(base + 3 * KBY, [4, 128], F32); rB = self.ares(base + 3 * KBY, 2048)
        CBm = self.av(base + 5 * KBY, [128], F32); rCB = self.ares(base + 5 * KBY, 512)
        GT = self.av(base + 6 * KBY, [4, 128]); rG = self.ares(base + 6 * KBY, 1024)
        CsT = self.av(base + 7 * KBY, [4, 128]); rCs = self.ares(base + 7 * KBY, 1024)
        xdt = self.av(base + 8 * KBY, [4, 64]); rxdt = self.ares(base + 8 * KBY, 512)
        xdte = self.av(base + 8 * KBY + 512, [4, 64]); rxdte = self.ares(base + 8 * KBY + 512, 512)
        ysb = self.av(base + 9 * KBY, [1024], F32); rys = self.ares(base + 9 * KBY, 4096)
        ynb = self.av(base + 13 * KBY, [1024]); ryn = self.ares(base + 13 * KBY, 2048)
        dt = dtb[0:L, bi, 0, :]
        dA = dtb[0:L, bi, 1, :]
        acum, racum = self.gsm()
        ps, rp = self.psum()
        self.MM(ps[0:L, 0:16], self.tri[0:L, 0:L], dA, True, True, [self.r_tri] + r_dtb, [rp])
        self.MM(ps[:, 16:32], self.onesf[0:L, :], dA, True, True, [self.r_onesf] + r_dtb, [rp])
        self.CP("dve", acum[0:L, 0:16], ps[0:L, 0:16], [rp], [racum])
        tot = acum[:, 16:32]
        self.CP("dve", tot, ps[:, 16:32], [rp], [racum])
        te = acum[:, 32:48]
        self.TT("dve", te[0:L, :], tot[0:L, :], acum[0:L, 0:16], ALU.subtract, [racum], [racum])
        self.ACT(te[0:L, :], te[0:L, :], AF.Exp, [racum], [racum])
        self.TT("dve", te[0:L, :], te[0:L, :], dt, ALU.mult, [racum] + r_dtb, [racum])
        dec = acum[:, 48:64]
        self.ACT(dec, tot, AF.Exp, [racum], [racum])
        psy = [None, None]
        for g in range(4):
            xs_g = xst[0:L, bi, g * 256:(g + 1) * 256].rearrange("p (r d) -> p r d", r=4)
            if not pre:
                self.TT("dve", bufA[0:L, :, 0:L], self.tri[0:L, 0:L].unsqueeze(1).to_broadcast([L, 4, L]),
                        dA[:, 4 * g:4 * g + 4].unsqueeze(2).to_broadcast([L, 4, L]), ALU.mult, [self.r_tri] + r_dtb, rA)
                ps, rp = self.psum()
                for r in range(4):
                    self.MM(ps[:, r * L:(r + 1) * L], self.onesf[0:L, :], bufA[0:L, r, 0:L], True, True, [self.r_onesf] + rA, [rp])
                self.CP("act", bufB[:, :, 0:L], ps[:, 0:4 * L].rearrange("p (r l) -> p r l", r=4), [rp], rB)
                self.TT("dve", bufA[0:L, :, 0:L], bufB[0:L, :, 0:L], acum[0:L, 4 * g:4 * g + 4].unsqueeze(2).to_broadcast([L, 4, L]),
                        ALU.subtract, rB + [racum], rA)
                self.TS("dve", bufA[0:L, :, 0:L], bufA[0:L, :, 0:L], 0.0, None, ALU.min, None, rA, rA)
                self.ACT(bufA[0:L, :, 0:L], bufA[0:L, :, 0:L], AF.Exp, rA, rA)
                ps2, rp2 = self.psum()
                self.MM(ps2[0:L, 0:L], BCT[:, g, c0:c0 + L], BCT[:, 4 + g, c0:c0 + L], True, True, [r_BCT[g], r_BCT[4 + g]], [rp2])
                self.TT("dve", CBm[0:L, 0:L], ps2[0:L, 0:L], self.tri[0:L, 0:L], ALU.mult, [rp2, self.r_tri], rCB)
                self.TT("dve", GT[0:L, :, 0:L], bufA[0:L, :, 0:L], CBm[0:L, 0:L].unsqueeze(1).to_broadcast([L, 4, L]), ALU.mult,
                        rA + rCB, rG)
                self.ACT(bufB[:, :, 0:L], bufB[:, :, 0:L], AF.Exp, rB, rB)
                self.TT("dve", CsT[:, :, 0:L], bufB[:, :, 0:L], BCT[:, 4 + g, c0:c0 + L].unsqueeze(1).to_broadcast([128, 4, L]), ALU.mult,
                        rB + [r_BCT[4 + g]], rCs)
                self.TT("dve", xdt[0:L, :, :], xs_g, dt[:, 4 * g:4 * g + 4].unsqueeze(2).to_broadcast([L, 4, 64]), ALU.mult,
                        r_xst[bi] + r_dtb, rxdt)
                if g % 2 == 0:
                    psy[g // 2] = (self.ps[6 + g // 2], self.r_ps[6 + g // 2])
                py, rpy = psy[g // 2]
                for r in range(4):
                    h = 4 * g + r
                    col = (h % 8) * 64
                    self.MM(py[0:L, col:col + 64], GT[0:L, r, 0:L], xdt[0:L, r, :], True, False, rG + rxdt, [rpy])
                    self.MM(py[0:L, col:col + 64], CsT[:, r, 0:L], hTb[:, h * 64:(h + 1) * 64], False, True, rCs + r_hTb, [rpy])
            self.TT("dve", xdte[0:L, :, :], xs_g, te[0:L, 4 * g:4 * g + 4].unsqueeze(2).to_broadcast([L, 4, 64]), ALU.mult,
                    r_xst[bi] + [racum], rxdte)
            ps3, rp3 = self.psum()
            self.MM(ps3[:, 0:256], Bt[0:L, bi, g * 128:(g + 1) * 128], xdte[0:L, :, :].rearrange("p r d -> p (r d)"), True, True,
                    r_Bt[bi] + rxdte, [rp3])
            hg = hT[:, g * 256:(g + 1) * 256]
            self.TT("dve", hg.rearrange("p (r d) -> p r d", r=4), hg.rearrange("p (r d) -> p r d", r=4),
                    dec[:, 4 * g:4 * g + 4].unsqueeze(2).to_broadcast([128, 4, 64]), ALU.mult, r_hT + [racum], r_hT)
            self.TT("dve", hg, hg, ps3[:, 0:256], ALU.add, r_hT + [rp3], r_hT)
            self.CP("act", hTb[:, g * 256:(g + 1) * 256], hg, r_hT, r_hTb)
        if bk["seg"] > 0:
            self.out_state(hT, r_hT, self.O["hst_s"][bk["seg"] - 1])
        if pre:
            return
        Dx = self.av(base + 1 * KBY, [1024], F32); rDx = self.ares(base + 1 * KBY, 4096)
        self.TT("dve", Dx[0:L, :].rearrange("p (h d) -> p h d", h=16), xst[0:L, bi, :].rearrange("p (h d) -> p h d", h=16),
                self.rowc[0:L, 2, :].unsqueeze(2).to_broadcast([L, 16, 64]), ALU.mult, r_xst[bi] + [self.r_rowc], rDx)
        for hh in range(2):
            py, rpy = psy[hh]
            self.TT("dve", ysb[0:L, hh * 512:(hh + 1) * 512], py[0:L, :], Dx[0:L, hh * 512:(hh + 1) * 512], ALU.add, [rpy] + rDx, rys)
        self.TT("dve", ysb[0:L, :], ysb[0:L, :], zb[0:L, bi, :], ALU.mult, rys + r_z[bi], rys)
        ssq, rssq = self.gsm()
        self.MEMSET("pool", ssq[:, 0:4], 0.0, [rssq])
        for g in range(4):
            self.ACT(Dx[0:L, g * 256:(g + 1) * 256], ysb[0:L, g * 256:(g + 1) * 256], AF.Square, rys + [rssq], rDx + [rssq],
                     accum_out=ssq[0:L, g:g + 1])
        self.P.op("act", lambda e: e.activation(out=ssq[0:L, 0:4], in_=ssq[0:L, 0:4], func=AF.Sqrt, scale=1.0 / 256, bias=EPS),
                  rDx + [rssq], [rssq])
        self.RECIP(ssq[0:L, 0:4], ssq[0:L, 0:4], [rssq], [rssq])
        self.TT("dve", ynb[0:L, :].rearrange("p (g d) -> p g d", g=4), ysb[0:L, :].rearrange("p (g d) -> p g d", g=4),
                ssq[0:L, 0:4].unsqueeze(2).to_broadcast([L, 4, 256]), ALU.mult, rys + [rssq], ryn)
        c = self.cst
        for hh in range(2):
            ps, rp = self.psum()
            pb = ps[:].bitcast(BF16)
            for q in range(4):
                cc = hh * 4 + q
                self.TR(pb[:, q * 128:q * 128 + L], ynb[0:L, cc * 128:(cc + 1) * 128], self.identb[0:L, 0:L], ryn + [self.r_identb], [rp])
            for q in range(4):
                cc = hh * 4 + q
                self.ACT(mix[:, 8 + cc, c0:c0 + L], pb[:, q * 128:q * 128 + L], AF.Identity, [rp, self.r_cst], [r_mix[8 + cc]],
                         scale=c[:, self.C_NG + cc:self.C_NG + cc + 1])

    def mlp(self, tile, l):
        P, I = self.P, self.I
        T, segs = tile["T"], tile["segs"]
        KB = 1024
        hn = self.av(0, [16, 512]); r_hn = [self.ares(c * KB, KB)[0] for c in range(16)]
        hid = self.av(16 * KB, [32, 512]); r_hid = [self.ares(16 * KB + c * KB, KB)[0] for c in range(32)]
        self.norm(T, segs, l, 1, hn, r_hn)
        for half in range(2):
            def evac_up(ci, ps, rp):
                t, rt = self.gtmp()
                self.ACT(t[:, 0:T], ps[:, 0:T], AF.Relu, [rp], [rt])
                self.TT("pool" if ci % 2 else "dve", hid[:, ci, 0:T], t[:, 0:T], t[:, 0:T], ALU.mult, [rt], [r_hid[ci]])

            self.proj_fm("w_up%d" % l, I["mlp_w_up"][l], DFF, 16, hn, r_hn, T, evac_up, col0=half * 4096, ncols=4096,
                         nunits=32, ubase=half * 16)

            def evac_dn(ci, ps, rp, half=half):
                for (c0, n, s) in segs:
                    self.STT(self.xres[:, ci, c0:c0 + n], ps[:, c0:c0 + n], self.m_gt(l, 1, ci, s), self.xres[:, ci, c0:c0 + n],
                             ALU.mult, ALU.add, [rp, self.r_mod[l][5], self.r_xres[ci]], [self.r_xres[ci]])
                if half == 1:
                    self.stat_chunk(ci, T)

            self.proj_fm("w_dn%d" % l, I["mlp_w_down"][l], D, 32, hid, r_hid, T, evac_dn, kbase=half * 32, nunits=32,
                         ubase=half * 16, unit_cols=128)

    def l1_conf(self, tile):
        P, I = self.P, self.I
        T, segs, kind = tile["T"], tile["segs"], tile["kind"]
        KB = 1024
        c = self.cst
        hn = self.av(0, [16, 512]); r_hn = [self.ares(cc * KB, KB)[0] for cc in range(16)]
        UW = tile["uw"]
        uo = tile["uoff"]
        UB = 16 * KB
        u = self.av(UB, [16, UW]); r_u = [self.ares(UB + cc * UW * 2, UW * 2) for cc in range(16)]
        VB = 34 * KB
        v = self.av(VB, [16, 512], F32); r_v = [self.ares(VB + cc * 2 * KB, 2 * KB) for cc in range(16)]
        hs = hn; r_hs = r_hn
        self.norm(T, segs, 1, 0, hn, r_hn)
        self.CP("pool", u[:, :, uo[0]:uo[0] + 30], self.uctx[:], [self.r_uctx], sum(r_u, []))
        if kind == "halo":
            for s in range(4):
                self.load_ctx_fm(I["st_cconv"][s], 30, u, r_u, uo[1 + s])
        st = {}

        def evac_w1(ci, ps, rp):
            cc, h = ci // 2, ci % 2
            if h == 0:
                st["a"] = (ps, rp)
            else:
                pa, rpa = st["a"]
                t, rt = self.gtmp()
                self.ACT(t[:, 0:T], ps[:, 0:T], AF.Sigmoid, [rp, self.r_cst], [rt], bias=c[:, self.C_B1 + 2 * cc + 1: self.C_B1 + 2 * cc + 2])
                for si, (c0, n, s) in enumerate(segs):
                    self.STT(u[:, cc, uo[si] + 30: uo[si] + 30 + n], pa[:, c0:c0 + n], c[:, self.C_B1 + 2 * cc: self.C_B1 + 2 * cc + 1],
                             t[:, c0:c0 + n], ALU.add, ALU.mult, [rpa, rt, self.r_cst], r_u[cc])
                    if si == 0:
                        self.STT(self.uctx[:, cc, :], pa[:, c0 + n - 30:c0 + n], c[:, self.C_B1 + 2 * cc: self.C_B1 + 2 * cc + 1],
                                 t[:, c0 + n - 30:c0 + n], ALU.add, ALU.mult, [rpa, rt, self.r_cst], [self.r_uctx])
                    else:
                        self.STT(tile["cconvf"][:, cc, (si - 1) * 16: si * 16], pa[:, c0:c0 + 16], c[:, self.C_B1 + 2 * cc: self.C_B1 + 2 * cc + 1],
                                 t[:, c0:c0 + 16], ALU.add, ALU.mult, [rpa, rt, self.r_cst], tile["r_cconvf"])

        self.proj_fm("w1", I["w1_ext"], 2 * D, 16, hn, r_hn, T, evac_w1)
        if P.dry:
            self.proj_fm("w2", I["conf_w2"], D, 16, hs, r_hs, T, None)
            return
        if kind == "halo":
            ranges = [(uo[0], 256, [(0, 0, 256)]), (uo[1], 3 * 46 + 16, None)]
        else:
            ranges = [(uo[0], segs[0][1], [(0, 0, segs[0][1])])]
        for cc in range(16):
            pss = [self.psum() for _ in ranges]
            for j in range(31):
                dg, rdg = self.gdg()
                self.TS("dve", dg, self.identb[:], c[:, self.C_DWW + j * 16 + cc: self.C_DWW + j * 16 + cc + 1], None, ALU.mult, None,
                        [self.r_identb, self.r_cst], [rdg])
                for (o, n, _), (ps, rp) in zip(ranges, pss):
                    self.MM(ps[:, 0:n], dg, u[:, cc, o + j:o + j + n], j == 0, j == 30, [rdg] + r_u[cc], [rp])
            bias = c[:, self.C_DWB + cc: self.C_DWB + cc + 1]
            ps, rp = pss[0]
            n0 = ranges[0][1]
            self.ACT(v[:, cc, 0:n0], ps[:, 0:n0], AF.Identity, [rp, self.r_cst], r_v[cc], bias=bias)
            if kind == "halo":
                ps, rp = pss[1]
                self.ACT(v[:, cc, 256:320].rearrange("p (s t) -> p s t", s=4),
                         ps[:, 0:184].rearrange("p (s w) -> p s w", w=46)[:, :, 0:16], AF.Identity, [rp, self.r_cst], r_v[cc], bias=bias)
        ps1, rp1 = self.psum()
        ps2, rp2 = self.psum()
        for cc in range(16):
            a, ra = self.gsqb()
            self.CP("act", a[:, 0:T], v[:, cc, 0:T], r_v[cc], [ra])
            self.MM(ps1[:, 0:T], self.onesb[:], a[:, 0:T], cc == 0, cc == 15, [self.r_onesb, ra], [rp1])
            b, rb = self.gsqb()
            self.ACT(b[:, 0:T], v[:, cc, 0:T], AF.Square, r_v[cc], [rb])
            self.MM(ps2[:, 0:T], self.onesb[:], b[:, 0:T], cc == 0, cc == 15, [self.r_onesb, rb], [rp2])
        mean, rmean = self.gtmp()
        self.TS("dve", mean[:, 0:T], ps1[:, 0:T], 1.0 / D, None, ALU.mult, None, [rp1], [rmean])
        msq, rmsq = self.gtmp()
        self.TT("dve", msq[:, 0:T], mean[:, 0:T], mean[:, 0:T], ALU.mult, [rmean], [rmsq])
        self.STT(msq[:, 0:T], ps2[:, 0:T], 1.0 / D, msq[:, 0:T], ALU.mult, ALU.subtract, [rp2, rmsq], [rmsq])
        self.ACT(msq[:, 0:T], msq[:, 0:T], AF.Sqrt, [rmsq], [rmsq], bias=EPS)
        self.RECIP(msq[:, 0:T], msq[:, 0:T], [rmsq], [rmsq])
        for cc in range(16):
            self.TT("dve", v[:, cc, 0:T], v[:, cc, 0:T], mean[:, 0:T], ALU.subtract, r_v[cc] + [rmean], r_v[cc])
            self.TT("pool", v[:, cc, 0:T], v[:, cc, 0:T], msq[:, 0:T], ALU.mult, r_v[cc] + [rmsq], r_v[cc])
            self.ACT(hs[:, cc, 0:T], v[:, cc, 0:T], AF.Silu, r_v[cc] + [self.r_cst], [r_hs[cc]],
                     scale=c[:, self.C_LNG + cc:self.C_LNG + cc + 1], bias=c[:, self.C_LNB + cc:self.C_LNB + cc + 1])

        def evac_w2(ci, ps, rp):
            for (c0, n, s) in segs:
                self.STT(self.xres[:, ci, c0:c0 + n], ps[:, c0:c0 + n], self.m_gt(1, 0, ci, s), self.xres[:, ci, c0:c0 + n],
                         ALU.mult, ALU.add, [rp, self.r_mod[1][2], self.r_xres[ci]], [self.r_xres[ci]])
                self.TS("dve", self.xres[:, ci, c0:c0 + n], self.xres[:, ci, c0:c0 + n], self.b2g[:, ci, s:s + 1], None, ALU.add, None,
                        [self.r_xres[ci], self.r_b2g], [self.r_xres[ci]])
            self.stat_chunk(ci, T)

        self.proj_fm("w2", I["conf_w2"], D, 16, hs, r_hs, T, evac_w2)

    def out_y(self, tile):
        P = self.P
        T, segs = tile["T"], tile["segs"]
        c = self.cst
        self.norm(T, segs, 0, 0, None, None, final=True)
        for (dst, c0, n) in tile["yout"]:
            pass
        nb = len(tile["yout"])
        stg = [self.av(i * 8192, [2048], F32) for i in range(4)]
        r_stg = [self.ares(i * 8192, 8192) for i in range(4)]
        for g in range(4):
            ts = []
            for q in range(4):
                cc = g * 4 + q
                t, rt = self.gtmp()
                self.STT(t[:, 0:T], self.xres[:, cc, 0:T], c[:, self.C_GFIN + cc:self.C_GFIN + cc + 1], self.rstd[:, 0:T], ALU.mult, ALU.mult,
                         [self.r_xres[cc], self.r_cst, self.r_rstd], [rt])
                ts.append((t, rt))
            for bi, (dst, c0, n) in enumerate(tile["yout"]):
                ps, rp = self.psum()
                for q in range(4):
                    t, rt = ts[q]
                    self.TR(ps[0:n, q * 128:(q + 1) * 128], t[:, c0:c0 + n], self.ident[:], [rt, self.r_ident], [rp])
                self.CP("act" if bi % 2 else "dve", stg[bi % 4][0:n, g * 512:(g + 1) * 512], ps[0:n, :], [rp], r_stg[bi % 4])
        for bi, (dst, c0, n) in enumerate(tile["yout"]):
            P.dma("sp", dst, stg[bi % 4][0:n, :], reads=r_stg[bi % 4])

    def out_fm_rows(self, src_fm, rd, nrows, dst):
        stg = self.av(32 * 1024, [2048], F32)
        r_stg = self.ares(32 * 1024, 8192)
        for g in range(4):
            ps, rp = self.psum()
            for q in range(4):
                cc = g * 4 + q
                self.TR(ps[0:nrows, q * 128:(q + 1) * 128], src_fm[:, cc, :], self.ident[:], rd + [self.r_ident], [rp])
            self.CP("dve", stg[0:nrows, g * 512:(g + 1) * 512], ps[0:nrows, :], [rp], r_stg)
        self.P.dma("sp", dst, stg[0:nrows, :], reads=r_stg)

    def out_state(self, hT, r_hT, dst):
        dv = dst.rearrange("(a p) n -> p a n", p=128)
        for g in range(2):
            stg, r_stg = self.gtmp()
            ps, rp = self.psum()
            for q in range(4):
                self.TR(ps[:, q * 128:(q + 1) * 128], hT[:, (g * 4 + q) * 128:(g * 4 + q + 1) * 128], self.ident[:], r_hT + [self.r_ident], [rp])
            self.CP("dve", stg[:, :], ps[:, :], [rp], [r_stg])
            self.P.dma("sp", dv[:, g * 4:(g + 1) * 4, :], stg[:, :].rearrange("p (a n) -> p a n", a=4), reads=[r_stg])

    def emit(self, P):
        self.P = P
        I, O = self.I, self.O
        self.ps_i = self.tmp_i = self.sqb_i = self.sm_i = self.dg_i = 0
        self.stat_pend = []
        self.stat_n = 0
        self.pt_i = 0
        if P.dry:
            self.wsched = []
            self.pc_list = []
        else:
            self.w_issued = self.w_consumed = 0
            seen = set()
            self.pc_list = []
            for spec in self.wsched:
                if not spec[6] and (spec[0], spec[1]) not in seen:
                    seen.add((spec[0], spec[1]))
                    self.pc_list.append(spec)
        self.pc_i = 0
        self.emit_setup()
        KB = 1024
        KSTOP = int(os.environ.get("KSTOP", "9"))
        if KSTOP < 1:
            if not P.dry:
                P.finish()
            return
        nb = self.npre
        b0 = 0
        while b0 < nb:
            nblk = min(4, nb - b0)
            T = nblk * 128
            tile = dict(kind="pre", T=T, segs=[(0, T, 0)], xpw=3 + T, segoff=[0],
                        blocks=[(i * 128, 128, dict(seg=0)) for i in range(nblk)])
            self.load_x([(I["x_pre"][(b0 + i) * 128:(b0 + i + 1) * 128, :], i * 128, 128) for i in range(nblk)])
            self.l0_mixer(tile)
            b0 += nblk
            self.adaln_flush(11)
            if int(os.environ.get("KSUB", "99")) < 99:
                break
        if KSTOP < 2:
            if not P.dry:
                P.finish()
            return
        self.adaln_finish()
        while not P.dry and self.pc_i < len(self.pc_list):
            self.precast_next()
        T = 320
        segs = [(0, 256, 0)] + [(256 + 16 * s, 16, 1 + s) for s in range(4)]
        blocks = [(0, 128, dict(seg=0)), (128, 128, dict(seg=0))] + [(256 + 16 * s, 16, dict(seg=1 + s)) for s in range(4)]
        S = self.S
        tile = dict(kind="halo", T=T, segs=segs, blocks=blocks, rc0=0,
                    xpw=3 + 256 + 4 * 19, segoff=[0] + [259 + 19 * s for s in range(4)],
                    uw=30 + 256 + 4 * 46, uoff=[0] + [286 + 46 * s for s in range(4)])
        if not hasattr(self, "halo_bufs"):
            hb = self.halo_bufs = {}
            hb["Kown"] = S("Kown", [128, 4, 64], BF16)
            hb["Ksmp"] = S("Ksmp", [128, 4, 144], BF16)
            hb["Vcache"] = S("Vcache", [128, 256], BF16)
            hb["Vsmp"] = S("Vsmp", [16, 4, 256], BF16)
            hb["hTs"] = S("hTs", [128, 1024])
            hb["hTsb"] = S("hTsb", [128, 1024], BF16)
            hb["sconvf"] = S("sconvf", [128, 4, 16, 3])
            hb["cconvf"] = S("cconvf", [128, 16, 64])
            for k in list(hb.keys()):
                hb["r_" + k] = [Res(k)]
            hb["kso"] = self.av(53 * KB, [4, 256], F32); hb["r_kso"] = self.ares(53 * KB, 4096)
            hb["vso"] = self.av(57 * KB, [4, 256], F32); hb["r_vso"] = self.ares(57 * KB, 4096)
            hb["kpo"] = self.av(61 * KB, [256], F32); hb["r_kpo"] = self.ares(61 * KB, 1024)
            hb["vpo"] = self.av(62 * KB, [256], F32); hb["r_vpo"] = self.ares(62 * KB, 1024)
            print("SBUF bytes/partition:", self.sb_bytes)
        tile.update(self.halo_bufs)
        hb = self.halo_bufs
        self.load_x([(I["x_main"][0:128, :], 0, 128), (I["x_main"][128:256, :], 128, 128), (I["x_smp"], 256, 64)])
        self.l0_mixer(tile)
        self.mlp(tile, 0)
        self.l1_conf(tile)
        self.mlp(tile, 1)
        tile["yout"] = [(O["y_smp"], 256, 64)]
        self.out_y(tile)
        if not P.dry:
            for sq in range(4):
                self.out_fm_rows(hb["sconvf"][:, sq, :, :], hb["r_sconvf"], 3, O["sconv_s"][sq])
                P.dma("sp", O["cconv_s"][sq, 0:14, :], I["st_cconv"][sq, 16:30, :])
                self.out_fm_rows(hb["cconvf"][:, :, sq * 16:(sq + 1) * 16], hb["r_cconvf"], 16, O["cconv_s"][sq, 14:30, :])
            f = self.flags[:, 0:1]
            self.TS("dve", self.hT[:], self.hT[:], f, None, ALU.mult, None, [self.r_hT, self.r_flags], [self.r_hT])
            self.TS("dve", self.hTb[:], self.hTb[:], f, None, ALU.mult, None, [self.r_hTb, self.r_flags], [self.r_hTb])
            self.TS("dve", self.xpctx[:], self.xpctx[:], f, None, ALU.mult, None, [self.r_xpctx, self.r_flags], [self.r_xpctx])
            self.TS("dve", self.uctx[:], self.uctx[:], f, None, ALU.mult, None, [self.r_uctx, self.r_flags], [self.r_uctx])
        if KSTOP < 3:
            if not P.dry:
                P.finish()
            return
        for ti in range(self.nmain):
            T = 512
            tile = dict(kind="main", T=T, segs=[(0, 512, 0)], blocks=[(i * 128, 128, dict(seg=0)) for i in range(4)],
                        rc0=320 + 512 * ti, xpw=515, segoff=[0], uw=542, uoff=[0], first_main=(ti == 0), last=(ti == self.nmain - 1))
            tile.update(self.halo_bufs)
            base = HALO + ti * 512
            self.load_x([(I["x_main"][base + i * 128: base + (i + 1) * 128, :], i * 128, 128) for i in range(4)])
            self.l0_mixer(tile)
            self.mlp(tile, 0)
            self.l1_conf(tile)
            self.mlp(tile, 1)
            tile["yout"] = [(O["y_main"][ti * 512 + i * 128: ti * 512 + (i + 1) * 128, :], i * 128, 128) for i in range(4)]
            self.out_y(tile)
        if not P.dry:
            self.out_state(self.hT[:], [self.r_hT], O["hst_p"])
            self.out_fm_rows(self.xpctxf[:, :, :], [self.r_xpctxf], 3, O["sconv_p"])
            self.out_fm_rows(self.uctx[:, :, :], [self.r_uctx], 30, O["cconv_p"])
            P.finish()


def build_program(npre, nmain, dbg=()):
    nc = bass.Bass("TRN2", target_bir_lowering=False)
    st = ExitStack()
    K = Kern(nc, st, npre, nmain, dbg)
    dry = Prog(dry=True)
    K.emit(dry)
    P = Prog()
    K.emit(P)
    P.build(nc, st)
    st.close()
    return nc, K, P


def _prep_weights(inp):
    w_in = np.asarray(inp["w_in"][0], np.float32)
    q = w_in[:, 0:1024]
    k = w_in[:, 1024:1280]
    sw = np.concatenate([np.arange(32, 64), np.arange(0, 32)])
    cols = []
    for c in range(8):
        qc = q[:, c * 128:(c + 1) * 128]
        qs = np.concatenate([qc[:, 0:64][:, sw], qc[:, 64:128][:, sw]], axis=1)
        cols += [qc, qs]
    for j in range(4):
        kj = k[:, j * 64:(j + 1) * 64]
        cols += [kj, kj, kj[:, sw], kj[:, sw]]
    cols += [w_in[:, 1280:1536], w_in[:, 1536:2560], w_in[:, 2560:4608], w_in[:, 4608:4624]]
    w_in_ext = np.ascontiguousarray(np.concatenate(cols, axis=1))
    assert w_in_ext.shape[1] == WEXT
    w1 = np.asarray(inp["conf_w1"][0], np.float32)
    b1 = np.asarray(inp["conf_b1"][0], np.float32)
    c1, bb = [], []
    for c in range(16):
        c1 += [w1[:, c * 128:(c + 1) * 128], w1[:, 2048 + c * 128: 2048 + (c + 1) * 128]]
        bb += [b1[c * 128:(c + 1) * 128], b1[2048 + c * 128: 2048 + (c + 1) * 128]]
    w1_ext = np.ascontiguousarray(np.concatenate(c1, axis=1))
    b1_ext = np.ascontiguousarray(np.concatenate(bb))[None, :]
    f = lambda a: np.ascontiguousarray(np.asarray(a, np.float32))
    W = dict(
        w_mod=f(inp["w_mod"]), b_mod=f(inp["b_mod"]), g_mix=f(inp["g_mix"]), g_mlp=f(inp["g_mlp"]),
        w_in_ext=w_in_ext, w_out=f(inp["w_out"][0]), attn_sinks=f(inp["attn_sinks"]), ssm_a_log=f(inp["ssm_a_log"]),
        ssm_dt_bias=f(inp["ssm_dt_bias"]), ssm_d=f(inp["ssm_d"]), ssm_conv_w=f(inp["ssm_conv_w"][0]),
        ssm_conv_b=f(inp["ssm_conv_b"]), ssm_norm_g=f(inp["ssm_norm_g"]), w1_ext=w1_ext, b1_ext=b1_ext,
        conf_dw_w=f(inp["conf_dw_w"][0]), conf_dw_b=f(inp["conf_dw_b"]), conf_ln_g=f(inp["conf_ln_g"]),
        conf_ln_b=f(inp["conf_ln_b"]), conf_w2=f(inp["conf_w2"][0]), conf_b2=f(inp["conf_b2"]),
        mlp_w_up=f(inp["mlp_w_up"]), mlp_w_down=f(inp["mlp_w_down"]), g_final=f(inp["g_final"])[None, :],
    )
    return W


def _rope_tables(pos):
    p = np.arange(128)
    d = (p % 64) % 32
    inv = 10000.0 ** (-d.astype(np.float64) / 32.0)
    ang = inv[:, None] * pos[None, :].astype(np.float64)
    ang32 = (pos[None, :].astype(np.float32) * (10000.0 ** (-(d.astype(np.float32)) / 32.0)).astype(np.float32)[:, None]).astype(np.float32)
    cos = np.cos(ang32.astype(np.float64)).astype(np.float32)
    sin = np.sin(ang32.astype(np.float64)).astype(np.float32)
    sign = np.where((p % 64) < 32, -1.0, 1.0).astype(np.float32)[:, None]
    return np.ascontiguousarray(cos), np.ascontiguousarray(sin * sign)


def run(inp, seq=SEQ, dbg=(), trace=False):
    half = seq // 2
    nmain = half // 512
    npre = (half - HALO) // 128
    nc, K, P = build_program(npre, nmain, dbg)
    W = _prep_weights(inp)
    xp = np.asarray(inp["x_prompt"], np.float32)
    xs = np.asarray(inp["x_sample"], np.float32)
    in_maps = []
    for core in range(NCORES):
        b, hf = core // 2, core % 2
        m = dict(W)
        if hf == 1:
            m["x_pre"] = np.ascontiguousarray(xp[b, 0:max(npre, 1) * 128])
            m["x_main"] = np.ascontiguousarray(xp[b, half - HALO: seq])
            pos0 = half - HALO
            flags = np.tile(np.array([[1.0, 0.0]], np.float32), (128, 1))
        else:
            m["x_pre"] = np.zeros((max(npre, 1) * 128, D), np.float32)
            m["x_main"] = np.ascontiguousarray(np.concatenate([np.zeros((HALO, D), np.float32), xp[b, 0:half]], axis=0))
            pos0 = -HALO
            flags = np.tile(np.array([[0.0, NEG]], np.float32), (128, 1))
        sl = slice(core * 4, core * 4 + 4)
        m["x_smp"] = np.ascontiguousarray(xs[sl].reshape(64, D))
        m["c5"] = np.ascontiguousarray(np.concatenate([np.asarray(inp["c_prompt"], np.float32)[b:b + 1],
                                                       np.asarray(inp["c_sample"], np.float32)[sl]], axis=0))
        ck = np.asarray(inp["cache_swa_k"], np.float32)[0, sl]
        m["ck"] = np.ascontiguousarray(np.stack([ck, ck], axis=3).reshape(4, 128, 512))
        m["cv"] = np.ascontiguousarray(np.asarray(inp["cache_swa_v"], np.float32)[0, sl].reshape(4, 128, 256))
        m["st_ssm"] = np.ascontiguousarray(np.asarray(inp["state_ssm"], np.float32)[0, sl].reshape(4, 1024, 128))
        m["st_sconv"] = np.ascontiguousarray(np.asarray(inp["state_ssm_conv"], np.float32)[0, sl])
        m["st_cconv"] = np.ascontiguousarray(np.asarray(inp["state_conf_conv"], np.float32)[0, sl])
        pos = np.concatenate([pos0 + np.arange(HALO), np.tile(PAST_LEN + np.arange(16), 4),
                              pos0 + HALO + np.arange(half)]).astype(np.float64)
        m["rope_cos"], m["rope_sin"] = _rope_tables(pos)
        m["flags"] = flags
        in_maps.append(m)
    res = run_bass_kernel_spmd(nc, in_maps, core_ids=list(range(NCORES)), **({"trace": True} if trace else {}))
    R = res.results
    B = xp.shape[0]
    y_prompt = np.zeros((B, seq, D), np.float32)
    for core in range(NCORES):
        b, hf = core // 2, core % 2
        y_prompt[b, hf * half:(hf + 1) * half] = R[core]["y_main"]
    y_sample = np.concatenate([R[c]["y_smp"].reshape(4, 16, D) for c in range(NCORES)], axis=0)
    last = [R[2 * b + 1] for b in range(B)]
    swa_k_p = np.stack([r["k_p"].reshape(128, 4, 64) for r in last])[None]
    swa_v_p = np.stack([r["v_p"].reshape(128, 4, 64) for r in last])[None]
    swa_k_s = np.concatenate([R[c]["k_s"].reshape(4, 16, 4, 64) for c in range(NCORES)], axis=0)[None]
    swa_v_s = np.concatenate([R[c]["v_s"].reshape(4, 16, 4, 64) for c in range(NCORES)], axis=0)[None]
    st_p = np.stack([r["hst_p"].reshape(16, 64, 128) for r in last])[None]
    st_s = np.concatenate([R[c]["hst_s"].reshape(4, 16, 64, 128) for c in range(NCORES)], axis=0)[None]
    sc_p = np.stack([r["sconv_p"] for r in last])[None]
    sc_s = np.concatenate([R[c]["sconv_s"] for c in range(NCORES)], axis=0)[None]
    cc_p = np.stack([r["cconv_p"] for r in last])[None]
    cc_s = np.concatenate([R[c]["cconv_s"] for c in range(NCORES)], axis=0)[None]
    outs = (y_prompt, y_sample, swa_k_p, swa_v_p, swa_k_s, swa_v_s, st_p, st_s, sc_p, sc_s, cc_p, cc_s)
    outs = tuple(np.ascontiguousarray(o.astype(np.float32)) for o in outs)
    return outs, res, R


def kernel(**inputs):
    outs, _, _ = run(inputs)
    return outs
```

```python
import math, os, sys
from contextlib import ExitStack
KDEBUG = bool(os.environ.get("KDEBUG"))
import numpy as np
import concourse.bass as bass
import concourse.mybir as mybir
from concourse.bass_utils import run_bass_kernel_spmd

F32 = mybir.dt.float32
BF16 = mybir.dt.bfloat16
ALU = mybir.AluOpType
AF = mybir.ActivationFunctionType

D = 2048
KC = 16
DFF = 8192
EPS = 1e-6
SEQ = 8192
NCORES = 8
HALO = 256
PAST_LEN = 1024
WEXT = 3072 + 256 + 1024 + 2048 + 16
QK0, V0, Z0, XBC0, DT0 = 0, 3072, 3328, 4352, 6400
NEG = -30000.0


class Res:
    __slots__ = ("name", "w", "rs", "excl")

    def __init__(self, name="", excl=False):
        self.name = name
        self.w = None
        self.rs = []
        self.excl = excl


class Engine:
    def __init__(self, name):
        self.name = name
        self.ops = []
        self.count = 0
        self.known = {}
        self.is_pe = name == "pe"


class Prog:
    def __init__(self, dry=False):
        self.dry = dry
        self.sems = {}
        self.sem_names = []
        self.E = {n: Engine(n) for n in ("pe", "act", "dve", "pool", "sp")}
        for n in self.E:
            self.sem_names.append("eng_" + n)
        self.dma_pools = {}
        for q in ("sp", "pool"):
            keys = [f"dq_{q}_{i}" for i in range(8)]
            self.sem_names += keys
            self.dma_pools[q] = {"keys": keys, "cnt": [0] * len(keys), "i": 0}
        self.n_instr = 0

    def _need(self, eng, ev, same_ok=False):
        if ev is None:
            return
        key, val, ename = ev
        if same_ok and ename == eng.name:
            return
        if eng.known.get(key, 0) >= val:
            return
        eng.known[key] = val
        eng.ops.append(lambda e, key=key, val=val: e.wait_ge(self.sems[key], val))

    def _deps(self, eng, reads, writes):
        for r in reads:
            self._need(eng, r.w, same_ok=False)
            if r.excl:
                for ev in r.rs:
                    self._need(eng, ev, same_ok=True)
        for w in writes:
            self._need(eng, w.w, same_ok=eng.is_pe)
            for ev in w.rs:
                self._need(eng, ev, same_ok=(eng.name != "pool"))

    def _commit(self, ev, reads, writes):
        for r in reads:
            r.rs.append(ev)
            if len(r.rs) > 48:
                best = {}
                for e in r.rs:
                    if e[0] not in best or best[e[0]][1] < e[1]:
                        best[e[0]] = e
                r.rs = list(best.values())
        for w in writes:
            w.w = ev
            w.rs = []

    def op(self, engname, fn, reads=(), writes=()):
        if self.dry:
            return None
        if KDEBUG:
            fr = sys._getframe(2)
            lab = f"{fr.f_code.co_name}:{fr.f_lineno}"
            fn0 = fn
            fn = lambda e, fn0=fn0, lab=lab: fn0(e).annotate(lab)
        eng = self.E[engname]
        self._deps(eng, reads, writes)
        eng.count += 1
        key = "eng_" + engname
        ev = (key, eng.count, engname)
        eng.ops.append(lambda e, fn=fn, key=key: fn(e).then_inc(self.sems[key], 1))
        self._commit(ev, reads, writes)
        self.n_instr += 1
        return ev

    def dma(self, q, out_ap, in_ap, reads=(), writes=(), **kw):
        if self.dry:
            return None
        eng = self.E[q]
        pool = self.dma_pools[q]
        i = pool["i"]
        pool["i"] = (i + 1) % len(pool["keys"])
        key = pool["keys"][i]
        prev = pool["cnt"][i]
        pool["cnt"][i] = prev + 16
        if prev > 0:
            self._need(eng, (key, prev, "dma"))
        self._deps(eng, reads, writes)
        ev = (key, prev + 16, "dma")
        eng.ops.append(lambda e, key=key, o=out_ap, i_=in_ap, kw=kw:
                       e.dma_start(out=o, in_=i_, **kw).then_inc(self.sems[key], 16))
        self._commit(ev, reads, writes)
        self.n_instr += 1
        return ev

    def finish(self):
        for q, pool in self.dma_pools.items():
            for key, cnt in zip(pool["keys"], pool["cnt"]):
                if cnt:
                    self._need(self.E["sp"], (key, cnt, "dma"))

    def build(self, nc, stack):
        for key in self.sem_names:
            self.sems[key] = stack.enter_context(nc.semaphore(key))
        block = stack.enter_context(nc.Block())
        E = self.E

        @block.tensor
        def _(e):
            for f in E["pe"].ops:
                f(e)

        @block.scalar
        def _(e):
            for f in E["act"].ops:
                f(e)

        @block.vector
        def _(e):
            for f in E["dve"].ops:
                f(e)

        @block.gpsimd
        def _(e):
            for f in E["pool"].ops:
                f(e)

        @block.sync
        def _(e):
            for f in E["sp"].ops:
                f(e)


class Kern:
    def __init__(self, nc, st, npre, nmain, dbg=()):
        self.nc, self.st = nc, st
        self.npre, self.nmain = npre, nmain
        self.dbg = set(dbg)
        self.dbg_out = {}
        self.ncols = 320 + 512 * nmain
        self._decl()
        self._alloc()

    def _decl(self):
        nc = self.nc
        di = lambda n, s: nc.dram_tensor(n, list(s), F32, kind="ExternalInput").ap()
        do = lambda n, s: nc.dram_tensor(n, list(s), F32, kind="ExternalOutput").ap()
        I = self.I = {}
        O = self.O = {}
        I["x_pre"] = di("x_pre", (max(self.npre, 1) * 128, D))
        I["x_main"] = di("x_main", (HALO + 512 * self.nmain, D))
        I["x_smp"] = di("x_smp", (64, D))
        I["c5"] = di("c5", (5, D))
        I["ck"] = di("ck", (4, 128, 512))
        I["cv"] = di("cv", (4, 128, 256))
        I["st_ssm"] = di("st_ssm", (4, 1024, 128))
        I["st_sconv"] = di("st_sconv", (4, 3, D))
        I["st_cconv"] = di("st_cconv", (4, 30, D))
        I["rope_cos"] = di("rope_cos", (128, self.ncols))
        I["rope_sin"] = di("rope_sin", (128, self.ncols))
        I["flags"] = di("flags", (128, 2))
        I["w_mod"] = di("w_mod", (2, D, 6 * D))
        I["b_mod"] = di("b_mod", (2, 6 * D))
        I["g_mix"] = di("g_mix", (2, D))
        I["g_mlp"] = di("g_mlp", (2, D))
        I["w_in_ext"] = di("w_in_ext", (D, WEXT))
        I["w_out"] = di("w_out", (D, D))
        I["attn_sinks"] = di("attn_sinks", (1, 16))
        I["ssm_a_log"] = di("ssm_a_log", (1, 16))
        I["ssm_dt_bias"] = di("ssm_dt_bias", (1, 16))
        I["ssm_d"] = di("ssm_d", (1, 16))
        I["ssm_conv_w"] = di("ssm_conv_w", (4, D))
        I["ssm_conv_b"] = di("ssm_conv_b", (1, D))
        I["ssm_norm_g"] = di("ssm_norm_g", (1, 1024))
        I["w1_ext"] = di("w1_ext", (D, 2 * D))
        I["b1_ext"] = di("b1_ext", (1, 2 * D))
        I["conf_dw_w"] = di("conf_dw_w", (31, D))
        I["conf_dw_b"] = di("conf_dw_b", (1, D))
        I["conf_ln_g"] = di("conf_ln_g", (1, D))
        I["conf_ln_b"] = di("conf_ln_b", (1, D))
        I["conf_w2"] = di("conf_w2", (D, D))
        I["conf_b2"] = di("conf_b2", (1, D))
        I["mlp_w_up"] = di("mlp_w_up", (2, D, DFF))
        I["mlp_w_down"] = di("mlp_w_down", (2, DFF, D))
        I["g_final"] = di("g_final", (1, D))
        O["y_main"] = do("y_main", (512 * self.nmain, D))
        O["y_smp"] = do("y_smp", (64, D))
        O["k_p"] = do("k_p", (128, 256))
        O["v_p"] = do("v_p", (128, 256))
        O["k_s"] = do("k_s", (64, 256))
        O["v_s"] = do("v_s", (64, 256))
        O["hst_p"] = do("hst_p", (1024, 128))
        O["hst_s"] = do("hst_s", (4, 1024, 128))
        O["sconv_p"] = do("sconv_p", (3, D))
        O["sconv_s"] = do("sconv_s", (4, 3, D))
        O["cconv_p"] = do("cconv_p", (30, D))
        O["cconv_s"] = do("cconv_s", (4, 30, D))
        self.scr = {}
        self.scr_res = {}

    def _scratch(self, name, nunits):
        if name not in self.scr:
            self.scr[name] = self.nc.dram_tensor("scr_" + name, [nunits, 128, 4096], BF16, kind="Internal").ap()
        return self.scr[name]

    def _alloc(self):
        nc, st = self.nc, self.st
        self.sb_bytes = 0

        def S(name, shape, dt=F32):
            n = 1
            for s in shape[1:]:
                n *= s
            self.sb_bytes += n * (4 if dt == F32 else 2)
            return st.enter_context(nc.sbuf_tensor("s_" + name, list(shape), dt))

        self.S = S
        self.xres = S("xres", [128, 16, 512])
        self.r_xres = [Res(f"xres{c}") for c in range(16)]
        self.NW = 3
        self.wsl = [S(f"wsl{i}", [128, 4096], BF16) for i in range(self.NW)]
        self.r_wsl = [Res(f"wsl{i}") for i in range(self.NW)]
        self.AR = 72 * 1024
        self.arena = S("arena", [128, self.AR // 2], BF16)
        self.r_ar = [Res(f"ar{i}") for i in range(self.AR // 1024)]
        self.ident = S("ident", [128, 128]); self.r_ident = Res("ident")
        self.identb = S("identb", [128, 128], BF16); self.r_identb = Res("identb")
        self.tri = S("tri", [128, 128]); self.r_tri = Res("tri")
        self.onesf = S("onesf", [128, 128]); self.r_onesf = Res("onesf")
        self.onesb = S("onesb", [128, 128], BF16); self.r_onesb = Res("onesb")
        self.diag = S("diag", [128, 64, 128], BF16); self.r_diag = Res("diag")
        self.modT = S("modT", [128, 2, 96, 5]); self.r_mod = [[Res(f"mod{l}{g}") for g in range(6)] for l in range(2)]
        self.gs = S("gs", [128, 2, 2, 16, 5]); self.r_gsv = [[Res(f"gs{l}{w}") for w in range(2)] for l in range(2)]
        self.bmT = S("bmT", [128, 192])
        self.b2g = S("b2g", [128, 16, 5]); self.r_b2g = Res("b2g")
        self.cst = S("cst", [128, 800]); self.r_cst = Res("cst")
        self.C_GMIX, self.C_GMLP, self.C_GFIN, self.C_CB, self.C_B1 = 0, 32, 64, 80, 96
        self.C_DWB, self.C_LNG, self.C_LNB, self.C_B2, self.C_NG, self.C_CW, self.C_DWW = 128, 144, 160, 176, 192, 200, 264
        self.rowc = S("rowc", [128, 5, 16]); self.r_rowc = Res("rowc")
        self.cbrow = S("cbrow", [1, 1536], BF16); self.r_cbrow = Res("cbrow")
        self.flags = S("flags", [128, 2]); self.r_flags = Res("flags")
        self.scT = S("scT", [128, 16, 5], BF16); self.r_scT = Res("scT")
        self.dgb = S("dgb", [128, 12, 128], BF16); self.r_dgb = [Res(f"dgb{i}") for i in range(12)]
        self.dg_i = 0
        self.Kbuf = S("Kbuf", [128, 4, 640], BF16); self.r_K = Res("Kbuf")
        self.Vbuf = S("Vbuf", [128, 5, 256], BF16); self.r_V = Res("Vbuf")
        self.hT = S("hT", [128, 1024]); self.r_hT = Res("hT")
        self.hTb = S("hTb", [128, 1024], BF16); self.r_hTb = Res("hTb")
        self.xpctx = S("xpctx", [128, 16, 3], BF16); self.r_xpctx = Res("xpctx")
        self.xpctxf = S("xpctxf", [128, 16, 3]); self.r_xpctxf = Res("xpctxf")
        self.uctx = S("uctx", [128, 16, 30]); self.r_uctx = Res("uctx")
        self.rstd = S("rstd", [128, 512]); self.r_rstd = Res("rstd")
        self.tmp = [S(f"tmp{i}", [128, 512]) for i in range(4)]
        self.r_tmp = [Res(f"tmp{i}") for i in range(4)]
        self.tmp_i = 0
        self.sqb = [S(f"sqb{i}", [128, 512], BF16) for i in range(4)]
        self.r_sqb = [Res(f"sqb{i}") for i in range(4)]
        self.sqb_i = 0
        self.sm = S("sm", [128, 4, 64]); self.r_sm = [Res(f"sm{i}") for i in range(4)]
        self.sm_i = 0
        self.ps = [st.enter_context(nc.psum_tensor(f"ps{i}", [128, 512], F32)) for i in range(8)]
        self.r_ps = [Res(f"ps{i}", excl=True) for i in range(8)]
        self.ps_i = 0

    def psum(self):
        i = self.ps_i
        self.ps_i = (i + 1) % 6
        return self.ps[i], self.r_ps[i]

    def gtmp(self):
        i = self.tmp_i
        self.tmp_i = (i + 1) % 4
        return self.tmp[i], self.r_tmp[i]

    def gsqb(self):
        i = self.sqb_i
        self.sqb_i = (i + 1) % 4
        return self.sqb[i], self.r_sqb[i]

    def gdg(self):
        i = self.dg_i
        self.dg_i = (i + 1) % 12
        return self.dgb[:, i, :], self.r_dgb[i]

    def gsm(self):
        i = self.sm_i
        self.sm_i = (i + 1) % 4
        return self.sm[:, i, :], self.r_sm[i]

    def av(self, off, shape, dt=BF16):
        n = 1
        for s in shape:
            n *= s
        esz = 4 if dt == F32 else 2
        assert off % 4 == 0 and off + n * esz <= self.AR, (off, shape)
        ap = self.arena[:, off // 2: off // 2 + n * esz // 2]
        if dt == F32:
            ap = ap.bitcast(F32)
        if len(shape) == 2:
            ap = ap.rearrange("p (a b) -> p a b", a=shape[0])
        elif len(shape) == 3:
            ap = ap.rearrange("p (a b c) -> p a b c", a=shape[0], b=shape[1])
        return ap

    def ares(self, off, nbytes):
        return self.r_ar[off // 1024: (off + nbytes + 1023) // 1024]

    def ACT(self, out, in_, func, rd, wr, **kw):
        self.P.op("act", lambda e: e.activation(out=out, in_=in_, func=func, **kw), rd, wr)

    def TT(self, eng, out, in0, in1, op, rd, wr):
        self.P.op(eng, lambda e: e.tensor_tensor(out=out, in0=in0, in1=in1, op=op), rd, wr)

    def TS(self, eng, out, in0, s1, s2, op0, op1, rd, wr):
        if s2 is None:
            self.P.op(eng, lambda e: e.tensor_scalar(out=out, in0=in0, scalar1=s1, scalar2=None, op0=op0), rd, wr)
        else:
            self.P.op(eng, lambda e: e.tensor_scalar(out=out, in0=in0, scalar1=s1, scalar2=s2, op0=op0, op1=op1), rd, wr)

    def STT(self, out, in0, scalar, in1, op0, op1, rd, wr):
        self.P.op("dve", lambda e: e.scalar_tensor_tensor(out=out, in0=in0, scalar=scalar, in1=in1, op0=op0, op1=op1), rd, wr)

    def CP(self, eng, out, in_, rd, wr):
        if eng == "act":
            self.P.op("act", lambda e: e.activation(out=out, in_=in_, func=AF.Identity), rd, wr)
        else:
            self.P.op(eng, lambda e: e.tensor_copy(out=out, in_=in_), rd, wr)

    def MM(self, out, lhsT, rhs, start, stop, rd, wr):
        self.P.op("pe", lambda e: e.matmul(out, lhsT=lhsT, rhs=rhs, start=start, stop=stop), rd, wr)

    def TR(self, out, in_, ident, rd, wr):
        self.P.op("pe", lambda e: e.transpose(out, in_, ident), rd, wr)

    def RECIP(self, out, in_, rd, wr):
        self.P.op("dve", lambda e: e.reciprocal(out=out, in_=in_), rd, wr)

    def MEMSET(self, eng, ap, val, wr):
        self.P.op(eng, lambda e: e.memset(ap, val), (), wr)

    def DBG(self, name, ap, rd, shape):
        if name not in self.dbg:
            return
        if name not in self.dbg_out:
            self.dbg_out[name] = self.nc.dram_tensor("dbg_" + name, list(shape), F32, kind="ExternalOutput").ap()
        self.P.dma("sp", self.dbg_out[name], ap, reads=rd)

    def wreq(self, name, uidx, nunits, src_ap, kc, ncols, once=False):
        P = self.P
        spec = (name, uidx, nunits, src_ap, kc, ncols, once)
        if P.dry:
            self.wsched.append(spec)
            return None, None
        i = self.w_consumed
        assert self.wsched[i][0] == name and self.wsched[i][1] == uidx, (self.wsched[i][:2], name, uidx)
        while self.w_issued < min(len(self.wsched), i + self.NW):
            self._wissue(self.w_issued)
            self.w_issued += 1
        self.w_consumed += 1
        s = i % self.NW
        view = self.wsl[s][:, 0:kc * ncols].rearrange("p (k n) -> p k n", k=kc)
        return view, self.r_wsl[s]

    def _wissue(self, i):
        name, uidx, nunits, src_ap, kc, ncols, once = self.wsched[i]
        s = i % self.NW
        P = self.P
        flat = self.wsl[s][:, 0:kc * ncols]
        view = flat.rearrange("p (k n) -> p k n", k=kc)
        key = (name, uidx)
        if once:
            P.dma("pool", view, src_ap.rearrange("(k p) n -> p k n", p=128), writes=[self.r_wsl[s]])
        else:
            while key not in self.scr_res:
                assert self.pc_i < len(self.pc_list)
                self.precast_next()
            scr = self._scratch(name, nunits)
            P.dma("sp", flat, scr[uidx, :, 0:kc * ncols], reads=[self.scr_res[key]], writes=[self.r_wsl[s]])

    def load_vec_fm(self, src_rows_ap, nrows, dst_ap):
        t, rt = self.gtmp()
        self.P.dma("sp", t[0:nrows, 0:128], src_rows_ap, writes=[rt])
        ps, rp = self.psum()
        self.TR(ps[:, 0:nrows], t[0:nrows, 0:128], self.ident[0:nrows, 0:nrows], [rt, self.r_ident], [rp])
        self.CP("dve", dst_ap, ps[:, 0:nrows], [rp], [self.r_cst])

    def emit_setup(self):
        P, I = self.P, self.I
        self.MEMSET("pool", self.ident[:], 0.0, [self.r_ident])
        P.op("pool", lambda e: e.affine_select(out=self.ident[:], in_=self.ident[:], pattern=[[-1, 128]],
                                                compare_op=ALU.not_equal, fill=1.0, base=0, channel_multiplier=1),
             [self.r_ident], [self.r_ident])
        self.CP("dve", self.identb[:], self.ident[:], [self.r_ident], [self.r_identb])
        self.MEMSET("pool", self.tri[:], 1.0, [self.r_tri])
        P.op("pool", lambda e: e.affine_select(out=self.tri[:], in_=self.tri[:], pattern=[[1, 128]],
                                                compare_op=ALU.is_ge, fill=0.0, base=0, channel_multiplier=-1),
             [self.r_tri], [self.r_tri])
        self.MEMSET("pool", self.onesf[:], 1.0, [self.r_onesf])
        self.MEMSET("pool", self.onesb[:], 1.0, [self.r_onesb])
        self.MEMSET("pool", self.Kbuf[:], 0.0, [self.r_K])
        self.MEMSET("pool", self.Vbuf[:], 0.0, [self.r_V])
        self.MEMSET("pool", self.hT[:], 0.0, [self.r_hT])
        self.MEMSET("pool", self.hTb[:], 0.0, [self.r_hTb])
        self.MEMSET("pool", self.xpctx[:], 0.0, [self.r_xpctx])
        self.MEMSET("pool", self.xpctxf[:], 0.0, [self.r_xpctxf])
        self.MEMSET("pool", self.uctx[:], 0.0, [self.r_uctx])
        P.dma("sp", self.flags[:], I["flags"], writes=[self.r_flags])
        c = self.cst
        rows = lambda ap, n: ap.rearrange("a (r p) -> (a r) p", p=128)
        self.load_vec_fm(rows(I["g_mix"], 32), 32, c[:, self.C_GMIX:self.C_GMIX + 32])
        self.load_vec_fm(rows(I["g_mlp"], 32), 32, c[:, self.C_GMLP:self.C_GMLP + 32])
        self.load_vec_fm(rows(I["g_final"], 16), 16, c[:, self.C_GFIN:self.C_GFIN + 16])
        self.load_vec_fm(rows(I["ssm_conv_b"], 16), 16, c[:, self.C_CB:self.C_CB + 16])
        self.load_vec_fm(rows(I["b1_ext"], 32), 32, c[:, self.C_B1:self.C_B1 + 32])
        self.load_vec_fm(rows(I["conf_dw_b"], 16), 16, c[:, self.C_DWB:self.C_DWB + 16])
        self.load_vec_fm(rows(I["conf_ln_g"], 16), 16, c[:, self.C_LNG:self.C_LNG + 16])
        self.load_vec_fm(rows(I["conf_ln_b"], 16), 16, c[:, self.C_LNB:self.C_LNB + 16])
        self.load_vec_fm(rows(I["conf_b2"], 16), 16, c[:, self.C_B2:self.C_B2 + 16])
        self.load_vec_fm(rows(I["ssm_norm_g"], 8), 8, c[:, self.C_NG:self.C_NG + 8])
        self.load_vec_fm(rows(I["ssm_conv_w"], 64), 64, c[:, self.C_CW:self.C_CW + 64])
        dww = rows(I["conf_dw_w"], 496)
        for q in range(4):
            self.load_vec_fm(dww[q * 124:(q + 1) * 124, :], 124, c[:, self.C_DWW + q * 124: self.C_DWW + (q + 1) * 124])
        for i, nm in enumerate(["ssm_dt_bias", "ssm_a_log", "ssm_d", "attn_sinks"]):
            P.dma("sp", self.rowc[:, i, :], I[nm].broadcast_to([128, 16]), writes=[self.r_rowc])
        self.ACT(self.rowc[:, 1, :], self.rowc[:, 1, :], AF.Exp, [self.r_rowc], [self.r_rowc])
        self.TS("dve", self.rowc[:, 1, :], self.rowc[:, 1, :], -1.0, None, ALU.mult, None, [self.r_rowc], [self.r_rowc])
        self.ACT(self.rowc[:, 3, :], self.rowc[:, 3, :], AF.Exp, [self.r_rowc], [self.r_rowc])
        t, rt = self.gtmp()
        P.dma("sp", t[0:1, 0:512], I["ssm_conv_b"][:, 0:512], writes=[rt])
        self.CP("dve", self.cbrow[0:1, 0:512], t[0:1, 0:512], [rt], [self.r_cbrow])
        t, rt = self.gtmp()
        P.dma("sp", t[0:1, 0:512], I["ssm_conv_b"][:, 512:1024], writes=[rt])
        self.CP("dve", self.cbrow[0:1, 512:1024], t[0:1, 0:512], [rt], [self.r_cbrow])
        t, rt = self.gtmp()
        P.dma("sp", t[0:1, 0:512], I["ssm_conv_b"][:, 1024:1536], writes=[rt])
        self.CP("dve", self.cbrow[0:1, 1024:1536], t[0:1, 0:512], [rt], [self.r_cbrow])
        for j in range(4):
            for cc in range(16):
                self.TS("dve", self.diag[:, j * 16 + cc, :], self.identb[:], c[:, self.C_CW + j * 16 + cc: self.C_CW + j * 16 + cc + 1],
                        None, ALU.mult, None, [self.r_identb, self.r_cst], [self.r_diag])
        c5t = self.av(0, [2048], F32)
        r_c5 = self.ares(0, 8192)
        P.dma("sp", c5t[0:5, :], I["c5"], writes=r_c5)
        ps, rp = self.psum()
        for k in range(16):
            self.TR(ps[:, k * 5:(k + 1) * 5], c5t[0:5, k * 128:(k + 1) * 128], self.ident[0:5, 0:5], r_c5 + [self.r_ident], [rp])
        self.ACT(self.scT[:].rearrange("p k s -> p (k s)"), ps[:, 0:80], AF.Silu, [rp], [self.r_scT])
        bm = self.bmT
        r_bm = [Res("bm")]
        self.r_bm = r_bm
        for l in range(2):
            t, rt = self.gtmp()
            P.dma("sp", t[0:96, 0:128], I["b_mod"][l:l + 1, :].rearrange("a (r p) -> (a r) p", p=128), writes=[rt])
            ps, rp = self.psum()
            self.TR(ps[:, 0:96], t[0:96, 0:128], self.ident[0:96, 0:96], [rt, self.r_ident], [rp])
            self.CP("dve", bm[:, l * 96:(l + 1) * 96], ps[:, 0:96], [rp], r_bm)
        for _ in range(8):
            self.precast_next()
        for u in range(16):
            self.adaln_unit(0, u)
        self.adaln_derive(0, 0)
        self.adaln_rest = [(0, u) for u in range(16, 48)] + [(1, u) for u in range(48)]
        self.adaln_finish()

    def adaln_unit(self, l, u):
        P, I = self.P, self.I
        wv, rw = self.wreq("w_mod%d" % l, u, 48, I["w_mod"][l, :, u * 256:(u + 1) * 256], 16, 256, once=True)
        if P.dry:
            return
        ps, rp = self.psum()
        for k in range(16):
            self.MM(ps[0:5, 0:256], self.scT[:, k, :], wv[:, k, :], k == 0, k == 15, [self.r_scT, rw], [rp])
        t, rt = self.gtmp()
        self.CP("act", t[0:5, 0:256], ps[0:5, 0:256], [rp], [rt])
        ps2, rp2 = self.psum()
        for h in range(2):
            self.TR(ps2[:, h * 5:(h + 1) * 5], t[0:5, h * 128:(h + 1) * 128], self.ident[0:5, 0:5], [rt, self.r_ident], [rp2])
        for h in range(2):
            ch = u * 2 + h
            self.TS("dve", self.modT[:, l, ch, :], ps2[:, h * 5:(h + 1) * 5], self.bmT[:, l * 96 + ch: l * 96 + ch + 1], None,
                    ALU.add, None, [rp2] + self.r_bm, [self.r_mod[l][ch // 16]])

    def adaln_flush(self, n):
        for _ in range(min(n, len(self.adaln_rest))):
            l, u = self.adaln_rest.pop(0)
            self.adaln_unit(l, u)

    def adaln_derive(self, l, which):
        c = self.cst
        for cc in range(16):
            if which == 0:
                self.TS("dve", self.gs[:, l, 0, cc, :], self.modT[:, l, 16 + cc, :], 1.0, c[:, self.C_GMIX + l * 16 + cc: self.C_GMIX + l * 16 + cc + 1],
                        ALU.add, ALU.mult, [self.r_mod[l][1], self.r_cst], [self.r_gsv[l][0]])
            else:
                self.TS("dve", self.gs[:, l, 1, cc, :], self.modT[:, l, 64 + cc, :], 1.0, c[:, self.C_GMLP + l * 16 + cc: self.C_GMLP + l * 16 + cc + 1],
                        ALU.add, ALU.mult, [self.r_mod[l][4], self.r_cst], [self.r_gsv[l][1]])

    def adaln_finish(self):
        c = self.cst
        self.adaln_flush(1000)
        self.adaln_derive(0, 1)
        self.adaln_derive(1, 0)
        self.adaln_derive(1, 1)
        for cc in range(16):
            self.TS("dve", self.b2g[:, cc, :], self.modT[:, 1, 32 + cc, :], c[:, self.C_B2 + cc: self.C_B2 + cc + 1], None,
                    ALU.mult, None, [self.r_mod[1][2], self.r_cst], [self.r_b2g])

    def precast_next(self):
        if self.P.dry or self.pc_i >= len(self.pc_list):
            return
        name, uidx, nunits, src_ap, kc, ncols, once = self.pc_list[self.pc_i]
        self.pc_i += 1
        scr = self._scratch(name, nunits)
        r = Res("scr")
        self.scr_res[(name, uidx)] = r
        self.P.dma("pool", scr[uidx, :, 0:kc * ncols].rearrange("p (k n) -> p k n", k=kc),
                   src_ap.rearrange("(k p) n -> p k n", p=128), writes=[r])

    def m_sh(self, l, which, cc, s):
        return self.modT[:, l, (0 if which == 0 else 48) + cc, s:s + 1]

    def m_gt(self, l, which, cc, s):
        return self.modT[:, l, (32 if which == 0 else 80) + cc, s:s + 1]

    def m_gs(self, l, which, cc, s):
        return self.gs[:, l, which, cc, s:s + 1]

    def load_x(self, blocks):
        P = self.P
        for bi, (src, c0, n) in enumerate(blocks):
            off = (bi % 2) * 8192
            xs = self.av(off, [2048], F32)
            rx = self.ares(off, 8192)
            P.dma("sp", xs[0:n, :], src, writes=rx)
            for g in range(4):
                ps, rp = self.psum()
                for q in range(4):
                    cc = g * 4 + q
                    self.TR(ps[:, q * 128: q * 128 + n], xs[0:n, cc * 128:(cc + 1) * 128], self.ident[0:n, 0:n], rx + [self.r_ident], [rp])
                for q in range(4):
                    cc = g * 4 + q
                    self.CP("act" if g % 2 else "dve", self.xres[:, cc, c0:c0 + n], ps[:, q * 128:q * 128 + n], [rp], [self.r_xres[cc]])
                    if bi == len(blocks) - 1:
                        self.stat_chunk(cc, c0 + n)

    def stat_chunk(self, cc, T):
        sq, rsq = self.gsqb()
        self.ACT(sq[:, 0:T], self.xres[:, cc, 0:T], AF.Square, [self.r_xres[cc]], [rsq])
        self.stat_pend.append((sq, rsq))
        if len(self.stat_pend) > 2:
            self._stat_mm(T)

    def _stat_mm(self, T):
        sq, rsq = self.stat_pend.pop(0)
        ps, rp = self.ps[7], self.r_ps[7]
        self.MM(ps[:, 0:T], self.onesb[:], sq[:, 0:T], self.stat_n == 0, self.stat_n == 15, [self.r_onesb, rsq], [rp])
        self.stat_n += 1

    def norm(self, T, segs, l, which, hn, r_hn, final=False):
        while self.stat_pend:
            self._stat_mm(T)
        assert self.P.dry or self.stat_n == 16, self.stat_n
        self.stat_n = 0
        ps, rp = self.ps[7], self.r_ps[7]
        self.ACT(self.rstd[:, 0:T], ps[:, 0:T], AF.Sqrt, [rp], [self.r_rstd], scale=1.0 / D, bias=EPS)
        self.RECIP(self.rstd[:, 0:T], self.rstd[:, 0:T], [self.r_rstd], [self.r_rstd])
        if final:
            return
        for cc in range(16):
            t, rt = self.gtmp()
            for (c0, n, s) in segs:
                self.STT(t[:, c0:c0 + n], self.xres[:, cc, c0:c0 + n], self.m_gs(l, which, cc, s), self.rstd[:, c0:c0 + n],
                         ALU.mult, ALU.mult, [self.r_xres[cc], self.r_gsv[l][which], self.r_rstd], [rt])
                self.ACT(hn[:, cc, c0:c0 + n], t[:, c0:c0 + n], AF.Identity, [rt, self.r_mod[l][0 if which == 0 else 3]], [r_hn[cc]],
                         bias=self.m_sh(l, which, cc, s))

    def proj_fm(self, name, src, ncols_total, kc, in_buf, r_in, T, evac, col0=0, ncols=None, kbase=0, nunits=None, ubase=0,
                unit_cols=256):
        P = self.P
        ncols = ncols_total - col0 if ncols is None else ncols
        nu = (ncols + unit_cols - 1) // unit_cols
        nunits = nu if nunits is None else nunits
        for u in range(nu):
            cw = min(unit_cols, ncols - u * unit_cols)
            sap = src[kbase * 128:(kbase + kc) * 128, col0 + u * unit_cols: col0 + u * unit_cols + cw]
            wv, rw = self.wreq(name, ubase + u, nunits, sap, kc, cw)
            if P.dry:
                continue
            for h in range(cw // 128):
                ps, rp = self.psum()
                for k in range(kc):
                    self.MM(ps[:, 0:T], wv[:, k, h * 128:(h + 1) * 128], in_buf[:, k, 0:T], k == 0, k == kc - 1,
                            [rw, r_in[k]], [rp])
                evac(u * (unit_cols // 128) + h, ps, rp)

    def l0_mixer(self, tile):
        P, I = self.P, self.I
        T, segs, blocks, kind = tile["T"], tile["segs"], tile["blocks"], tile["kind"]
        pre = kind == "pre"
        KB = 1024
        hn = self.av(0, [16, 512]); r_hn = [self.ares(c * KB, KB)[0] for c in range(16)]
        Q = self.av(16 * KB, [8, 512]); r_Q = [self.ares(16 * KB + c * KB, KB)[0] for c in range(8)]
        nslot = len(blocks)
        zb = self.av(24 * KB, [6, 1024]); r_z = [self.ares(24 * KB + b * 2048, 2048) for b in range(6)]
        XW = tile["xpw"]
        xp = self.av(36 * KB, [16, XW]); r_xp = [self.ares(36 * KB + c * XW * 2, XW * 2) for c in range(16)]
        xst = self.av(53 * KB, [6, 1024]); r_xst = [self.ares(53 * KB + b * 2048, 2048) for b in range(6)]
        Bt = self.av(65 * KB, [6, 512]); r_Bt = [self.ares(65 * KB + b * 1024, 1024) for b in range(6)]
        BCT = self.av(0, [8, 512]); r_BCT = [self.ares(c * KB, KB)[0] for c in range(8)]
        mix = self.av(36 * KB, [16, 512]); r_mix = [self.ares(36 * KB + c * KB, KB)[0] for c in range(16)]
        dtb = self.av(71 * KB, [6, 2, 16], F32); r_dtb = self.ares(71 * KB, 768)
        KSUB = int(os.environ.get("KSUB", "99"))
        if KSUB <= 1:
            return
        self.norm(T, segs, 0, 0, hn, r_hn)
        if KSUB <= 2:
            return
        ropec = self.av(63 * KB, [512], F32); ropes = self.av(65 * KB, [512], F32)
        r_rope = self.ares(63 * KB, 4096)
        if not pre:
            P.dma("sp", ropec[:, 0:T], I["rope_cos"][:, tile["rc0"]:tile["rc0"] + T], writes=r_rope)
            P.dma("sp", ropes[:, 0:T], I["rope_sin"][:, tile["rc0"]:tile["rc0"] + T], writes=r_rope)
        W = I["w_in_ext"]
        if not pre:
            kfull = {}

            def evac_qk(ci, ps, rp, st={}):
                u, h = ci // 2, ci % 2
                if h == 0:
                    t1, r1 = self.gtmp()
                    self.TT("dve", t1[:, 0:T], ps[:, 0:T], ropec[:, 0:T], ALU.mult, [rp] + r_rope, [r1])
                    st["t1"] = (t1, r1)
                else:
                    t1, r1 = st["t1"]
                    t2, r2 = self.gtmp()
                    self.TT("dve", t2[:, 0:T], ps[:, 0:T], ropes[:, 0:T], ALU.mult, [rp] + r_rope, [r2])
                    if u < 8:
                        self.TT("pool", Q[:, u, 0:T], t1[:, 0:T], t2[:, 0:T], ALU.add, [r1, r2], [r_Q[u]])
                    else:
                        j = u - 8
                        self.TT("pool", t1[:, 0:T], t1[:, 0:T], t2[:, 0:T], ALU.add, [r1, r2], [r1])
                        self.k_store(tile, j, t1, r1)

            self.proj_fm("w_in", W, WEXT, 16, hn, r_hn, T, evac_qk, col0=QK0, ncols=3072, nunits=29, ubase=0)
            wv, rw = self.wreq("w_in", 12, 29, W[:, V0:V0 + 256], 16, 256)
            if not P.dry:
                for bi, (c0, n, bk) in enumerate(blocks):
                    ps, rp = self.psum()
                    for k in range(16):
                        self.MM(ps[0:n, 0:256], hn[:, k, c0:c0 + n], wv[:, k, :], k == 0, k == 15, [r_hn[k], rw], [rp])
                    self.v_store(tile, bi, ps, rp)
        if not pre:
            for u in range(4):
                wv, rw = self.wreq("w_in", 13 + u, 29, W[:, Z0 + u * 256: Z0 + (u + 1) * 256], 16, 256)
                if P.dry:
                    continue
                for bi, (c0, n, bk) in enumerate(blocks):
                    ps, rp = self.psum()
                    for k in range(16):
                        self.MM(ps[0:n, 0:256], hn[:, k, c0:c0 + n], wv[:, k, :], k == 0, k == 15, [r_hn[k], rw], [rp])
                    self.ACT(zb[0:n, bi, u * 256:(u + 1) * 256], ps[0:n, 0:256], AF.Silu, [rp], r_z[bi])
        so = tile["segoff"]
        self.CP("pool", xp[:, :, so[0]:so[0] + 3], self.xpctx[:], [self.r_xpctx], sum(r_xp, []))

        def evac_xbc(ci, ps, rp):
            for si, (c0, n, s) in enumerate(segs):
                self.CP("act", xp[:, ci, so[si] + 3: so[si] + 3 + n], ps[:, c0:c0 + n], [rp], r_xp[ci])
                if si == 0 and not pre:
                    self.CP("dve", self.xpctxf[:, ci, :], ps[:, c0 + n - 3:c0 + n], [rp], [self.r_xpctxf])
                elif si > 0:
                    self.CP("dve", tile["sconvf"][:, si - 1, ci, :], ps[:, c0 + n - 3:c0 + n], [rp], tile["r_sconvf"])

        ncx = 1536 if pre else 2048
        self.proj_fm("w_in", W, WEXT, 16, hn, r_hn, T, evac_xbc, col0=XBC0, ncols=ncx, nunits=29, ubase=17)
        if KSUB <= 3:
            return
        wv, rw = self.wreq("w_in", 25, 29, W[:, DT0:DT0 + 16], 16, 16)
        if not P.dry:
            for bi, (c0, n, bk) in enumerate(blocks):
                ps, rp = self.psum()
                for k in range(16):
                    self.MM(ps[0:n, 0:16], hn[:, k, c0:c0 + n], wv[:, k, :], k == 0, k == 15, [r_hn[k], rw], [rp])
                self.TT("dve", dtb[0:n, bi, 0, :], ps[0:n, 0:16], self.rowc[0:n, 0, :], ALU.add, [rp, self.r_rowc], r_dtb)
                self.ACT(dtb[0:n, bi, 0, :], dtb[0:n, bi, 0, :], AF.Exp, r_dtb, r_dtb)
                self.ACT(dtb[0:n, bi, 0, :], dtb[0:n, bi, 0, :], AF.Ln, r_dtb, r_dtb, bias=1.0)
                self.TT("dve", dtb[0:n, bi, 1, :], dtb[0:n, bi, 0, :], self.rowc[0:n, 1, :], ALU.mult, r_dtb + [self.r_rowc], r_dtb)
        if P.dry:
            if not pre:
                self.proj_fm("w_out", I["w_out"], D, 16, mix, r_mix, T, None)
            return
        if KSUB <= 4:
            return
        if kind == "halo":
            P.dma("sp", self.O["k_s"].rearrange("(s t) c -> t s c", t=16), tile["kso"][0:16, :, :], reads=tile["r_kso"])
            P.dma("sp", self.O["v_s"].rearrange("(s t) c -> t s c", t=16), tile["vso"][0:16, :, :], reads=tile["r_vso"])
        if tile.get("last"):
            P.dma("sp", self.O["k_p"], tile["kpo"][:, :], reads=tile["r_kpo"])
            P.dma("sp", self.O["v_p"], tile["vpo"][:, :], reads=tile["r_vpo"])
        if kind == "halo":
            for s in range(4):
                self.load_ctx_fm(I["st_sconv"][s], 3, xp, r_xp, so[1 + s])
        c = self.cst
        for si, (c0, n, s) in enumerate(segs):
            if not pre:
                for cc in range(8, 16):
                    ps, rp = self.psum()
                    for j in range(4):
                        self.MM(ps[:, 0:n], self.diag[:, j * 16 + cc, :], xp[:, cc, so[si] + j: so[si] + j + n], j == 0, j == 3,
                                [self.r_diag] + r_xp[cc], [rp])
                    self.ACT(BCT[:, cc - 8, c0:c0 + n], ps[:, 0:n], AF.Silu, [rp, self.r_cst], [r_BCT[cc - 8]],
                             bias=c[:, self.C_CB + cc:self.C_CB + cc + 1])
        for bi, (c0, n, bk) in enumerate(blocks):
            si = bk["seg"]
            o = so[si] + (c0 - segs[si][0])
            for g3 in range(3):
                ps, rp = self.psum()
                for q in range(4):
                    cc = g3 * 4 + q
                    for j in range(4):
                        self.MM(ps[0:n, q * 128:(q + 1) * 128], xp[:, cc, o + j: o + j + n], self.diag[:, j * 16 + cc, :], j == 0, False,
                                r_xp[cc] + [self.r_diag], [rp])
                    self.MM(ps[0:n, q * 128:(q + 1) * 128], self.onesb[0:1, 0:n], self.cbrow[0:1, cc * 128:(cc + 1) * 128], False, True,
                            [self.r_onesb, self.r_cbrow], [rp])
                if g3 < 2:
                    self.ACT(xst[0:n, bi, g3 * 512:(g3 + 1) * 512], ps[0:n, :], AF.Silu, [rp], r_xst[bi])
                else:
                    self.ACT(Bt[0:n, bi, :], ps[0:n, :], AF.Silu, [rp], r_Bt[bi])
        n0 = segs[0][1]
        ncs = 12 if pre else 16
        self.CP("pool", self.xpctx[:, 0:ncs, :], xp[:, 0:ncs, so[0] + n0: so[0] + n0 + 3], sum(r_xp[0:ncs], []), [self.r_xpctx])
        if KSUB <= 5:
            return
        if not pre:
            self.attention(tile, Q, r_Q, mix, r_mix)
        for bi, (c0, n, bk) in enumerate(blocks):
            self.ssd_block(tile, bi, c0, n, bk, xst, r_xst, Bt, r_Bt, BCT, r_BCT, zb, r_z, dtb, r_dtb, mix, r_mix)
        if pre:
            return
        def evac_out(ci, ps, rp):
            for (c0, n, s) in segs:
                self.STT(self.xres[:, ci, c0:c0 + n], ps[:, c0:c0 + n], self.m_gt(0, 0, ci, s), self.xres[:, ci, c0:c0 + n],
                         ALU.mult, ALU.add, [rp, self.r_mod[0][2], self.r_xres[ci]], [self.r_xres[ci]])
            self.stat_chunk(ci, T)

        self.proj_fm("w_out", I["w_out"], D, 16, mix, r_mix, T, evac_out)

    def load_ctx_fm(self, src, nrows, dst, r_dst, coloff):
        for qq in range(4):
            stg, r_stg = self.gtmp()
            self.P.dma("sp", stg[0:nrows, :], src[:, qq * 512:(qq + 1) * 512], writes=[r_stg])
            ps, rp = self.psum()
            for q in range(4):
                self.TR(ps[:, q * 32: q * 32 + nrows], stg[0:nrows, q * 128:(q + 1) * 128], self.ident[0:nrows, 0:nrows],
                        [r_stg, self.r_ident], [rp])
            for q in range(4):
                cc = qq * 4 + q
                self.CP("dve", dst[:, cc, coloff:coloff + nrows], ps[:, q * 32:q * 32 + nrows], [rp], r_dst[cc])

    def k_store(self, tile, j, t1, r1):
        segs = tile["segs"]
        c0, n, s = segs[0]
        self.CP("act", self.Kbuf[:, j, 128:128 + n], t1[:, 0:n], [r1], [self.r_K])
        if tile["kind"] == "halo":
            self.CP("act", tile["Kown"][:, j, :], t1[:, 256:320], [r1], tile["r_Kown"])
            for si in range(1, 5):
                c0, n, s = segs[si]
                ps, rp = self.psum()
                self.TR(ps[0:16, 0:64], t1[0:64, c0:c0 + 16], self.ident[0:64, 0:64], [r1, self.r_ident], [rp])
                self.CP("dve", tile["kso"][0:16, si - 1, j * 64:(j + 1) * 64], ps[0:16, 0:64], [rp], tile["r_kso"])
        if tile.get("last"):
            ps, rp = self.psum()
            self.TR(ps[:, 0:64], t1[0:64, 384:512], self.ident[0:64, 0:64], [r1, self.r_ident], [rp])
            self.CP("dve", tile["kpo"][:, j * 64:(j + 1) * 64], ps[:, 0:64], [rp], tile["r_kpo"])

    def v_store(self, tile, bi, ps, rp):
        c0, n, bk = tile["blocks"][bi]
        if bk["seg"] == 0:
            self.CP("act", self.Vbuf[0:n, 1 + bi, :], ps[0:n, 0:256], [rp], [self.r_V])
            if tile.get("last") and bi == 3:
                self.CP("dve", tile["vpo"][:, :], ps[:, 0:256], [rp], tile["r_vpo"])
        else:
            s = bk["seg"] - 1
            self.CP("act", tile["Vsmp"][0:16, s, :], ps[0:16, 0:256], [rp], tile["r_Vsmp"])
            self.CP("dve", tile["vso"][0:16, s, :], ps[0:16, 0:256], [rp], tile["r_vso"])

    def attn_group(self, Q, r_Q, qc0, nq, ktiles, mix, r_mix, mc0):
        KBY = 1024
        PT = self.av(8 * KBY, [2, 4, 64]); r_PT = self.ares(8 * KBY, 1024)
        nkt = len(ktiles)
        for j in range(4):
            pss = [self.psum(), self.psum()]
            for kt, (Kfn, Vfn, nk, p0, kp, bias, rd) in enumerate(ktiles):
                for g in range(4):
                    h = 4 * j + g
                    half = h % 2
                    ps, rp = pss[half]
                    self.MM(ps[p0:p0 + nk, kt * 128 + (g // 2) * 64: kt * 128 + (g // 2) * 64 + nq], Kfn(j, half),
                            Q[half * 64:(half + 1) * 64, h // 2, qc0:qc0 + nq], True, True, rd + [r_Q[h // 2]], [rp])
            for kt, (Kfn, Vfn, nk, p0, kp, bias, rd) in enumerate(ktiles):
                if nk < kp:
                    self.MEMSET("pool", PT[:, kt, :, :], 0.0, r_PT)
                for half in range(2):
                    ps, rp = pss[half]
                    src = ps[p0:p0 + nk, kt * 128:(kt + 1) * 128].rearrange("p (g q) -> p g q", g=2)[:, :, 0:nq]
                    dst = PT[p0:p0 + nk, kt, :, :].rearrange("p (gg hh) q -> p gg hh q", hh=2)[:, :, half, 0:nq]
                    kw = {"scale": 0.125}
                    rdd = [rp]
                    if bias is not None:
                        kw["bias"] = bias[p0:p0 + nk, :]
                        rdd.append(self.r_flags)
                    self.ACT(dst, src, AF.Exp, rdd, r_PT, **kw)
            psd, rpd = self.psum()
            for kt, (Kfn, Vfn, nk, p0, kp, bias, rd) in enumerate(ktiles):
                self.MM(psd[:, 0:4 * nq].rearrange("p (g q) -> p g q", g=4), self.onesb[0:kp, :], PT[0:kp, kt, :, 0:nq],
                        kt == 0, kt == nkt - 1, [self.r_onesb] + r_PT, [rpd])
            pso, rpo = self.psum()
            for g in range(4):
                ph = (g % 2) * 64
                for kt, (Kfn, Vfn, nk, p0, kp, bias, rd) in enumerate(ktiles):
                    self.MM(pso[ph:ph + 64, (g // 2) * 64:(g // 2) * 64 + nq], Vfn(j), PT[0:kp, kt, g, 0:nq],
                            kt == 0, kt == nkt - 1, rd + r_PT, [rpo])
            den, rden = self.gtmp()
            self.TT("dve", den[:, 0:4 * nq].rearrange("p (g q) -> p g q", g=4), psd[:, 0:4 * nq].rearrange("p (g q) -> p g q", g=4),
                    self.rowc[:, 3, 4 * j:4 * j + 4].unsqueeze(2).to_broadcast([128, 4, nq]), ALU.add, [rpd, self.r_rowc], [rden])
            self.RECIP(den[:, 0:4 * nq], den[:, 0:4 * nq], [rden], [rden])
            for g in range(4):
                ph = (g % 2) * 64
                cc = 2 * j + g // 2
                self.TT("dve", mix[ph:ph + 64, cc, mc0:mc0 + nq], pso[ph:ph + 64, (g // 2) * 64:(g // 2) * 64 + nq],
                        den[ph:ph + 64, g * nq:(g + 1) * nq], ALU.mult, [rpo, rden], [r_mix[cc]])

    def attention(self, tile, Q, r_Q, mix, r_mix):
        P, I = self.P, self.I
        segs = tile["segs"]
        c0, n, s = segs[0]
        first_main = tile.get("first_main", False)
        for qi in range(n // 64):
            qc = qi * 64
            kts = []
            lo = qc - 128
            pieces = [(lo, 128), (lo + 128, 64)] if lo % 128 == 0 else [(lo, 64), (lo + 64, 128)]
            for (k0, nk) in pieces:
                blk = (k0 + 128) // 128
                p0 = (k0 + 128) % 128
                bias = self.flags[:, 1:2] if (first_main and k0 < 0) else None
                Kfn = (lambda j, half, k0=k0, nk=nk: self.Kbuf[half * 64:(half + 1) * 64, j, 128 + k0:128 + k0 + nk])
                Vfn = (lambda j, blk=blk: self.Vbuf[:, blk, j * 64:(j + 1) * 64])
                kts.append((Kfn, Vfn, nk, p0, 128, bias, [self.r_K, self.r_V]))
            self.attn_group(Q, r_Q, c0 + qc, 64, kts, mix, r_mix, c0 + qc)
        if tile["kind"] == "halo":
            Ks, Vc, Vo, Kown = tile["Ksmp"], tile["Vcache"], tile["Vsmp"], tile["Kown"]
            for si in range(1, 5):
                c0, n, s = segs[si]
                sq = si - 1
                stg, r_stg = self.gtmp()
                P.dma("sp", stg[:, :], I["ck"][sq], writes=[r_stg])
                ps, rp = self.psum()
                for j in range(4):
                    self.TR(ps[:, j * 128:(j + 1) * 128], stg[:, j * 128:(j + 1) * 128], self.ident[:], [r_stg, self.r_ident], [rp])
                self.CP("dve", Ks[:, :, 0:128], ps[:, :].rearrange("p (j t) -> p j t", j=4), [rp], tile["r_Ksmp"])
                self.CP("pool", Ks[:, :, 128:144], Kown[:, :, sq * 16:(sq + 1) * 16], tile["r_Kown"], tile["r_Ksmp"])
                stg2, r_stg2 = self.gtmp()
                P.dma("sp", stg2[:, 0:256], I["cv"][sq], writes=[r_stg2])
                self.CP("dve", Vc[:, :], stg2[:, 0:256], [r_stg2], tile["r_Vcache"])
                kts = [
                    ((lambda j, half: Ks[half * 64:(half + 1) * 64, j, 0:128]),
                     (lambda j: Vc[:, j * 64:(j + 1) * 64]), 128, 0, 128, None, tile["r_Ksmp"] + tile["r_Vcache"]),
                    ((lambda j, half: Ks[half * 64:(half + 1) * 64, j, 128:144]),
                     (lambda j, sq=sq: Vo[0:16, sq, j * 64:(j + 1) * 64]), 16, 0, 16, None, tile["r_Ksmp"] + tile["r_Vsmp"]),
                ]
                self.attn_group(Q, r_Q, c0, 16, kts, mix, r_mix, c0)
        c0, n, s = segs[0]
        self.CP("pool", self.Kbuf[:, :, 0:128], self.Kbuf[:, :, n:n + 128], [self.r_K], [self.r_K])
        self.CP("pool", self.Vbuf[:, 0, :], self.Vbuf[:, n // 128, :], [self.r_V], [self.r_V])

    def ssd_block(self, tile, bi, c0, L, bk, xst, r_xst, Bt, r_Bt, BCT, r_BCT, zb, r_z, dtb, r_dtb, mix, r_mix):
        pre = tile["kind"] == "pre"
        KBY = 1024
        base = 8 * KBY
        if bk["seg"] == 0:
            hT, r_hT, hTb, r_hTb = self.hT, [self.r_hT], self.hTb, [self.r_hTb]
        else:
            sq = bk["seg"] - 1
            hT, r_hT = tile["hTs"], tile["r_hTs"]
            hTb, r_hTb = tile["hTsb"], tile["r_hTsb"]
            for g2 in range(2):
                stg3, r_stg3 = self.gtmp()
                self.P.dma("sp", stg3[:, :].rearrange("p (a n) -> p a n", a=4),
                           self.I["st_ssm"][sq].rearrange("(a p) n -> p a n", p=128)[:, g2 * 4:(g2 + 1) * 4, :], writes=[r_stg3])
                ps, rp = self.psum()
                for q in range(4):
                    self.TR(ps[:, q * 128:(q + 1) * 128], stg3[:, q * 128:(q + 1) * 128], self.ident[:], [r_stg3, self.r_ident], [rp])
                self.CP("dve", hT[:, g2 * 512:(g2 + 1) * 512], ps[:, :], [rp], r_hT)
                self.CP("act", hTb[:, g2 * 512:(g2 + 1) * 512], ps[:, :], [rp], r_hTb)
        bufA = self.av(base + 1 * KBY, [4, 128], F32); rA = self.ares(base + 1 * KBY, 2048)
        bufB = self.av(base + 3 * KBY, [4, 128], F32); rB = self.ares(base + 3 * KBY, 2048)
        CBm = self.av(base + 5 * KBY, [128], F32); rCB = self.ares(base + 5 * KBY, 512)
        GT = self.av(base + 6 * KBY, [4, 128]); rG = self.ares(base + 6 * KBY, 1024)
        CsT = self.av(base + 7 * KBY, [4, 128]); rCs = self.ares(base + 7 * KBY, 1024)
        xdt = self.av(base + 8 * KBY, [4, 64]); rxdt = self.ares(base + 8 * KBY, 512)
        xdte = self.av(base + 8 * KBY + 512, [4, 64]); rxdte = self.ares(base + 8 * KBY + 512, 512)
        ysb = self.av(base + 9 * KBY, [1024], F32); rys = self.ares(base + 9 * KBY, 4096)
        ynb = self.av(base + 13 * KBY, [1024]); ryn = self.ares(base + 13 * KBY, 2048)
        dt = dtb[0:L, bi, 0, :]
        dA = dtb[0:L, bi, 1, :]
        acum, racum = self.gsm()
        ps, rp = self.psum()
        self.MM(ps[0:L, 0:16], self.tri[0:L, 0:L], dA, True, True, [self.r_tri] + r_dtb, [rp])
        self.MM(ps[:, 16:32], self.onesf[0:L, :], dA, True, True, [self.r_onesf] + r_dtb, [rp])
        self.CP("dve", acum[0:L, 0:16], ps[0:L, 0:16], [rp], [racum])
        tot = acum[:, 16:32]
        self.CP("dve", tot, ps[:, 16:32], [rp], [racum])
        te = acum[:, 32:48]
        self.TT("dve", te[0:L, :], tot[0:L, :], acum[0:L, 0:16], ALU.subtract, [racum], [racum])
        self.ACT(te[0:L, :], te[0:L, :], AF.Exp, [racum], [racum])
        self.TT("dve", te[0:L, :], te[0:L, :], dt, ALU.mult, [racum] + r_dtb, [racum])
        dec = acum[:, 48:64]
        self.ACT(dec, tot, AF.Exp, [racum], [racum])
        psy = [None, None]
        for g in range(4):
            xs_g = xst[0:L, bi, g * 256:(g + 1) * 256].rearrange("p (r d) -> p r d", r=4)
            if not pre:
                self.TT("dve", bufA[0:L, :, 0:L], self.tri[0:L, 0:L].unsqueeze(1).to_broadcast([L, 4, L]),
                        dA[:, 4 * g:4 * g + 4].unsqueeze(2).to_broadcast([L, 4, L]), ALU.mult, [self.r_tri] + r_dtb, rA)
                ps, rp = self.psum()
                for r in range(4):
                    self.MM(ps[:, r * L:(r + 1) * L], self.onesf[0:L, :], bufA[0:L, r, 0:L], True, True, [self.r_onesf] + rA, [rp])
                self.CP("act", bufB[:, :, 0:L], ps[:, 0:4 * L].rearrange("p (r l) -> p r l", r=4), [rp], rB)
                self.TT("dve", bufA[0:L, :, 0:L], bufB[0:L, :, 0:L], acum[0:L, 4 * g:4 * g + 4].unsqueeze(2).to_broadcast([L, 4, L]),
                        ALU.subtract, rB + [racum], rA)
                self.TS("dve", bufA[0:L, :, 0:L], bufA[0:L, :, 0:L], 0.0, None, ALU.min, None, rA, rA)
                self.ACT(bufA[0:L, :, 0:L], bufA[0:L, :, 0:L], AF.Exp, rA, rA)
                ps2, rp2 = self.psum()
                self.MM(ps2[0:L, 0:L], BCT[:, g, c0:c0 + L], BCT[:, 4 + g, c0:c0 + L], True, True, [r_BCT[g], r_BCT[4 + g]], [rp2])
                self.TT("dve", CBm[0:L, 0:L], ps2[0:L, 0:L], self.tri[0:L, 0:L], ALU.mult, [rp2, self.r_tri], rCB)
                self.TT("dve", GT[0:L, :, 0:L], bufA[0:L, :, 0:L], CBm[0:L, 0:L].unsqueeze(1).to_broadcast([L, 4, L]), ALU.mult,
                        rA + rCB, rG)
                self.ACT(bufB[:, :, 0:L], bufB[:, :, 0:L], AF.Exp, rB, rB)
                self.TT("dve", CsT[:, :, 0:L], bufB[:, :, 0:L], BCT[:, 4 + g, c0:c0 + L].unsqueeze(1).to_broadcast([128, 4, L]), ALU.mult,
                        rB + [r_BCT[4 + g]], rCs)
                self.TT("dve", xdt[0:L, :, :], xs_g, dt[:, 4 * g:4 * g + 4].unsqueeze(2).to_broadcast([L, 4, 64]), ALU.mult,
                        r_xst[bi] + r_dtb, rxdt)
                if g % 2 == 0:
                    psy[g // 2] = (self.ps[6 + g // 2], self.r_ps[6 + g // 2])
                py, rpy = psy[g // 2]
                for r in range(4):
                    h = 4 * g + r
                    col = (h % 8) * 64
                    self.MM(py[0:L, col:col + 64], GT[0:L, r, 0:L], xdt[0:L, r, :], True, False, rG + rxdt, [rpy])
                    self.MM(py[0:L, col:col + 64], CsT[:, r, 0:L], hTb[:, h * 64:(h + 1) * 64], False, True, rCs + r_hTb, [rpy])
            self.TT("dve", xdte[0:L, :, :], xs_g, te[0:L, 4 * g:4 * g + 4].unsqueeze(2).to_broadcast([L, 4, 64]), ALU.mult,
                    r_xst[bi] + [racum], rxdte)
            ps3, rp3 = self.psum()
            self.MM(ps3[:, 0:256], Bt[0:L, bi, g * 128:(g + 1) * 128], xdte[0:L, :, :].rearrange("p r d -> p (r d)"), True, True,
                    r_Bt[bi] + rxdte, [rp3])
            hg = hT[:, g * 256:(g + 1) * 256]
            self.TT("dve", hg.rearrange("p (r d) -> p r d", r=4), hg.rearrange("p (r d) -> p r d", r=4),
                    dec[:, 4 * g:4 * g + 4].unsqueeze(2).to_broadcast([128, 4, 64]), ALU.mult, r_hT + [racum], r_hT)
            self.TT("dve", hg, hg, ps3[:, 0:256], ALU.add, r_hT + [rp3], r_hT)
            self.CP("act", hTb[:, g * 256:(g + 1) * 256], hg, r_hT, r_hTb)
        if bk["seg"] > 0:
            self.out_state(hT, r_hT, self.O["hst_s"][bk["seg"] - 1])
        if pre:
            return
        Dx = self.av(base + 1 * KBY, [1024], F32); rDx = self.ares(base + 1 * KBY, 4096)
        self.TT("dve", Dx[0:L, :].rearrange("p (h d) -> p h d", h=16), xst[0:L, bi, :].rearrange("p (h d) -> p h d", h=16),
                self.rowc[0:L, 2, :].unsqueeze(2).to_broadcast([L, 16, 64]), ALU.mult, r_xst[bi] + [self.r_rowc], rDx)
        for hh in range(2):
            py, rpy = psy[hh]
            self.TT("dve", ysb[0:L, hh * 512:(hh + 1) * 512], py[0:L, :], Dx[0:L, hh * 512:(hh + 1) * 512], ALU.add, [rpy] + rDx, rys)
        self.TT("dve", ysb[0:L, :], ysb[0:L, :], zb[0:L, bi, :], ALU.mult, rys + r_z[bi], rys)
        ssq, rssq = self.gsm()
        self.MEMSET("pool", ssq[:, 0:4], 0.0, [rssq])
        for g in range(4):
            self.ACT(Dx[0:L, g * 256:(g + 1) * 256], ysb[0:L, g * 256:(g + 1) * 256], AF.Square, rys + [rssq], rDx + [rssq],
                     accum_out=ssq[0:L, g:g + 1])
        self.P.op("act", lambda e: e.activation(out=ssq[0:L, 0:4], in_=ssq[0:L, 0:4], func=AF.Sqrt, scale=1.0 / 256, bias=EPS),
                  rDx + [rssq], [rssq])
        self.RECIP(ssq[0:L, 0:4], ssq[0:L, 0:4], [rssq], [rssq])
        self.TT("dve", ynb[0:L, :].rearrange("p (g d) -> p g d", g=4), ysb[0:L, :].rearrange("p (g d) -> p g d", g=4),
                ssq[0:L, 0:4].unsqueeze(2).to_broadcast([L, 4, 256]), ALU.mult, rys + [rssq], ryn)
        c = self.cst
        for hh in range(2):
            ps, rp = self.psum()
            pb = ps[:].bitcast(BF16)
            for q in range(4):
                cc = hh * 4 + q
                self.TR(pb[:, q * 128:q * 128 + L], ynb[0:L, cc * 128:(cc + 1) * 128], self.identb[0:L, 0:L], ryn + [self.r_identb], [rp])
            for q in range(4):
                cc = hh * 4 + q
                self.ACT(mix[:, 8 + cc, c0:c0 + L], pb[:, q * 128:q * 128 + L], AF.Identity, [rp, self.r_cst], [r_mix[8 + cc]],
                         scale=c[:, self.C_NG + cc:self.C_NG + cc + 1])

    def mlp(self, tile, l):
        P, I = self.P, self.I
        T, segs = tile["T"], tile["segs"]
        KB = 1024
        hn = self.av(0, [16, 512]); r_hn = [self.ares(c * KB, KB)[0] for c in range(16)]
        hid = self.av(16 * KB, [32, 512]); r_hid = [self.ares(16 * KB + c * KB, KB)[0] for c in range(32)]
        self.norm(T, segs, l, 1, hn, r_hn)
        for half in range(2):
            def evac_up(ci, ps, rp):
                t, rt = self.gtmp()
                self.ACT(t[:, 0:T], ps[:, 0:T], AF.Relu, [rp], [rt])
                self.TT("pool" if ci % 2 else "dve", hid[:, ci, 0:T], t[:, 0:T], t[:, 0:T], ALU.mult, [rt], [r_hid[ci]])

            self.proj_fm("w_up%d" % l, I["mlp_w_up"][l], DFF, 16, hn, r_hn, T, evac_up, col0=half * 4096, ncols=4096,
                         nunits=32, ubase=half * 16)

            def evac_dn(ci, ps, rp, half=half):
                for (c0, n, s) in segs:
                    self.STT(self.xres[:, ci, c0:c0 + n], ps[:, c0:c0 + n], self.m_gt(l, 1, ci, s), self.xres[:, ci, c0:c0 + n],
                             ALU.mult, ALU.add, [rp, self.r_mod[l][5], self.r_xres[ci]], [self.r_xres[ci]])
                if half == 1:
                    self.stat_chunk(ci, T)

            self.proj_fm("w_dn%d" % l, I["mlp_w_down"][l], D, 32, hid, r_hid, T, evac_dn, kbase=half * 32, nunits=32,
                         ubase=half * 16, unit_cols=128)

    def l1_conf(self, tile):
        P, I = self.P, self.I
        T, segs, kind = tile["T"], tile["segs"], tile["kind"]
        KB = 1024
        c = self.cst
        hn = self.av(0, [16, 512]); r_hn = [self.ares(cc * KB, KB)[0] for cc in range(16)]
        UW = tile["uw"]
        uo = tile["uoff"]
        UB = 16 * KB
        u = self.av(UB, [16, UW]); r_u = [self.ares(UB + cc * UW * 2, UW * 2) for cc in range(16)]
        VB = 34 * KB
        v = self.av(VB, [16, 512], F32); r_v = [self.ares(VB + cc * 2 * KB, 2 * KB) for cc in range(16)]
        hs = hn; r_hs = r_hn
        self.norm(T, segs, 1, 0, hn, r_hn)
        self.CP("pool", u[:, :, uo[0]:uo[0] + 30], self.uctx[:], [self.r_uctx], sum(r_u, []))
        if kind == "halo":
            for s in range(4):
                self.load_ctx_fm(I["st_cconv"][s], 30, u, r_u, uo[1 + s])
        st = {}

        def evac_w1(ci, ps, rp):
            cc, h = ci // 2, ci % 2
            if h == 0:
                st["a"] = (ps, rp)
            else:
                pa, rpa = st["a"]
                t, rt = self.gtmp()
                self.ACT(t[:, 0:T], ps[:, 0:T], AF.Sigmoid, [rp, self.r_cst], [rt], bias=c[:, self.C_B1 + 2 * cc + 1: self.C_B1 + 2 * cc + 2])
                for si, (c0, n, s) in enumerate(segs):
                    self.STT(u[:, cc, uo[si] + 30: uo[si] + 30 + n], pa[:, c0:c0 + n], c[:, self.C_B1 + 2 * cc: self.C_B1 + 2 * cc + 1],
                             t[:, c0:c0 + n], ALU.add, ALU.mult, [rpa, rt, self.r_cst], r_u[cc])
                    if si == 0:
                        self.STT(self.uctx[:, cc, :], pa[:, c0 + n - 30:c0 + n], c[:, self.C_B1 + 2 * cc: self.C_B1 + 2 * cc + 1],
                                 t[:, c0 + n - 30:c0 + n], ALU.add, ALU.mult, [rpa, rt, self.r_cst], [self.r_uctx])
                    else:
                        self.STT(tile["cconvf"][:, cc, (si - 1) * 16: si * 16], pa[:, c0:c0 + 16], c[:, self.C_B1 + 2 * cc: self.C_B1 + 2 * cc + 1],
                                 t[:, c0:c0 + 16], ALU.add, ALU.mult, [rpa, rt, self.r_cst], tile["r_cconvf"])

        self.proj_fm("w1", I["w1_ext"], 2 * D, 16, hn, r_hn, T, evac_w1)
        if P.dry:
            self.proj_fm("w2", I["conf_w2"], D, 16, hs, r_hs, T, None)
            return
        if kind == "halo":
            ranges = [(uo[0], 256, [(0, 0, 256)]), (uo[1], 3 * 46 + 16, None)]
        else:
            ranges = [(uo[0], segs[0][1], [(0, 0, segs[0][1])])]
        for cc in range(16):
            pss = [self.psum() for _ in ranges]
            for j in range(31):
                dg, rdg = self.gdg()
                self.TS("dve", dg, self.identb[:], c[:, self.C_DWW + j * 16 + cc: self.C_DWW + j * 16 + cc + 1], None, ALU.mult, None,
                        [self.r_identb, self.r_cst], [rdg])
                for (o, n, _), (ps, rp) in zip(ranges, pss):
                    self.MM(ps[:, 0:n], dg, u[:, cc, o + j:o + j + n], j == 0, j == 30, [rdg] + r_u[cc], [rp])
            bias = c[:, self.C_DWB + cc: self.C_DWB + cc + 1]
            ps, rp = pss[0]
            n0 = ranges[0][1]
            self.ACT(v[:, cc, 0:n0], ps[:, 0:n0], AF.Identity, [rp, self.r_cst], r_v[cc], bias=bias)
            if kind == "halo":
                ps, rp = pss[1]
                self.ACT(v[:, cc, 256:320].rearrange("p (s t) -> p s t", s=4),
                         ps[:, 0:184].rearrange("p (s w) -> p s w", w=46)[:, :, 0:16], AF.Identity, [rp, self.r_cst], r_v[cc], bias=bias)
        ps1, rp1 = self.psum()
        ps2, rp2 = self.psum()
        for cc in range(16):
            a, ra = self.gsqb()
            self.CP("act", a[:, 0:T], v[:, cc, 0:T], r_v[cc], [ra])
            self.MM(ps1[:, 0:T], self.onesb[:], a[:, 0:T], cc == 0, cc == 15, [self.r_onesb, ra], [rp1])
            b, rb = self.gsqb()
            self.ACT(b[:, 0:T], v[:, cc, 0:T], AF.Square, r_v[cc], [rb])
            self.MM(ps2[:, 0:T], self.onesb[:], b[:, 0:T], cc == 0, cc == 15, [self.r_onesb, rb], [rp2])
        mean, rmean = self.gtmp()
        self.TS("dve", mean[:, 0:T], ps1[:, 0:T], 1.0 / D, None, ALU.mult, None, [rp1], [rmean])
        msq, rmsq = self.gtmp()
        self.TT("dve", msq[:, 0:T], mean[:, 0:T], mean[:, 0:T], ALU.mult, [rmean], [rmsq])
        self.STT(msq[:, 0:T], ps2[:, 0:T], 1.0 / D, msq[:, 0:T], ALU.mult, ALU.subtract, [rp2, rmsq], [rmsq])
        self.ACT(msq[:, 0:T], msq[:, 0:T], AF.Sqrt, [rmsq], [rmsq], bias=EPS)
        self.RECIP(msq[:, 0:T], msq[:, 0:T], [rmsq], [rmsq])
        for cc in range(16):
            self.TT("dve", v[:, cc, 0:T], v[:, cc, 0:T], mean[:, 0:T], ALU.subtract, r_v[cc] + [rmean], r_v[cc])
            self.TT("pool", v[:, cc, 0:T], v[:, cc, 0:T], msq[:, 0:T], ALU.mult, r_v[cc] + [rmsq], r_v[cc])
            self.ACT(hs[:, cc, 0:T], v[:, cc, 0:T], AF.Silu, r_v[cc] + [self.r_cst], [r_hs[cc]],
                     scale=c[:, self.C_LNG + cc:self.C_LNG + cc + 1], bias=c[:, self.C_LNB + cc:self.C_LNB + cc + 1])

        def evac_w2(ci, ps, rp):
            for (c0, n, s) in segs:
                self.STT(self.xres[:, ci, c0:c0 + n], ps[:, c0:c0 + n], self.m_gt(1, 0, ci, s), self.xres[:, ci, c0:c0 + n],
                         ALU.mult, ALU.add, [rp, self.r_mod[1][2], self.r_xres[ci]], [self.r_xres[ci]])
                self.TS("dve", self.xres[:, ci, c0:c0 + n], self.xres[:, ci, c0:c0 + n], self.b2g[:, ci, s:s + 1], None, ALU.add, None,
                        [self.r_xres[ci], self.r_b2g], [self.r_xres[ci]])
            self.stat_chunk(ci, T)

        self.proj_fm("w2", I["conf_w2"], D, 16, hs, r_hs, T, evac_w2)

    def out_y(self, tile):
        P = self.P
        T, segs = tile["T"], tile["segs"]
        c = self.cst
        self.norm(T, segs, 0, 0, None, None, final=True)
        for (dst, c0, n) in tile["yout"]:
            pass
        nb = len(tile["yout"])
        stg = [self.av(i * 8192, [2048], F32) for i in range(4)]
        r_stg = [self.ares(i * 8192, 8192) for i in range(4)]
        for g in range(4):
            ts = []
            for q in range(4):
                cc = g * 4 + q
                t, rt = self.gtmp()
                self.STT(t[:, 0:T], self.xres[:, cc, 0:T], c[:, self.C_GFIN + cc:self.C_GFIN + cc + 1], self.rstd[:, 0:T], ALU.mult, ALU.mult,
                         [self.r_xres[cc], self.r_cst, self.r_rstd], [rt])
                ts.append((t, rt))
            for bi, (dst, c0, n) in enumerate(tile["yout"]):
                ps, rp = self.psum()
                for q in range(4):
                    t, rt = ts[q]
                    self.TR(ps[0:n, q * 128:(q + 1) * 128], t[:, c0:c0 + n], self.ident[:], [rt, self.r_ident], [rp])
                self.CP("act" if bi % 2 else "dve", stg[bi % 4][0:n, g * 512:(g + 1) * 512], ps[0:n, :], [rp], r_stg[bi % 4])
        for bi, (dst, c0, n) in enumerate(tile["yout"]):
            P.dma("sp", dst, stg[bi % 4][0:n, :], reads=r_stg[bi % 4])

    def out_fm_rows(self, src_fm, rd, nrows, dst):
        stg = self.av(32 * 1024, [2048], F32)
        r_stg = self.ares(32 * 1024, 8192)
        for g in range(4):
            ps, rp = self.psum()
            for q in range(4):
                cc = g * 4 + q
                self.TR(ps[0:nrows, q * 128:(q + 1) * 128], src_fm[:, cc, :], self.ident[:], rd + [self.r_ident], [rp])
            self.CP("dve", stg[0:nrows, g * 512:(g + 1) * 512], ps[0:nrows, :], [rp], r_stg)
        self.P.dma("sp", dst, stg[0:nrows, :], reads=r_stg)

    def out_state(self, hT, r_hT, dst):
        dv = dst.rearrange("(a p) n -> p a n", p=128)
        for g in range(2):
            stg, r_stg = self.gtmp()
            ps, rp = self.psum()
            for q in range(4):
                self.TR(ps[:, q * 128:(q + 1) * 128], hT[:, (g * 4 + q) * 128:(g * 4 + q + 1) * 128], self.ident[:], r_hT + [self.r_ident], [rp])
            self.CP("dve", stg[:, :], ps[:, :], [rp], [r_stg])
            self.P.dma("sp", dv[:, g * 4:(g + 1) * 4, :], stg[:, :].rearrange("p (a n) -> p a n", a=4), reads=[r_stg])

    def emit(self, P):
        self.P = P
        I, O = self.I, self.O
        self.ps_i = self.tmp_i = self.sqb_i = self.sm_i = self.dg_i = 0
        self.stat_pend = []
        self.stat_n = 0
        self.pt_i = 0
        if P.dry:
            self.wsched = []
            self.pc_list = []
        else:
            self.w_issued = self.w_consumed = 0
            seen = set()
            self.pc_list = []
            for spec in self.wsched:
                if not spec[6] and (spec[0], spec[1]) not in seen:
                    seen.add((spec[0], spec[1]))
                    self.pc_list.append(spec)
        self.pc_i = 0
        self.emit_setup()
        KB = 1024
        KSTOP = int(os.environ.get("KSTOP", "9"))
        if KSTOP < 1:
            if not P.dry:
                P.finish()
            return
        nb = self.npre
        b0 = 0
        while b0 < nb:
            nblk = min(4, nb - b0)
            T = nblk * 128
            tile = dict(kind="pre", T=T, segs=[(0, T, 0)], xpw=3 + T, segoff=[0],
                        blocks=[(i * 128, 128, dict(seg=0)) for i in range(nblk)])
            self.load_x([(I["x_pre"][(b0 + i) * 128:(b0 + i + 1) * 128, :], i * 128, 128) for i in range(nblk)])
            self.l0_mixer(tile)
            b0 += nblk
            for _ in range(40):
                self.precast_next()
            if int(os.environ.get("KSUB", "99")) < 99:
                break
        if KSTOP < 2:
            if not P.dry:
                P.finish()
            return
        while not P.dry and self.pc_i < len(self.pc_list):
            self.precast_next()
        T = 320
        segs = [(0, 256, 0)] + [(256 + 16 * s, 16, 1 + s) for s in range(4)]
        blocks = [(0, 128, dict(seg=0)), (128, 128, dict(seg=0))] + [(256 + 16 * s, 16, dict(seg=1 + s)) for s in range(4)]
        S = self.S
        tile = dict(kind="halo", T=T, segs=segs, blocks=blocks, rc0=0,
                    xpw=3 + 256 + 4 * 19, segoff=[0] + [259 + 19 * s for s in range(4)],
                    uw=30 + 256 + 4 * 46, uoff=[0] + [286 + 46 * s for s in range(4)])
        if not hasattr(self, "halo_bufs"):
            hb = self.halo_bufs = {}
            hb["Kown"] = S("Kown", [128, 4, 64], BF16)
            hb["Ksmp"] = S("Ksmp", [128, 4, 144], BF16)
            hb["Vcache"] = S("Vcache", [128, 256], BF16)
            hb["Vsmp"] = S("Vsmp", [16, 4, 256], BF16)
            hb["hTs"] = S("hTs", [128, 1024])
            hb["hTsb"] = S("hTsb", [128, 1024], BF16)
            hb["sconvf"] = S("sconvf", [128, 4, 16, 3])
            hb["cconvf"] = S("cconvf", [128, 16, 64])
            for k in list(hb.keys()):
                hb["r_" + k] = [Res(k)]
            hb["kso"] = self.av(53 * KB, [4, 256], F32); hb["r_kso"] = self.ares(53 * KB, 4096)
            hb["vso"] = self.av(57 * KB, [4, 256], F32); hb["r_vso"] = self.ares(57 * KB, 4096)
            hb["kpo"] = self.av(61 * KB, [256], F32); hb["r_kpo"] = self.ares(61 * KB, 1024)
            hb["vpo"] = self.av(62 * KB, [256], F32); hb["r_vpo"] = self.ares(62 * KB, 1024)
            print("SBUF bytes/partition:", self.sb_bytes)
        tile.update(self.halo_bufs)
        hb = self.halo_bufs
        self.load_x([(I["x_main"][0:128, :], 0, 128), (I["x_main"][128:256, :], 128, 128), (I["x_smp"], 256, 64)])
        self.l0_mixer(tile)
        self.mlp(tile, 0)
        self.l1_conf(tile)
        self.mlp(tile, 1)
        tile["yout"] = [(O["y_smp"], 256, 64)]
        self.out_y(tile)
        if not P.dry:
            for sq in range(4):
                self.out_fm_rows(hb["sconvf"][:, sq, :, :], hb["r_sconvf"], 3, O["sconv_s"][sq])
                P.dma("sp", O["cconv_s"][sq, 0:14, :], I["st_cconv"][sq, 16:30, :])
                self.out_fm_rows(hb["cconvf"][:, :, sq * 16:(sq + 1) * 16], hb["r_cconvf"], 16, O["cconv_s"][sq, 14:30, :])
            f = self.flags[:, 0:1]
            self.TS("dve", self.hT[:], self.hT[:], f, None, ALU.mult, None, [self.r_hT, self.r_flags], [self.r_hT])
            self.TS("dve", self.hTb[:], self.hTb[:], f, None, ALU.mult, None, [self.r_hTb, self.r_flags], [self.r_hTb])
            self.TS("dve", self.xpctx[:], self.xpctx[:], f, None, ALU.mult, None, [self.r_xpctx, self.r_flags], [self.r_xpctx])
            self.TS("dve", self.uctx[:], self.uctx[:], f, None, ALU.mult, None, [self.r_uctx, self.r_flags], [self.r_uctx])
        if KSTOP < 3:
            if not P.dry:
                P.finish()
            return
        for ti in range(self.nmain):
            T = 512
            tile = dict(kind="main", T=T, segs=[(0, 512, 0)], blocks=[(i * 128, 128, dict(seg=0)) for i in range(4)],
                        rc0=320 + 512 * ti, xpw=515, segoff=[0], uw=542, uoff=[0], first_main=(ti == 0), last=(ti == self.nmain - 1))
            tile.update(self.halo_bufs)
            base = HALO + ti * 512
            self.load_x([(I["x_main"][base + i * 128: base + (i + 1) * 128, :], i * 128, 128) for i in range(4)])
            self.l0_mixer(tile)
            self.mlp(tile, 0)
            self.l1_conf(tile)
            self.mlp(tile, 1)
            tile["yout"] = [(O["y_main"][ti * 512 + i * 128: ti * 512 + (i + 1) * 128, :], i * 128, 128) for i in range(4)]
            self.out_y(tile)
        if not P.dry:
            self.out_state(self.hT[:], [self.r_hT], O["hst_p"])
            self.out_fm_rows(self.xpctxf[:, :, :], [self.r_xpctxf], 3, O["sconv_p"])
            self.out_fm_rows(self.uctx[:, :, :], [self.r_uctx], 30, O["cconv_p"])
            P.finish()


def build_program(npre, nmain, dbg=()):
    nc = bass.Bass("TRN2", target_bir_lowering=False)
    st = ExitStack()
    K = Kern(nc, st, npre, nmain, dbg)
    dry = Prog(dry=True)
    K.emit(dry)
    P = Prog()
    K.emit(P)
    P.build(nc, st)
    st.close()
    return nc, K, P


def _prep_weights(inp):
    w_in = np.asarray(inp["w_in"][0], np.float32)
    q = w_in[:, 0:1024]
    k = w_in[:, 1024:1280]
    sw = np.concatenate([np.arange(32, 64), np.arange(0, 32)])
    cols = []
    for c in range(8):
        qc = q[:, c * 128:(c + 1) * 128]
        qs = np.concatenate([qc[:, 0:64][:, sw], qc[:, 64:128][:, sw]], axis=1)
        cols += [qc, qs]
    for j in range(4):
        kj = k[:, j * 64:(j + 1) * 64]
        cols += [kj, kj, kj[:, sw], kj[:, sw]]
    cols += [w_in[:, 1280:1536], w_in[:, 1536:2560], w_in[:, 2560:4608], w_in[:, 4608:4624]]
    w_in_ext = np.ascontiguousarray(np.concatenate(cols, axis=1))
    assert w_in_ext.shape[1] == WEXT
    w1 = np.asarray(inp["conf_w1"][0], np.float32)
    b1 = np.asarray(inp["conf_b1"][0], np.float32)
    c1, bb = [], []
    for c in range(16):
        c1 += [w1[:, c * 128:(c + 1) * 128], w1[:, 2048 + c * 128: 2048 + (c + 1) * 128]]
        bb += [b1[c * 128:(c + 1) * 128], b1[2048 + c * 128: 2048 + (c + 1) * 128]]
    w1_ext = np.ascontiguousarray(np.concatenate(c1, axis=1))
    b1_ext = np.ascontiguousarray(np.concatenate(bb))[None, :]
    f = lambda a: np.ascontiguousarray(np.asarray(a, np.float32))
    W = dict(
        w_mod=f(inp["w_mod"]), b_mod=f(inp["b_mod"]), g_mix=f(inp["g_mix"]), g_mlp=f(inp["g_mlp"]),
        w_in_ext=w_in_ext, w_out=f(inp["w_out"][0]), attn_sinks=f(inp["attn_sinks"]), ssm_a_log=f(inp["ssm_a_log"]),
        ssm_dt_bias=f(inp["ssm_dt_bias"]), ssm_d=f(inp["ssm_d"]), ssm_conv_w=f(inp["ssm_conv_w"][0]),
        ssm_conv_b=f(inp["ssm_conv_b"]), ssm_norm_g=f(inp["ssm_norm_g"]), w1_ext=w1_ext, b1_ext=b1_ext,
        conf_dw_w=f(inp["conf_dw_w"][0]), conf_dw_b=f(inp["conf_dw_b"]), conf_ln_g=f(inp["conf_ln_g"]),
        conf_ln_b=f(inp["conf_ln_b"]), conf_w2=f(inp["conf_w2"][0]), conf_b2=f(inp["conf_b2"]),
        mlp_w_up=f(inp["mlp_w_up"]), mlp_w_down=f(inp["mlp_w_down"]), g_final=f(inp["g_final"])[None, :],
    )
    return W


def _rope_tables(pos):
    p = np.arange(128)
    d = (p % 64) % 32
    inv = 10000.0 ** (-d.astype(np.float64) / 32.0)
    ang = inv[:, None] * pos[None, :].astype(np.float64)
    ang32 = (pos[None, :].astype(np.float32) * (10000.0 ** (-(d.astype(np.float32)) / 32.0)).astype(np.float32)[:, None]).astype(np.float32)
    cos = np.cos(ang32.astype(np.float64)).astype(np.float32)
    sin = np.sin(ang32.astype(np.float64)).astype(np.float32)
    sign = np.where((p % 64) < 32, -1.0, 1.0).astype(np.float32)[:, None]
    return np.ascontiguousarray(cos), np.ascontiguousarray(sin * sign)


def run(inp, seq=SEQ, dbg=(), trace=False):
    half = seq // 2
    nmain = half // 512
    npre = (half - HALO) // 128
    nc, K, P = build_program(npre, nmain, dbg)
    W = _prep_weights(inp)
    xp = np.asarray(inp["x_prompt"], np.float32)
    xs = np.asarray(inp["x_sample"], np.float32)
    in_maps = []
    for core in range(NCORES):
        b, hf = core // 2, core % 2
        m = dict(W)
        if hf == 1:
            m["x_pre"] = np.ascontiguousarray(xp[b, 0:max(npre, 1) * 128])
            m["x_main"] = np.ascontiguousarray(xp[b, half - HALO: seq])
            pos0 = half - HALO
            flags = np.tile(np.array([[1.0, 0.0]], np.float32), (128, 1))
        else:
            m["x_pre"] = np.zeros((max(npre, 1) * 128, D), np.float32)
            m["x_main"] = np.ascontiguousarray(np.concatenate([np.zeros((HALO, D), np.float32), xp[b, 0:half]], axis=0))
            pos0 = -HALO
            flags = np.tile(np.array([[0.0, NEG]], np.float32), (128, 1))
        sl = slice(core * 4, core * 4 + 4)
        m["x_smp"] = np.ascontiguousarray(xs[sl].reshape(64, D))
        m["c5"] = np.ascontiguousarray(np.concatenate([np.asarray(inp["c_prompt"], np.float32)[b:b + 1],
                                                       np.asarray(inp["c_sample"], np.float32)[sl]], axis=0))
        ck = np.asarray(inp["cache_swa_k"], np.float32)[0, sl]
        m["ck"] = np.ascontiguousarray(np.stack([ck, ck], axis=3).reshape(4, 128, 512))
        m["cv"] = np.ascontiguousarray(np.asarray(inp["cache_swa_v"], np.float32)[0, sl].reshape(4, 128, 256))
        m["st_ssm"] = np.ascontiguousarray(np.asarray(inp["state_ssm"], np.float32)[0, sl].reshape(4, 1024, 128))
        m["st_sconv"] = np.ascontiguousarray(np.asarray(inp["state_ssm_conv"], np.float32)[0, sl])
        m["st_cconv"] = np.ascontiguousarray(np.asarray(inp["state_conf_conv"], np.float32)[0, sl])
        pos = np.concatenate([pos0 + np.arange(HALO), np.tile(PAST_LEN + np.arange(16), 4),
                              pos0 + HALO + np.arange(half)]).astype(np.float64)
        m["rope_cos"], m["rope_sin"] = _rope_tables(pos)
        m["flags"] = flags
        in_maps.append(m)
    res = run_bass_kernel_spmd(nc, in_maps, core_ids=list(range(NCORES)), **({"trace": True} if trace else {}))
    R = res.results
    B = xp.shape[0]
    y_prompt = np.zeros((B, seq, D), np.float32)
    for core in range(NCORES):
        b, hf = core // 2, core % 2
        y_prompt[b, hf * half:(hf + 1) * half] = R[core]["y_main"]
    y_sample = np.concatenate([R[c]["y_smp"].reshape(4, 16, D) for c in range(NCORES)], axis=0)
    last = [R[2 * b + 1] for b in range(B)]
    swa_k_p = np.stack([r["k_p"].reshape(128, 4, 64) for r in last])[None]
    swa_v_p = np.stack([r["v_p"].reshape(128, 4, 64) for r in last])[None]
    swa_k_s = np.concatenate([R[c]["k_s"].reshape(4, 16, 4, 64) for c in range(NCORES)], axis=0)[None]
    swa_v_s = np.concatenate([R[c]["v_s"].reshape(4, 16, 4, 64) for c in range(NCORES)], axis=0)[None]
    st_p = np.stack([r["hst_p"].reshape(16, 64, 128) for r in last])[None]
    st_s = np.concatenate([R[c]["hst_s"].reshape(4, 16, 64, 128) for c in range(NCORES)], axis=0)[None]
    sc_p = np.stack([r["sconv_p"] for r in last])[None]
    sc_s = np.concatenate([R[c]["sconv_s"] for c in range(NCORES)], axis=0)[None]
    cc_p = np.stack([r["cconv_p"] for r in last])[None]
    cc_s = np.concatenate([R[c]["cconv_s"] for c in range(NCORES)], axis=0)[None]
    outs = (y_prompt, y_sample, swa_k_p, swa_v_p, swa_k_s, swa_v_s, st_p, st_s, sc_p, sc_s, cc_p, cc_s)
    outs = tuple(np.ascontiguousarray(o.astype(np.float32)) for o in outs)
    return outs, res, R


def kernel(**inputs):
    outs, _, _ = run(inputs)
    return outs
```

```python
import math, os, sys
from contextlib import ExitStack
KDEBUG = bool(os.environ.get("KDEBUG"))
import numpy as np
import concourse.bass as bass
import concourse.mybir as mybir
from concourse.bass_utils import run_bass_kernel_spmd

F32 = mybir.dt.float32
BF16 = mybir.dt.bfloat16
ALU = mybir.AluOpType
AF = mybir.ActivationFunctionType

D = 2048
KC = 16
DFF = 8192
EPS = 1e-6
SEQ = 8192
NCORES = 8
HALO = 256
PAST_LEN = 1024
WEXT = 3072 + 256 + 1024 + 2048 + 16
QK0, V0, Z0, XBC0, DT0 = 0, 3072, 3328, 4352, 6400
NEG = -30000.0


class Res:
    __slots__ = ("name", "w", "rs", "excl")

    def __init__(self, name="", excl=False):
        self.name = name
        self.w = None
        self.rs = []
        self.excl = excl


class Engine:
    def __init__(self, name):
        self.name = name
        self.ops = []
        self.count = 0
        self.known = {}
        self.is_pe = name == "pe"


class Prog:
    def __init__(self, dry=False):
        self.dry = dry
        self.sems = {}
        self.sem_names = []
        self.E = {n: Engine(n) for n in ("pe", "act", "dve", "pool", "sp")}
        for n in self.E:
            self.sem_names.append("eng_" + n)
        self.dma_pools = {}
        for q in ("sp", "pool"):
            keys = [f"dq_{q}_{i}" for i in range(8)]
            self.sem_names += keys
            self.dma_pools[q] = {"keys": keys, "cnt": [0] * len(keys), "i": 0}
        self.n_instr = 0

    def _need(self, eng, ev, same_ok=False):
        if ev is None:
            return
        key, val, ename = ev
        if same_ok and ename == eng.name:
            return
        if eng.known.get(key, 0) >= val:
            return
        eng.known[key] = val
        eng.ops.append(lambda e, key=key, val=val: e.wait_ge(self.sems[key], val))

    def _deps(self, eng, reads, writes):
        for r in reads:
            self._need(eng, r.w, same_ok=False)
            if r.excl:
                for ev in r.rs:
                    self._need(eng, ev, same_ok=True)
        for w in writes:
            self._need(eng, w.w, same_ok=eng.is_pe)
            for ev in w.rs:
                self._need(eng, ev, same_ok=(eng.name != "pool"))

    def _commit(self, ev, reads, writes):
        for r in reads:
            r.rs.append(ev)
            if len(r.rs) > 48:
                best = {}
                for e in r.rs:
                    if e[0] not in best or best[e[0]][1] < e[1]:
                        best[e[0]] = e
                r.rs = list(best.values())
        for w in writes:
            w.w = ev
            w.rs = []

    def op(self, engname, fn, reads=(), writes=()):
        if self.dry:
            return None
        if KDEBUG:
            fr = sys._getframe(2)
            lab = f"{fr.f_code.co_name}:{fr.f_lineno}"
            fn0 = fn
            fn = lambda e, fn0=fn0, lab=lab: fn0(e).annotate(lab)
        eng = self.E[engname]
        self._deps(eng, reads, writes)
        eng.count += 1
        key = "eng_" + engname
        ev = (key, eng.count, engname)
        eng.ops.append(lambda e, fn=fn, key=key: fn(e).then_inc(self.sems[key], 1))
        self._commit(ev, reads, writes)
        self.n_instr += 1
        return ev

    def dma(self, q, out_ap, in_ap, reads=(), writes=(), **kw):
        if self.dry:
            return None
        eng = self.E[q]
        pool = self.dma_pools[q]
        i = pool["i"]
        pool["i"] = (i + 1) % len(pool["keys"])
        key = pool["keys"][i]
        prev = pool["cnt"][i]
        pool["cnt"][i] = prev + 16
        if prev > 0:
            self._need(eng, (key, prev, "dma"))
        self._deps(eng, reads, writes)
        ev = (key, prev + 16, "dma")
        eng.ops.append(lambda e, key=key, o=out_ap, i_=in_ap, kw=kw:
                       e.dma_start(out=o, in_=i_, **kw).then_inc(self.sems[key], 16))
        self._commit(ev, reads, writes)
        self.n_instr += 1
        return ev

    def finish(self):
        for q, pool in self.dma_pools.items():
            for key, cnt in zip(pool["keys"], pool["cnt"]):
                if cnt:
                    self._need(self.E["sp"], (key, cnt, "dma"))

    def build(self, nc, stack):
        for key in self.sem_names:
            self.sems[key] = stack.enter_context(nc.semaphore(key))
        block = stack.enter_context(nc.Block())
        E = self.E

        @block.tensor
        def _(e):
            for f in E["pe"].ops:
                f(e)

        @block.scalar
        def _(e):
            for f in E["act"].ops:
                f(e)

        @block.vector
        def _(e):
            for f in E["dve"].ops:
                f(e)

        @block.gpsimd
        def _(e):
            for f in E["pool"].ops:
                f(e)

        @block.sync
        def _(e):
            for f in E["sp"].ops:
                f(e)


class Kern:
    def __init__(self, nc, st, npre, nmain, dbg=()):
        self.nc, self.st = nc, st
        self.npre, self.nmain = npre, nmain
        self.dbg = set(dbg)
        self.dbg_out = {}
        self.ncols = 320 + 512 * nmain
        self._decl()
        self._alloc()

    def _decl(self):
        nc = self.nc
        di = lambda n, s: nc.dram_tensor(n, list(s), F32, kind="ExternalInput").ap()
        do = lambda n, s: nc.dram_tensor(n, list(s), F32, kind="ExternalOutput").ap()
        I = self.I = {}
        O = self.O = {}
        I["x_pre"] = di("x_pre", (max(self.npre, 1) * 128, D))
        I["x_main"] = di("x_main", (HALO + 512 * self.nmain, D))
        I["x_smp"] = di("x_smp", (64, D))
        I["c5"] = di("c5", (5, D))
        I["ck"] = di("ck", (4, 128, 512))
        I["cv"] = di("cv", (4, 128, 256))
        I["st_ssm"] = di("st_ssm", (4, 1024, 128))
        I["st_sconv"] = di("st_sconv", (4, 3, D))
        I["st_cconv"] = di("st_cconv", (4, 30, D))
        I["rope_cos"] = di("rope_cos", (128, self.ncols))
        I["rope_sin"] = di("rope_sin", (128, self.ncols))
        I["flags"] = di("flags", (128, 2))
        I["w_mod"] = di("w_mod", (2, D, 6 * D))
        I["b_mod"] = di("b_mod", (2, 6 * D))
        I["g_mix"] = di("g_mix", (2, D))
        I["g_mlp"] = di("g_mlp", (2, D))
        I["w_in_ext"] = di("w_in_ext", (D, WEXT))
        I["w_out"] = di("w_out", (D, D))
        I["attn_sinks"] = di("attn_sinks", (1, 16))
        I["ssm_a_log"] = di("ssm_a_log", (1, 16))
        I["ssm_dt_bias"] = di("ssm_dt_bias", (1, 16))
        I["ssm_d"] = di("ssm_d", (1, 16))
        I["ssm_conv_w"] = di("ssm_conv_w", (4, D))
        I["ssm_conv_b"] = di("ssm_conv_b", (1, D))
        I["ssm_norm_g"] = di("ssm_norm_g", (1, 1024))
        I["w1_ext"] = di("w1_ext", (D, 2 * D))
        I["b1_ext"] = di("b1_ext", (1, 2 * D))
        I["conf_dw_w"] = di("conf_dw_w", (31, D))
        I["conf_dw_b"] = di("conf_dw_b", (1, D))
        I["conf_ln_g"] = di("conf_ln_g", (1, D))
        I["conf_ln_b"] = di("conf_ln_b", (1, D))
        I["conf_w2"] = di("conf_w2", (D, D))
        I["conf_b2"] = di("conf_b2", (1, D))
        I["mlp_w_up"] = di("mlp_w_up", (2, D, DFF))
        I["mlp_w_down"] = di("mlp_w_down", (2, DFF, D))
        I["g_final"] = di("g_final", (1, D))
        O["y_main"] = do("y_main", (512 * self.nmain, D))
        O["y_smp"] = do("y_smp", (64, D))
        O["k_p"] = do("k_p", (128, 256))
        O["v_p"] = do("v_p", (128, 256))
        O["k_s"] = do("k_s", (64, 256))
        O["v_s"] = do("v_s", (64, 256))
        O["hst_p"] = do("hst_p", (1024, 128))
        O["hst_s"] = do("hst_s", (4, 1024, 128))
        O["sconv_p"] = do("sconv_p", (3, D))
        O["sconv_s"] = do("sconv_s", (4, 3, D))
        O["cconv_p"] = do("cconv_p", (30, D))
        O["cconv_s"] = do("cconv_s", (4, 30, D))
        self.scr = {}
        self.scr_res = {}

    def _scratch(self, name, nunits):
        if name not in self.scr:
            self.scr[name] = self.nc.dram_tensor("scr_" + name, [nunits, 128, 4096], BF16, kind="Internal").ap()
        return self.scr[name]

    def _alloc(self):
        nc, st = self.nc, self.st
        self.sb_bytes = 0

        def S(name, shape, dt=F32):
            n = 1
            for s in shape[1:]:
                n *= s
            self.sb_bytes += n * (4 if dt == F32 else 2)
            return st.enter_context(nc.sbuf_tensor("s_" + name, list(shape), dt))

        self.S = S
        self.xres = S("xres", [128, 16, 512])
        self.r_xres = [Res(f"xres{c}") for c in range(16)]
        self.NW = 3
        self.wsl = [S(f"wsl{i}", [128, 4096], BF16) for i in range(self.NW)]
        self.r_wsl = [Res(f"wsl{i}") for i in range(self.NW)]
        self.AR = 72 * 1024
        self.arena = S("arena", [128, self.AR // 2], BF16)
        self.r_ar = [Res(f"ar{i}") for i in range(self.AR // 1024)]
        self.ident = S("ident", [128, 128]); self.r_ident = Res("ident")
        self.identb = S("identb", [128, 128], BF16); self.r_identb = Res("identb")
        self.tri = S("tri", [128, 128]); self.r_tri = Res("tri")
        self.onesf = S("onesf", [128, 128]); self.r_onesf = Res("onesf")
        self.onesb = S("onesb", [128, 128], BF16); self.r_onesb = Res("onesb")
        self.diag = S("diag", [128, 64, 128], BF16); self.r_diag = Res("diag")
        self.modT = S("modT", [128, 2, 96, 5]); self.r_modT = Res("modT")
        self.gs = S("gs", [128, 2, 2, 16, 5]); self.r_gs = Res("gs")
        self.b2g = S("b2g", [128, 16, 5]); self.r_b2g = Res("b2g")
        self.cst = S("cst", [128, 800]); self.r_cst = Res("cst")
        self.C_GMIX, self.C_GMLP, self.C_GFIN, self.C_CB, self.C_B1 = 0, 32, 64, 80, 96
        self.C_DWB, self.C_LNG, self.C_LNB, self.C_B2, self.C_NG, self.C_CW, self.C_DWW = 128, 144, 160, 176, 192, 200, 264
        self.rowc = S("rowc", [128, 5, 16]); self.r_rowc = Res("rowc")
        self.cbrow = S("cbrow", [1, 1536], BF16); self.r_cbrow = Res("cbrow")
        self.flags = S("flags", [128, 2]); self.r_flags = Res("flags")
        self.scT = S("scT", [128, 16, 5], BF16); self.r_scT = Res("scT")
        self.dgb = S("dgb", [128, 12, 128], BF16); self.r_dgb = [Res(f"dgb{i}") for i in range(12)]
        self.dg_i = 0
        self.Kbuf = S("Kbuf", [128, 4, 640], BF16); self.r_K = Res("Kbuf")
        self.Vbuf = S("Vbuf", [128, 5, 256], BF16); self.r_V = Res("Vbuf")
        self.hT = S("hT", [128, 1024]); self.r_hT = Res("hT")
        self.hTb = S("hTb", [128, 1024], BF16); self.r_hTb = Res("hTb")
        self.xpctx = S("xpctx", [128, 16, 3], BF16); self.r_xpctx = Res("xpctx")
        self.xpctxf = S("xpctxf", [128, 16, 3]); self.r_xpctxf = Res("xpctxf")
        self.uctx = S("uctx", [128, 16, 30]); self.r_uctx = Res("uctx")
        self.rstd = S("rstd", [128, 512]); self.r_rstd = Res("rstd")
        self.tmp = [S(f"tmp{i}", [128, 512]) for i in range(4)]
        self.r_tmp = [Res(f"tmp{i}") for i in range(4)]
        self.tmp_i = 0
        self.sqb = [S(f"sqb{i}", [128, 512], BF16) for i in range(4)]
        self.r_sqb = [Res(f"sqb{i}") for i in range(4)]
        self.sqb_i = 0
        self.sm = S("sm", [128, 4, 64]); self.r_sm = [Res(f"sm{i}") for i in range(4)]
        self.sm_i = 0
        self.ps = [st.enter_context(nc.psum_tensor(f"ps{i}", [128, 512], F32)) for i in range(8)]
        self.r_ps = [Res(f"ps{i}", excl=True) for i in range(8)]
        self.ps_i = 0

    def psum(self):
        i = self.ps_i
        self.ps_i = (i + 1) % 6
        return self.ps[i], self.r_ps[i]

    def gtmp(self):
        i = self.tmp_i
        self.tmp_i = (i + 1) % 4
        return self.tmp[i], self.r_tmp[i]

    def gsqb(self):
        i = self.sqb_i
        self.sqb_i = (i + 1) % 4
        return self.sqb[i], self.r_sqb[i]

    def gdg(self):
        i = self.dg_i
        self.dg_i = (i + 1) % 12
        return self.dgb[:, i, :], self.r_dgb[i]

    def gsm(self):
        i = self.sm_i
        self.sm_i = (i + 1) % 4
        return self.sm[:, i, :], self.r_sm[i]

    def av(self, off, shape, dt=BF16):
        n = 1
        for s in shape:
            n *= s
        esz = 4 if dt == F32 else 2
        assert off % 4 == 0 and off + n * esz <= self.AR, (off, shape)
        ap = self.arena[:, off // 2: off // 2 + n * esz // 2]
        if dt == F32:
            ap = ap.bitcast(F32)
        if len(shape) == 2:
            ap = ap.rearrange("p (a b) -> p a b", a=shape[0])
        elif len(shape) == 3:
            ap = ap.rearrange("p (a b c) -> p a b c", a=shape[0], b=shape[1])
        return ap

    def ares(self, off, nbytes):
        return self.r_ar[off // 1024: (off + nbytes + 1023) // 1024]

    def ACT(self, out, in_, func, rd, wr, **kw):
        self.P.op("act", lambda e: e.activation(out=out, in_=in_, func=func, **kw), rd, wr)

    def TT(self, eng, out, in0, in1, op, rd, wr):
        self.P.op(eng, lambda e: e.tensor_tensor(out=out, in0=in0, in1=in1, op=op), rd, wr)

    def TS(self, eng, out, in0, s1, s2, op0, op1, rd, wr):
        if s2 is None:
            self.P.op(eng, lambda e: e.tensor_scalar(out=out, in0=in0, scalar1=s1, scalar2=None, op0=op0), rd, wr)
        else:
            self.P.op(eng, lambda e: e.tensor_scalar(out=out, in0=in0, scalar1=s1, scalar2=s2, op0=op0, op1=op1), rd, wr)

    def STT(self, out, in0, scalar, in1, op0, op1, rd, wr):
        self.P.op("dve", lambda e: e.scalar_tensor_tensor(out=out, in0=in0, scalar=scalar, in1=in1, op0=op0, op1=op1), rd, wr)

    def CP(self, eng, out, in_, rd, wr):
        if eng == "act":
            self.P.op("act", lambda e: e.activation(out=out, in_=in_, func=AF.Identity), rd, wr)
        else:
            self.P.op(eng, lambda e: e.tensor_copy(out=out, in_=in_), rd, wr)

    def MM(self, out, lhsT, rhs, start, stop, rd, wr):
        self.P.op("pe", lambda e: e.matmul(out, lhsT=lhsT, rhs=rhs, start=start, stop=stop), rd, wr)

    def TR(self, out, in_, ident, rd, wr):
        self.P.op("pe", lambda e: e.transpose(out, in_, ident), rd, wr)

    def RECIP(self, out, in_, rd, wr):
        self.P.op("dve", lambda e: e.reciprocal(out=out, in_=in_), rd, wr)

    def MEMSET(self, eng, ap, val, wr):
        self.P.op(eng, lambda e: e.memset(ap, val), (), wr)

    def DBG(self, name, ap, rd, shape):
        if name not in self.dbg:
            return
        if name not in self.dbg_out:
            self.dbg_out[name] = self.nc.dram_tensor("dbg_" + name, list(shape), F32, kind="ExternalOutput").ap()
        self.P.dma("sp", self.dbg_out[name], ap, reads=rd)

    def wreq(self, name, uidx, nunits, src_ap, kc, ncols, once=False):
        P = self.P
        spec = (name, uidx, nunits, src_ap, kc, ncols, once)
        if P.dry:
            self.wsched.append(spec)
            return None, None
        i = self.w_consumed
        assert self.wsched[i][0] == name and self.wsched[i][1] == uidx, (self.wsched[i][:2], name, uidx)
        while self.w_issued < min(len(self.wsched), i + self.NW):
            self._wissue(self.w_issued)
            self.w_issued += 1
        self.w_consumed += 1
        s = i % self.NW
        view = self.wsl[s][:, 0:kc * ncols].rearrange("p (k n) -> p k n", k=kc)
        return view, self.r_wsl[s]

    def _wissue(self, i):
        name, uidx, nunits, src_ap, kc, ncols, once = self.wsched[i]
        s = i % self.NW
        P = self.P
        flat = self.wsl[s][:, 0:kc * ncols]
        view = flat.rearrange("p (k n) -> p k n", k=kc)
        key = (name, uidx)
        if once or key not in self.scr_res:
            P.dma("pool", view, src_ap.rearrange("(k p) n -> p k n", p=128), writes=[self.r_wsl[s]])
            if not once:
                scr = self._scratch(name, nunits)
                r = Res("scr")
                self.scr_res[key] = r
                P.dma("sp", scr[uidx, :, 0:kc * ncols], flat, reads=[self.r_wsl[s]], writes=[r])
        else:
            scr = self._scratch(name, nunits)
            P.dma("sp", flat, scr[uidx, :, 0:kc * ncols], reads=[self.scr_res[key]], writes=[self.r_wsl[s]])

    def load_vec_fm(self, src_rows_ap, nrows, dst_ap):
        t, rt = self.gtmp()
        self.P.dma("sp", t[0:nrows, 0:128], src_rows_ap, writes=[rt])
        ps, rp = self.psum()
        self.TR(ps[:, 0:nrows], t[0:nrows, 0:128], self.ident[0:nrows, 0:nrows], [rt, self.r_ident], [rp])
        self.CP("dve", dst_ap, ps[:, 0:nrows], [rp], [self.r_cst])

    def emit_setup(self):
        P, I = self.P, self.I
        self.MEMSET("pool", self.ident[:], 0.0, [self.r_ident])
        P.op("pool", lambda e: e.affine_select(out=self.ident[:], in_=self.ident[:], pattern=[[-1, 128]],
                                                compare_op=ALU.not_equal, fill=1.0, base=0, channel_multiplier=1),
             [self.r_ident], [self.r_ident])
        self.CP("dve", self.identb[:], self.ident[:], [self.r_ident], [self.r_identb])
        self.MEMSET("pool", self.tri[:], 1.0, [self.r_tri])
        P.op("pool", lambda e: e.affine_select(out=self.tri[:], in_=self.tri[:], pattern=[[1, 128]],
                                                compare_op=ALU.is_ge, fill=0.0, base=0, channel_multiplier=-1),
             [self.r_tri], [self.r_tri])
        self.MEMSET("pool", self.onesf[:], 1.0, [self.r_onesf])
        self.MEMSET("pool", self.onesb[:], 1.0, [self.r_onesb])
        self.MEMSET("pool", self.Kbuf[:], 0.0, [self.r_K])
        self.MEMSET("pool", self.Vbuf[:], 0.0, [self.r_V])
        self.MEMSET("pool", self.hT[:], 0.0, [self.r_hT])
        self.MEMSET("pool", self.hTb[:], 0.0, [self.r_hTb])
        self.MEMSET("pool", self.xpctx[:], 0.0, [self.r_xpctx])
        self.MEMSET("pool", self.xpctxf[:], 0.0, [self.r_xpctxf])
        self.MEMSET("pool", self.uctx[:], 0.0, [self.r_uctx])
        P.dma("sp", self.flags[:], I["flags"], writes=[self.r_flags])
        c = self.cst
        rows = lambda ap, n: ap.rearrange("a (r p) -> (a r) p", p=128)
        self.load_vec_fm(rows(I["g_mix"], 32), 32, c[:, self.C_GMIX:self.C_GMIX + 32])
        self.load_vec_fm(rows(I["g_mlp"], 32), 32, c[:, self.C_GMLP:self.C_GMLP + 32])
        self.load_vec_fm(rows(I["g_final"], 16), 16, c[:, self.C_GFIN:self.C_GFIN + 16])
        self.load_vec_fm(rows(I["ssm_conv_b"], 16), 16, c[:, self.C_CB:self.C_CB + 16])
        self.load_vec_fm(rows(I["b1_ext"], 32), 32, c[:, self.C_B1:self.C_B1 + 32])
        self.load_vec_fm(rows(I["conf_dw_b"], 16), 16, c[:, self.C_DWB:self.C_DWB + 16])
        self.load_vec_fm(rows(I["conf_ln_g"], 16), 16, c[:, self.C_LNG:self.C_LNG + 16])
        self.load_vec_fm(rows(I["conf_ln_b"], 16), 16, c[:, self.C_LNB:self.C_LNB + 16])
        self.load_vec_fm(rows(I["conf_b2"], 16), 16, c[:, self.C_B2:self.C_B2 + 16])
        self.load_vec_fm(rows(I["ssm_norm_g"], 8), 8, c[:, self.C_NG:self.C_NG + 8])
        self.load_vec_fm(rows(I["ssm_conv_w"], 64), 64, c[:, self.C_CW:self.C_CW + 64])
        dww = rows(I["conf_dw_w"], 496)
        for q in range(4):
            self.load_vec_fm(dww[q * 124:(q + 1) * 124, :], 124, c[:, self.C_DWW + q * 124: self.C_DWW + (q + 1) * 124])
        for i, nm in enumerate(["ssm_dt_bias", "ssm_a_log", "ssm_d", "attn_sinks"]):
            P.dma("sp", self.rowc[:, i, :], I[nm].broadcast_to([128, 16]), writes=[self.r_rowc])
        self.ACT(self.rowc[:, 1, :], self.rowc[:, 1, :], AF.Exp, [self.r_rowc], [self.r_rowc])
        self.TS("dve", self.rowc[:, 1, :], self.rowc[:, 1, :], -1.0, None, ALU.mult, None, [self.r_rowc], [self.r_rowc])
        self.ACT(self.rowc[:, 3, :], self.rowc[:, 3, :], AF.Exp, [self.r_rowc], [self.r_rowc])
        t, rt = self.gtmp()
        P.dma("sp", t[0:1, 0:512], I["ssm_conv_b"][:, 0:512], writes=[rt])
        self.CP("dve", self.cbrow[0:1, 0:512], t[0:1, 0:512], [rt], [self.r_cbrow])
        t, rt = self.gtmp()
        P.dma("sp", t[0:1, 0:512], I["ssm_conv_b"][:, 512:1024], writes=[rt])
        self.CP("dve", self.cbrow[0:1, 512:1024], t[0:1, 0:512], [rt], [self.r_cbrow])
        t, rt = self.gtmp()
        P.dma("sp", t[0:1, 0:512], I["ssm_conv_b"][:, 1024:1536], writes=[rt])
        self.CP("dve", self.cbrow[0:1, 1024:1536], t[0:1, 0:512], [rt], [self.r_cbrow])
        for j in range(4):
            for cc in range(16):
                self.TS("dve", self.diag[:, j * 16 + cc, :], self.identb[:], c[:, self.C_CW + j * 16 + cc: self.C_CW + j * 16 + cc + 1],
                        None, ALU.mult, None, [self.r_identb, self.r_cst], [self.r_diag])
        c5t = self.av(0, [2048], F32)
        r_c5 = self.ares(0, 8192)
        P.dma("sp", c5t[0:5, :], I["c5"], writes=r_c5)
        ps, rp = self.psum()
        for k in range(16):
            self.TR(ps[:, k * 5:(k + 1) * 5], c5t[0:5, k * 128:(k + 1) * 128], self.ident[0:5, 0:5], r_c5 + [self.r_ident], [rp])
        self.ACT(self.scT[:].rearrange("p k s -> p (k s)"), ps[:, 0:80], AF.Silu, [rp], [self.r_scT])
        bm = self.av(8192, [192], F32)
        r_bm = self.ares(8192, 768)
        for l in range(2):
            t, rt = self.gtmp()
            P.dma("sp", t[0:96, 0:128], I["b_mod"][l:l + 1, :].rearrange("a (r p) -> (a r) p", p=128), writes=[rt])
            ps, rp = self.psum()
            self.TR(ps[:, 0:96], t[0:96, 0:128], self.ident[0:96, 0:96], [rt, self.r_ident], [rp])
            self.CP("dve", bm[:, l * 96:(l + 1) * 96], ps[:, 0:96], [rp], r_bm)
        for l in range(2):
            for u in range(48):
                wv, rw = self.wreq("w_mod%d" % l, u, 48, I["w_mod"][l, :, u * 256:(u + 1) * 256], 16, 256, once=True)
                ps, rp = self.psum()
                if not P.dry:
                    for k in range(16):
                        self.MM(ps[0:5, 0:256], self.scT[:, k, :], wv[:, k, :], k == 0, k == 15, [self.r_scT, rw], [rp])
                    t, rt = self.gtmp()
                    self.CP("act", t[0:5, 0:256], ps[0:5, 0:256], [rp], [rt])
                    ps2, rp2 = self.psum()
                    for h in range(2):
                        self.TR(ps2[:, h * 5:(h + 1) * 5], t[0:5, h * 128:(h + 1) * 128], self.ident[0:5, 0:5], [rt, self.r_ident], [rp2])
                    for h in range(2):
                        ch = u * 2 + h
                        self.TS("dve", self.modT[:, l, ch, :], ps2[:, h * 5:(h + 1) * 5], bm[:, l * 96 + ch: l * 96 + ch + 1], None,
                                ALU.add, None, [rp2] + r_bm, [self.r_modT])
        for l in range(2):
            for cc in range(16):
                self.TS("dve", self.gs[:, l, 0, cc, :], self.modT[:, l, 16 + cc, :], 1.0, c[:, self.C_GMIX + l * 16 + cc: self.C_GMIX + l * 16 + cc + 1],
                        ALU.add, ALU.mult, [self.r_modT, self.r_cst], [self.r_gs])
                self.TS("dve", self.gs[:, l, 1, cc, :], self.modT[:, l, 64 + cc, :], 1.0, c[:, self.C_GMLP + l * 16 + cc: self.C_GMLP + l * 16 + cc + 1],
                        ALU.add, ALU.mult, [self.r_modT, self.r_cst], [self.r_gs])
        for cc in range(16):
            self.TS("dve", self.b2g[:, cc, :], self.modT[:, 1, 32 + cc, :], c[:, self.C_B2 + cc: self.C_B2 + cc + 1], None,
                    ALU.mult, None, [self.r_modT, self.r_cst], [self.r_b2g])

    def m_sh(self, l, which, cc, s):
        return self.modT[:, l, (0 if which == 0 else 48) + cc, s:s + 1]

    def m_gt(self, l, which, cc, s):
        return self.modT[:, l, (32 if which == 0 else 80) + cc, s:s + 1]

    def m_gs(self, l, which, cc, s):
        return self.gs[:, l, which, cc, s:s + 1]

    def load_x(self, blocks):
        P = self.P
        for bi, (src, c0, n) in enumerate(blocks):
            off = (bi % 2) * 8192
            xs = self.av(off, [2048], F32)
            rx = self.ares(off, 8192)
            P.dma("sp", xs[0:n, :], src, writes=rx)
            for g in range(4):
                ps, rp = self.psum()
                for q in range(4):
                    cc = g * 4 + q
                    self.TR(ps[:, q * 128: q * 128 + n], xs[0:n, cc * 128:(cc + 1) * 128], self.ident[0:n, 0:n], rx + [self.r_ident], [rp])
                for q in range(4):
                    cc = g * 4 + q
                    self.CP("act" if g % 2 else "dve", self.xres[:, cc, c0:c0 + n], ps[:, q * 128:q * 128 + n], [rp], [self.r_xres[cc]])
                    if bi == len(blocks) - 1:
                        self.stat_chunk(cc, c0 + n)

    def stat_chunk(self, cc, T):
        sq, rsq = self.gsqb()
        self.ACT(sq[:, 0:T], self.xres[:, cc, 0:T], AF.Square, [self.r_xres[cc]], [rsq])
        self.stat_pend.append((sq, rsq))
        if len(self.stat_pend) > 2:
            self._stat_mm(T)

    def _stat_mm(self, T):
        sq, rsq = self.stat_pend.pop(0)
        ps, rp = self.ps[7], self.r_ps[7]
        self.MM(ps[:, 0:T], self.onesb[:], sq[:, 0:T], self.stat_n == 0, self.stat_n == 15, [self.r_onesb, rsq], [rp])
        self.stat_n += 1

    def norm(self, T, segs, l, which, hn, r_hn, final=False):
        while self.stat_pend:
            self._stat_mm(T)
        assert self.P.dry or self.stat_n == 16, self.stat_n
        self.stat_n = 0
        ps, rp = self.ps[7], self.r_ps[7]
        self.ACT(self.rstd[:, 0:T], ps[:, 0:T], AF.Sqrt, [rp], [self.r_rstd], scale=1.0 / D, bias=EPS)
        self.RECIP(self.rstd[:, 0:T], self.rstd[:, 0:T], [self.r_rstd], [self.r_rstd])
        if final:
            return
        for cc in range(16):
            t, rt = self.gtmp()
            for (c0, n, s) in segs:
                self.STT(t[:, c0:c0 + n], self.xres[:, cc, c0:c0 + n], self.m_gs(l, which, cc, s), self.rstd[:, c0:c0 + n],
                         ALU.mult, ALU.mult, [self.r_xres[cc], self.r_gs, self.r_rstd], [rt])
                self.ACT(hn[:, cc, c0:c0 + n], t[:, c0:c0 + n], AF.Identity, [rt, self.r_modT], [r_hn[cc]],
                         bias=self.m_sh(l, which, cc, s))

    def proj_fm(self, name, src, ncols_total, kc, in_buf, r_in, T, evac, col0=0, ncols=None, kbase=0, nunits=None, ubase=0,
                unit_cols=256):
        P = self.P
        ncols = ncols_total - col0 if ncols is None else ncols
        nu = (ncols + unit_cols - 1) // unit_cols
        nunits = nu if nunits is None else nunits
        for u in range(nu):
            cw = min(unit_cols, ncols - u * unit_cols)
            sap = src[kbase * 128:(kbase + kc) * 128, col0 + u * unit_cols: col0 + u * unit_cols + cw]
            wv, rw = self.wreq(name, ubase + u, nunits, sap, kc, cw)
            if P.dry:
                continue
            for h in range(cw // 128):
                ps, rp = self.psum()
                for k in range(kc):
                    self.MM(ps[:, 0:T], wv[:, k, h * 128:(h + 1) * 128], in_buf[:, k, 0:T], k == 0, k == kc - 1,
                            [rw, r_in[k]], [rp])
                evac(u * (unit_cols // 128) + h, ps, rp)

    def l0_mixer(self, tile):
        P, I = self.P, self.I
        T, segs, blocks, kind = tile["T"], tile["segs"], tile["blocks"], tile["kind"]
        pre = kind == "pre"
        KB = 1024
        hn = self.av(0, [16, 512]); r_hn = [self.ares(c * KB, KB)[0] for c in range(16)]
        Q = self.av(16 * KB, [8, 512]); r_Q = [self.ares(16 * KB + c * KB, KB)[0] for c in range(8)]
        nslot = len(blocks)
        zb = self.av(24 * KB, [6, 1024]); r_z = [self.ares(24 * KB + b * 2048, 2048) for b in range(6)]
        XW = tile["xpw"]
        xp = self.av(36 * KB, [16, XW]); r_xp = [self.ares(36 * KB + c * XW * 2, XW * 2) for c in range(16)]
        xst = self.av(53 * KB, [6, 1024]); r_xst = [self.ares(53 * KB + b * 2048, 2048) for b in range(6)]
        Bt = self.av(65 * KB, [6, 512]); r_Bt = [self.ares(65 * KB + b * 1024, 1024) for b in range(6)]
        BCT = self.av(0, [8, 512]); r_BCT = [self.ares(c * KB, KB)[0] for c in range(8)]
        mix = self.av(36 * KB, [16, 512]); r_mix = [self.ares(36 * KB + c * KB, KB)[0] for c in range(16)]
        dtb = self.av(71 * KB, [6, 2, 16], F32); r_dtb = self.ares(71 * KB, 768)
        KSUB = int(os.environ.get("KSUB", "99"))
        if KSUB <= 1:
            return
        self.norm(T, segs, 0, 0, hn, r_hn)
        if KSUB <= 2:
            return
        ropec = self.av(63 * KB, [512], F32); ropes = self.av(65 * KB, [512], F32)
        r_rope = self.ares(63 * KB, 4096)
        if not pre:
            P.dma("sp", ropec[:, 0:T], I["rope_cos"][:, tile["rc0"]:tile["rc0"] + T], writes=r_rope)
            P.dma("sp", ropes[:, 0:T], I["rope_sin"][:, tile["rc0"]:tile["rc0"] + T], writes=r_rope)
        W = I["w_in_ext"]
        if not pre:
            kfull = {}

            def evac_qk(ci, ps, rp, st={}):
                u, h = ci // 2, ci % 2
                if h == 0:
                    t1, r1 = self.gtmp()
                    self.TT("dve", t1[:, 0:T], ps[:, 0:T], ropec[:, 0:T], ALU.mult, [rp] + r_rope, [r1])
                    st["t1"] = (t1, r1)
                else:
                    t1, r1 = st["t1"]
                    t2, r2 = self.gtmp()
                    self.TT("dve", t2[:, 0:T], ps[:, 0:T], ropes[:, 0:T], ALU.mult, [rp] + r_rope, [r2])
                    if u < 8:
                        self.TT("pool", Q[:, u, 0:T], t1[:, 0:T], t2[:, 0:T], ALU.add, [r1, r2], [r_Q[u]])
                    else:
                        j = u - 8
                        self.TT("pool", t1[:, 0:T], t1[:, 0:T], t2[:, 0:T], ALU.add, [r1, r2], [r1])
                        self.k_store(tile, j, t1, r1)

            self.proj_fm("w_in", W, WEXT, 16, hn, r_hn, T, evac_qk, col0=QK0, ncols=3072, nunits=29, ubase=0)
            wv, rw = self.wreq("w_in", 12, 29, W[:, V0:V0 + 256], 16, 256)
            if not P.dry:
                for bi, (c0, n, bk) in enumerate(blocks):
                    ps, rp = self.psum()
                    for k in range(16):
                        self.MM(ps[0:n, 0:256], hn[:, k, c0:c0 + n], wv[:, k, :], k == 0, k == 15, [r_hn[k], rw], [rp])
                    self.v_store(tile, bi, ps, rp)
        if not pre:
            for u in range(4):
                wv, rw = self.wreq("w_in", 13 + u, 29, W[:, Z0 + u * 256: Z0 + (u + 1) * 256], 16, 256)
                if P.dry:
                    continue
                for bi, (c0, n, bk) in enumerate(blocks):
                    ps, rp = self.psum()
                    for k in range(16):
                        self.MM(ps[0:n, 0:256], hn[:, k, c0:c0 + n], wv[:, k, :], k == 0, k == 15, [r_hn[k], rw], [rp])
                    self.ACT(zb[0:n, bi, u * 256:(u + 1) * 256], ps[0:n, 0:256], AF.Silu, [rp], r_z[bi])
        so = tile["segoff"]
        self.CP("pool", xp[:, :, so[0]:so[0] + 3], self.xpctx[:], [self.r_xpctx], sum(r_xp, []))

        def evac_xbc(ci, ps, rp):
            for si, (c0, n, s) in enumerate(segs):
                self.CP("act", xp[:, ci, so[si] + 3: so[si] + 3 + n], ps[:, c0:c0 + n], [rp], r_xp[ci])
                if si == 0 and not pre:
                    self.CP("dve", self.xpctxf[:, ci, :], ps[:, c0 + n - 3:c0 + n], [rp], [self.r_xpctxf])
                elif si > 0:
                    self.CP("dve", tile["sconvf"][:, si - 1, ci, :], ps[:, c0 + n - 3:c0 + n], [rp], tile["r_sconvf"])

        ncx = 1536 if pre else 2048
        self.proj_fm("w_in", W, WEXT, 16, hn, r_hn, T, evac_xbc, col0=XBC0, ncols=ncx, nunits=29, ubase=17)
        if KSUB <= 3:
            return
        wv, rw = self.wreq("w_in", 25, 29, W[:, DT0:DT0 + 16], 16, 16)
        if not P.dry:
            for bi, (c0, n, bk) in enumerate(blocks):
                ps, rp = self.psum()
                for k in range(16):
                    self.MM(ps[0:n, 0:16], hn[:, k, c0:c0 + n], wv[:, k, :], k == 0, k == 15, [r_hn[k], rw], [rp])
                self.TT("dve", dtb[0:n, bi, 0, :], ps[0:n, 0:16], self.rowc[0:n, 0, :], ALU.add, [rp, self.r_rowc], r_dtb)
                self.ACT(dtb[0:n, bi, 0, :], dtb[0:n, bi, 0, :], AF.Exp, r_dtb, r_dtb)
                self.ACT(dtb[0:n, bi, 0, :], dtb[0:n, bi, 0, :], AF.Ln, r_dtb, r_dtb, bias=1.0)
                self.TT("dve", dtb[0:n, bi, 1, :], dtb[0:n, bi, 0, :], self.rowc[0:n, 1, :], ALU.mult, r_dtb + [self.r_rowc], r_dtb)
        if P.dry:
            if not pre:
                self.proj_fm("w_out", I["w_out"], D, 16, mix, r_mix, T, None)
            return
        if KSUB <= 4:
            return
        if kind == "halo":
            P.dma("sp", self.O["k_s"].rearrange("(s t) c -> t s c", t=16), tile["kso"][0:16, :, :], reads=tile["r_kso"])
            P.dma("sp", self.O["v_s"].rearrange("(s t) c -> t s c", t=16), tile["vso"][0:16, :, :], reads=tile["r_vso"])
        if tile.get("last"):
            P.dma("sp", self.O["k_p"], tile["kpo"][:, :], reads=tile["r_kpo"])
            P.dma("sp", self.O["v_p"], tile["vpo"][:, :], reads=tile["r_vpo"])
        if kind == "halo":
            for s in range(4):
                self.load_ctx_fm(I["st_sconv"][s], 3, xp, r_xp, so[1 + s])
        c = self.cst
        for si, (c0, n, s) in enumerate(segs):
            if not pre:
                for cc in range(8, 16):
                    ps, rp = self.psum()
                    for j in range(4):
                        self.MM(ps[:, 0:n], self.diag[:, j * 16 + cc, :], xp[:, cc, so[si] + j: so[si] + j + n], j == 0, j == 3,
                                [self.r_diag] + r_xp[cc], [rp])
                    self.ACT(BCT[:, cc - 8, c0:c0 + n], ps[:, 0:n], AF.Silu, [rp, self.r_cst], [r_BCT[cc - 8]],
                             bias=c[:, self.C_CB + cc:self.C_CB + cc + 1])
        for bi, (c0, n, bk) in enumerate(blocks):
            si = bk["seg"]
            o = so[si] + (c0 - segs[si][0])
            for g3 in range(3):
                ps, rp = self.psum()
                for q in range(4):
                    cc = g3 * 4 + q
                    for j in range(4):
                        self.MM(ps[0:n, q * 128:(q + 1) * 128], xp[:, cc, o + j: o + j + n], self.diag[:, j * 16 + cc, :], j == 0, False,
                                r_xp[cc] + [self.r_diag], [rp])
                    self.MM(ps[0:n, q * 128:(q + 1) * 128], self.onesb[0:1, 0:n], self.cbrow[0:1, cc * 128:(cc + 1) * 128], False, True,
                            [self.r_onesb, self.r_cbrow], [rp])
                if g3 < 2:
                    self.ACT(xst[0:n, bi, g3 * 512:(g3 + 1) * 512], ps[0:n, :], AF.Silu, [rp], r_xst[bi])
                else:
                    self.ACT(Bt[0:n, bi, :], ps[0:n, :], AF.Silu, [rp], r_Bt[bi])
        n0 = segs[0][1]
        ncs = 12 if pre else 16
        self.CP("pool", self.xpctx[:, 0:ncs, :], xp[:, 0:ncs, so[0] + n0: so[0] + n0 + 3], sum(r_xp[0:ncs], []), [self.r_xpctx])
        if KSUB <= 5:
            return
        if not pre:
            self.attention(tile, Q, r_Q, mix, r_mix)
        for bi, (c0, n, bk) in enumerate(blocks):
            self.ssd_block(tile, bi, c0, n, bk, xst, r_xst, Bt, r_Bt, BCT, r_BCT, zb, r_z, dtb, r_dtb, mix, r_mix)
        if pre:
            return
        def evac_out(ci, ps, rp):
            for (c0, n, s) in segs:
                self.STT(self.xres[:, ci, c0:c0 + n], ps[:, c0:c0 + n], self.m_gt(0, 0, ci, s), self.xres[:, ci, c0:c0 + n],
                         ALU.mult, ALU.add, [rp, self.r_modT, self.r_xres[ci]], [self.r_xres[ci]])
            self.stat_chunk(ci, T)

        self.proj_fm("w_out", I["w_out"], D, 16, mix, r_mix, T, evac_out)

    def load_ctx_fm(self, src, nrows, dst, r_dst, coloff):
        for qq in range(4):
            stg, r_stg = self.gtmp()
            self.P.dma("sp", stg[0:nrows, :], src[:, qq * 512:(qq + 1) * 512], writes=[r_stg])
            ps, rp = self.psum()
            for q in range(4):
                self.TR(ps[:, q * 32: q * 32 + nrows], stg[0:nrows, q * 128:(q + 1) * 128], self.ident[0:nrows, 0:nrows],
                        [r_stg, self.r_ident], [rp])
            for q in range(4):
                cc = qq * 4 + q
                self.CP("dve", dst[:, cc, coloff:coloff + nrows], ps[:, q * 32:q * 32 + nrows], [rp], r_dst[cc])

    def k_store(self, tile, j, t1, r1):
        segs = tile["segs"]
        c0, n, s = segs[0]
        self.CP("act", self.Kbuf[:, j, 128:128 + n], t1[:, 0:n], [r1], [self.r_K])
        if tile["kind"] == "halo":
            self.CP("act", tile["Kown"][:, j, :], t1[:, 256:320], [r1], tile["r_Kown"])
            for si in range(1, 5):
                c0, n, s = segs[si]
                ps, rp = self.psum()
                self.TR(ps[0:16, 0:64], t1[0:64, c0:c0 + 16], self.ident[0:64, 0:64], [r1, self.r_ident], [rp])
                self.CP("dve", tile["kso"][0:16, si - 1, j * 64:(j + 1) * 64], ps[0:16, 0:64], [rp], tile["r_kso"])
        if tile.get("last"):
            ps, rp = self.psum()
            self.TR(ps[:, 0:64], t1[0:64, 384:512], self.ident[0:64, 0:64], [r1, self.r_ident], [rp])
            self.CP("dve", tile["kpo"][:, j * 64:(j + 1) * 64], ps[:, 0:64], [rp], tile["r_kpo"])

    def v_store(self, tile, bi, ps, rp):
        c0, n, bk = tile["blocks"][bi]
        if bk["seg"] == 0:
            self.CP("act", self.Vbuf[0:n, 1 + bi, :], ps[0:n, 0:256], [rp], [self.r_V])
            if tile.get("last") and bi == 3:
                self.CP("dve", tile["vpo"][:, :], ps[:, 0:256], [rp], tile["r_vpo"])
        else:
            s = bk["seg"] - 1
            self.CP("act", tile["Vsmp"][0:16, s, :], ps[0:16, 0:256], [rp], tile["r_Vsmp"])
            self.CP("dve", tile["vso"][0:16, s, :], ps[0:16, 0:256], [rp], tile["r_vso"])

    def attn_group(self, Q, r_Q, qc0, nq, ktiles, mix, r_mix, mc0):
        KBY = 1024
        PT = self.av(8 * KBY, [2, 4, 64]); r_PT = self.ares(8 * KBY, 1024)
        nkt = len(ktiles)
        for j in range(4):
            pss = [self.psum(), self.psum()]
            for kt, (Kfn, Vfn, nk, p0, kp, bias, rd) in enumerate(ktiles):
                for g in range(4):
                    h = 4 * j + g
                    half = h % 2
                    ps, rp = pss[half]
                    self.MM(ps[p0:p0 + nk, kt * 128 + (g // 2) * 64: kt * 128 + (g // 2) * 64 + nq], Kfn(j, half),
                            Q[half * 64:(half + 1) * 64, h // 2, qc0:qc0 + nq], True, True, rd + [r_Q[h // 2]], [rp])
            for kt, (Kfn, Vfn, nk, p0, kp, bias, rd) in enumerate(ktiles):
                if nk < kp:
                    self.MEMSET("pool", PT[:, kt, :, :], 0.0, r_PT)
                for half in range(2):
                    ps, rp = pss[half]
                    src = ps[p0:p0 + nk, kt * 128:(kt + 1) * 128].rearrange("p (g q) -> p g q", g=2)[:, :, 0:nq]
                    dst = PT[p0:p0 + nk, kt, :, :].rearrange("p (gg hh) q -> p gg hh q", hh=2)[:, :, half, 0:nq]
                    kw = {"scale": 0.125}
                    rdd = [rp]
                    if bias is not None:
                        kw["bias"] = bias[p0:p0 + nk, :]
                        rdd.append(self.r_flags)
                    self.ACT(dst, src, AF.Exp, rdd, r_PT, **kw)
            psd, rpd = self.psum()
            for kt, (Kfn, Vfn, nk, p0, kp, bias, rd) in enumerate(ktiles):
                self.MM(psd[:, 0:4 * nq].rearrange("p (g q) -> p g q", g=4), self.onesb[0:kp, :], PT[0:kp, kt, :, 0:nq],
                        kt == 0, kt == nkt - 1, [self.r_onesb] + r_PT, [rpd])
            pso, rpo = self.psum()
            for g in range(4):
                ph = (g % 2) * 64
                for kt, (Kfn, Vfn, nk, p0, kp, bias, rd) in enumerate(ktiles):
                    self.MM(pso[ph:ph + 64, (g // 2) * 64:(g // 2) * 64 + nq], Vfn(j), PT[0:kp, kt, g, 0:nq],
                            kt == 0, kt == nkt - 1, rd + r_PT, [rpo])
            den, rden = self.gtmp()
            self.TT("dve", den[:, 0:4 * nq].rearrange("p (g q) -> p g q", g=4), psd[:, 0:4 * nq].rearrange("p (g q) -> p g q", g=4),
                    self.rowc[:, 3, 4 * j:4 * j + 4].unsqueeze(2).to_broadcast([128, 4, nq]), ALU.add, [rpd, self.r_rowc], [rden])
            self.RECIP(den[:, 0:4 * nq], den[:, 0:4 * nq], [rden], [rden])
            for g in range(4):
                ph = (g % 2) * 64
                cc = 2 * j + g // 2
                self.TT("dve", mix[ph:ph + 64, cc, mc0:mc0 + nq], pso[ph:ph + 64, (g // 2) * 64:(g // 2) * 64 + nq],
                        den[ph:ph + 64, g * nq:(g + 1) * nq], ALU.mult, [rpo, rden], [r_mix[cc]])

    def attention(self, tile, Q, r_Q, mix, r_mix):
        P, I = self.P, self.I
        segs = tile["segs"]
        c0, n, s = segs[0]
        first_main = tile.get("first_main", False)
        for qi in range(n // 64):
            qc = qi * 64
            kts = []
            lo = qc - 128
            pieces = [(lo, 128), (lo + 128, 64)] if lo % 128 == 0 else [(lo, 64), (lo + 64, 128)]
            for (k0, nk) in pieces:
                blk = (k0 + 128) // 128
                p0 = (k0 + 128) % 128
                bias = self.flags[:, 1:2] if (first_main and k0 < 0) else None
                Kfn = (lambda j, half, k0=k0, nk=nk: self.Kbuf[half * 64:(half + 1) * 64, j, 128 + k0:128 + k0 + nk])
                Vfn = (lambda j, blk=blk: self.Vbuf[:, blk, j * 64:(j + 1) * 64])
                kts.append((Kfn, Vfn, nk, p0, 128, bias, [self.r_K, self.r_V]))
            self.attn_group(Q, r_Q, c0 + qc, 64, kts, mix, r_mix, c0 + qc)
        if tile["kind"] == "halo":
            Ks, Vc, Vo, Kown = tile["Ksmp"], tile["Vcache"], tile["Vsmp"], tile["Kown"]
            for si in range(1, 5):
                c0, n, s = segs[si]
                sq = si - 1
                stg, r_stg = self.gtmp()
                P.dma("sp", stg[:, :], I["ck"][sq], writes=[r_stg])
                ps, rp = self.psum()
                for j in range(4):
                    self.TR(ps[:, j * 128:(j + 1) * 128], stg[:, j * 128:(j + 1) * 128], self.ident[:], [r_stg, self.r_ident], [rp])
                self.CP("dve", Ks[:, :, 0:128], ps[:, :].rearrange("p (j t) -> p j t", j=4), [rp], tile["r_Ksmp"])
                self.CP("pool", Ks[:, :, 128:144], Kown[:, :, sq * 16:(sq + 1) * 16], tile["r_Kown"], tile["r_Ksmp"])
                stg2, r_stg2 = self.gtmp()
                P.dma("sp", stg2[:, 0:256], I["cv"][sq], writes=[r_stg2])
                self.CP("dve", Vc[:, :], stg2[:, 0:256], [r_stg2], tile["r_Vcache"])
                kts = [
                    ((lambda j, half: Ks[half * 64:(half + 1) * 64, j, 0:128]),
                     (lambda j: Vc[:, j * 64:(j + 1) * 64]), 128, 0, 128, None, tile["r_Ksmp"] + tile["r_Vcache"]),
                    ((lambda j, half: Ks[half * 64:(half + 1) * 64, j, 128:144]),
                     (lambda j, sq=sq: Vo[0:16, sq, j * 64:(j + 1) * 64]), 16, 0, 16, None, tile["r_Ksmp"] + tile["r_Vsmp"]),
                ]
                self.attn_group(Q, r_Q, c0, 16, kts, mix, r_mix, c0)
        c0, n, s = segs[0]
        self.CP("pool", self.Kbuf[:, :, 0:128], self.Kbuf[:, :, n:n + 128], [self.r_K], [self.r_K])
        self.CP("pool", self.Vbuf[:, 0, :], self.Vbuf[:, n // 128, :], [self.r_V], [self.r_V])

    def ssd_block(self, tile, bi, c0, L, bk, xst, r_xst, Bt, r_Bt, BCT, r_BCT, zb, r_z, dtb, r_dtb, mix, r_mix):
        pre = tile["kind"] == "pre"
        KBY = 1024
        base = 8 * KBY
        if bk["seg"] == 0:
            hT, r_hT, hTb, r_hTb = self.hT, [self.r_hT], self.hTb, [self.r_hTb]
        else:
            sq = bk["seg"] - 1
            hT, r_hT = tile["hTs"], tile["r_hTs"]
            hTb, r_hTb = tile["hTsb"], tile["r_hTsb"]
            for g2 in range(2):
                stg3, r_stg3 = self.gtmp()
                self.P.dma("sp", stg3[:, :].rearrange("p (a n) -> p a n", a=4),
                           self.I["st_ssm"][sq].rearrange("(a p) n -> p a n", p=128)[:, g2 * 4:(g2 + 1) * 4, :], writes=[r_stg3])
                ps, rp = self.psum()
                for q in range(4):
                    self.TR(ps[:, q * 128:(q + 1) * 128], stg3[:, q * 128:(q + 1) * 128], self.ident[:], [r_stg3, self.r_ident], [rp])
                self.CP("dve", hT[:, g2 * 512:(g2 + 1) * 512], ps[:, :], [rp], r_hT)
                self.CP("act", hTb[:, g2 * 512:(g2 + 1) * 512], ps[:, :], [rp], r_hTb)
        ysb = self.av(base + 8 * KBY, [1024], F32); rys = self.ares(base + 8 * KBY, 4096)
        ynb = self.av(base + 12 * KBY, [1024]); ryn = self.ares(base + 12 * KBY, 2048)
        dt = dtb[0:L, bi, 0, :]
        dA = dtb[0:L, bi, 1, :]
        acum, racum = self.gsm()
        ps, rp = self.psum()
        self.MM(ps[0:L, 0:16], self.tri[0:L, 0:L], dA, True, True, [self.r_tri] + r_dtb, [rp])
        self.MM(ps[:, 16:32], self.onesf[0:L, :], dA, True, True, [self.r_onesf] + r_dtb, [rp])
        self.CP("dve", acum[0:L, 0:16], ps[0:L, 0:16], [rp], [racum])
        tot = acum[:, 16:32]
        self.CP("dve", tot, ps[:, 16:32], [rp], [racum])
        te = acum[:, 32:48]
        self.TT("dve", te[0:L, :], tot[0:L, :], acum[0:L, 0:16], ALU.subtract, [racum], [racum])
        self.ACT(te[0:L, :], te[0:L, :], AF.Exp, [racum], [racum])
        self.TT("dve", te[0:L, :], te[0:L, :], dt, ALU.mult, [racum] + r_dtb, [racum])
        dec = acum[:, 48:64]
        self.ACT(dec, tot, AF.Exp, [racum], [racum])
        psy = [None, None]
        for g in range(4):
            so_ = base + (g % 2) * 8 * KBY
            bufA = self.av(so_, [4, 128], F32); rA = self.ares(so_, 2048)
            bufB = self.av(so_ + 2 * KBY, [4, 128], F32); rB = self.ares(so_ + 2 * KBY, 2048)
            CBm = self.av(so_ + 4 * KBY, [128], F32); rCB = self.ares(so_ + 4 * KBY, 512)
            GT = self.av(so_ + 5 * KBY, [4, 128]); rG = self.ares(so_ + 5 * KBY, 1024)
            CsT = self.av(so_ + 6 * KBY, [4, 128]); rCs = self.ares(so_ + 6 * KBY, 1024)
            xdt = self.av(so_ + 7 * KBY, [4, 64]); rxdt = self.ares(so_ + 7 * KBY, 512)
            xdte = self.av(so_ + 7 * KBY + 512, [4, 64]); rxdte = self.ares(so_ + 7 * KBY + 512, 512)
            xs_g = xst[0:L, bi, g * 256:(g + 1) * 256].rearrange("p (r d) -> p r d", r=4)
            if not pre:
                self.TT("dve", bufA[0:L, :, 0:L], self.tri[0:L, 0:L].unsqueeze(1).to_broadcast([L, 4, L]),
                        dA[:, 4 * g:4 * g + 4].unsqueeze(2).to_broadcast([L, 4, L]), ALU.mult, [self.r_tri] + r_dtb, rA)
                ps, rp = self.psum()
                for r in range(4):
                    self.MM(ps[:, r * L:(r + 1) * L], self.onesf[0:L, :], bufA[0:L, r, 0:L], True, True, [self.r_onesf] + rA, [rp])
                self.CP("act", bufB[:, :, 0:L], ps[:, 0:4 * L].rearrange("p (r l) -> p r l", r=4), [rp], rB)
                self.TT("dve", bufA[0:L, :, 0:L], bufB[0:L, :, 0:L], acum[0:L, 4 * g:4 * g + 4].unsqueeze(2).to_broadcast([L, 4, L]),
                        ALU.subtract, rB + [racum], rA)
                self.TS("dve", bufA[0:L, :, 0:L], bufA[0:L, :, 0:L], 0.0, None, ALU.min, None, rA, rA)
                self.ACT(bufA[0:L, :, 0:L], bufA[0:L, :, 0:L], AF.Exp, rA, rA)
                ps2, rp2 = self.psum()
                self.MM(ps2[0:L, 0:L], BCT[:, g, c0:c0 + L], BCT[:, 4 + g, c0:c0 + L], True, True, [r_BCT[g], r_BCT[4 + g]], [rp2])
                self.TT("dve", CBm[0:L, 0:L], ps2[0:L, 0:L], self.tri[0:L, 0:L], ALU.mult, [rp2, self.r_tri], rCB)
                self.TT("dve", GT[0:L, :, 0:L], bufA[0:L, :, 0:L], CBm[0:L, 0:L].unsqueeze(1).to_broadcast([L, 4, L]), ALU.mult,
                        rA + rCB, rG)
                self.ACT(bufB[:, :, 0:L], bufB[:, :, 0:L], AF.Exp, rB, rB)
                self.TT("dve", CsT[:, :, 0:L], bufB[:, :, 0:L], BCT[:, 4 + g, c0:c0 + L].unsqueeze(1).to_broadcast([128, 4, L]), ALU.mult,
                        rB + [r_BCT[4 + g]], rCs)
                self.TT("dve", xdt[0:L, :, :], xs_g, dt[:, 4 * g:4 * g + 4].unsqueeze(2).to_broadcast([L, 4, 64]), ALU.mult,
                        r_xst[bi] + r_dtb, rxdt)
                if g % 2 == 0:
                    psy[g // 2] = (self.ps[6 + g // 2], self.r_ps[6 + g // 2])
                py, rpy = psy[g // 2]
                for r in range(4):
                    h = 4 * g + r
                    col = (h % 8) * 64
                    self.MM(py[0:L, col:col + 64], GT[0:L, r, 0:L], xdt[0:L, r, :], True, False, rG + rxdt, [rpy])
                    self.MM(py[0:L, col:col + 64], CsT[:, r, 0:L], hTb[:, h * 64:(h + 1) * 64], False, True, rCs + r_hTb, [rpy])
            self.TT("dve", xdte[0:L, :, :], xs_g, te[0:L, 4 * g:4 * g + 4].unsqueeze(2).to_broadcast([L, 4, 64]), ALU.mult,
                    r_xst[bi] + [racum], rxdte)
            ps3, rp3 = self.psum()
            self.MM(ps3[:, 0:256], Bt[0:L, bi, g * 128:(g + 1) * 128], xdte[0:L, :, :].rearrange("p r d -> p (r d)"), True, True,
                    r_Bt[bi] + rxdte, [rp3])
            hg = hT[:, g * 256:(g + 1) * 256]
            self.TT("dve", hg.rearrange("p (r d) -> p r d", r=4), hg.rearrange("p (r d) -> p r d", r=4),
                    dec[:, 4 * g:4 * g + 4].unsqueeze(2).to_broadcast([128, 4, 64]), ALU.mult, r_hT + [racum], r_hT)
            self.TT("dve", hg, hg, ps3[:, 0:256], ALU.add, r_hT + [rp3], r_hT)
            self.CP("act", hTb[:, g * 256:(g + 1) * 256], hg, r_hT, r_hTb)
        if bk["seg"] > 0:
            self.out_state(hT, r_hT, self.O["hst_s"][bk["seg"] - 1])
        if pre:
            return
        Dx = self.av(base, [1024], F32); rDx = self.ares(base, 4096)
        self.TT("dve", Dx[0:L, :].rearrange("p (h d) -> p h d", h=16), xst[0:L, bi, :].rearrange("p (h d) -> p h d", h=16),
                self.rowc[0:L, 2, :].unsqueeze(2).to_broadcast([L, 16, 64]), ALU.mult, r_xst[bi] + [self.r_rowc], rDx)
        for hh in range(2):
            py, rpy = psy[hh]
            self.TT("dve", ysb[0:L, hh * 512:(hh + 1) * 512], py[0:L, :], Dx[0:L, hh * 512:(hh + 1) * 512], ALU.add, [rpy] + rDx, rys)
        self.TT("dve", ysb[0:L, :], ysb[0:L, :], zb[0:L, bi, :], ALU.mult, rys + r_z[bi], rys)
        ssq, rssq = self.gsm()
        self.MEMSET("pool", ssq[:, 0:4], 0.0, [rssq])
        for g in range(4):
            self.ACT(Dx[0:L, g * 256:(g + 1) * 256], ysb[0:L, g * 256:(g + 1) * 256], AF.Square, rys + [rssq], rDx + [rssq],
                     accum_out=ssq[0:L, g:g + 1])
        self.P.op("act", lambda e: e.activation(out=ssq[0:L, 0:4], in_=ssq[0:L, 0:4], func=AF.Sqrt, scale=1.0 / 256, bias=EPS),
                  rDx + [rssq], [rssq])
        self.RECIP(ssq[0:L, 0:4], ssq[0:L, 0:4], [rssq], [rssq])
        self.TT("dve", ynb[0:L, :].rearrange("p (g d) -> p g d", g=4), ysb[0:L, :].rearrange("p (g d) -> p g d", g=4),
                ssq[0:L, 0:4].unsqueeze(2).to_broadcast([L, 4, 256]), ALU.mult, rys + [rssq], ryn)
        c = self.cst
        for hh in range(2):
            ps, rp = self.psum()
            pb = ps[:].bitcast(BF16)
            for q in range(4):
                cc = hh * 4 + q
                self.TR(pb[:, q * 128:q * 128 + L], ynb[0:L, cc * 128:(cc + 1) * 128], self.identb[0:L, 0:L], ryn + [self.r_identb], [rp])
            for q in range(4):
                cc = hh * 4 + q
                self.ACT(mix[:, 8 + cc, c0:c0 + L], pb[:, q * 128:q * 128 + L], AF.Identity, [rp, self.r_cst], [r_mix[8 + cc]],
                         scale=c[:, self.C_NG + cc:self.C_NG + cc + 1])

    def mlp(self, tile, l):
        P, I = self.P, self.I
        T, segs = tile["T"], tile["segs"]
        KB = 1024
        hn = self.av(0, [16, 512]); r_hn = [self.ares(c * KB, KB)[0] for c in range(16)]
        hid = self.av(16 * KB, [32, 512]); r_hid = [self.ares(16 * KB + c * KB, KB)[0] for c in range(32)]
        self.norm(T, segs, l, 1, hn, r_hn)
        for half in range(2):
            def evac_up(ci, ps, rp):
                t, rt = self.gtmp()
                self.ACT(t[:, 0:T], ps[:, 0:T], AF.Relu, [rp], [rt])
                self.TT("pool" if ci % 2 else "dve", hid[:, ci, 0:T], t[:, 0:T], t[:, 0:T], ALU.mult, [rt], [r_hid[ci]])

            self.proj_fm("w_up%d" % l, I["mlp_w_up"][l], DFF, 16, hn, r_hn, T, evac_up, col0=half * 4096, ncols=4096,
                         nunits=32, ubase=half * 16)

            def evac_dn(ci, ps, rp, half=half):
                for (c0, n, s) in segs:
                    self.STT(self.xres[:, ci, c0:c0 + n], ps[:, c0:c0 + n], self.m_gt(l, 1, ci, s), self.xres[:, ci, c0:c0 + n],
                             ALU.mult, ALU.add, [rp, self.r_modT, self.r_xres[ci]], [self.r_xres[ci]])
                if half == 1:
                    self.stat_chunk(ci, T)

            self.proj_fm("w_dn%d" % l, I["mlp_w_down"][l], D, 32, hid, r_hid, T, evac_dn, kbase=half * 32, nunits=32,
                         ubase=half * 16, unit_cols=128)

    def l1_conf(self, tile):
        P, I = self.P, self.I
        T, segs, kind = tile["T"], tile["segs"], tile["kind"]
        KB = 1024
        c = self.cst
        hn = self.av(0, [16, 512]); r_hn = [self.ares(cc * KB, KB)[0] for cc in range(16)]
        UW = tile["uw"]
        uo = tile["uoff"]
        UB = 16 * KB
        u = self.av(UB, [16, UW]); r_u = [self.ares(UB + cc * UW * 2, UW * 2) for cc in range(16)]
        VB = 34 * KB
        v = self.av(VB, [16, 512], F32); r_v = [self.ares(VB + cc * 2 * KB, 2 * KB) for cc in range(16)]
        hs = hn; r_hs = r_hn
        self.norm(T, segs, 1, 0, hn, r_hn)
        self.CP("pool", u[:, :, uo[0]:uo[0] + 30], self.uctx[:], [self.r_uctx], sum(r_u, []))
        if kind == "halo":
            for s in range(4):
                self.load_ctx_fm(I["st_cconv"][s], 30, u, r_u, uo[1 + s])
        st = {}

        def evac_w1(ci, ps, rp):
            cc, h = ci // 2, ci % 2
            if h == 0:
                st["a"] = (ps, rp)
            else:
                pa, rpa = st["a"]
                t, rt = self.gtmp()
                self.ACT(t[:, 0:T], ps[:, 0:T], AF.Sigmoid, [rp, self.r_cst], [rt], bias=c[:, self.C_B1 + 2 * cc + 1: self.C_B1 + 2 * cc + 2])
                for si, (c0, n, s) in enumerate(segs):
                    self.STT(u[:, cc, uo[si] + 30: uo[si] + 30 + n], pa[:, c0:c0 + n], c[:, self.C_B1 + 2 * cc: self.C_B1 + 2 * cc + 1],
                             t[:, c0:c0 + n], ALU.add, ALU.mult, [rpa, rt, self.r_cst], r_u[cc])
                    if si == 0:
                        self.STT(self.uctx[:, cc, :], pa[:, c0 + n - 30:c0 + n], c[:, self.C_B1 + 2 * cc: self.C_B1 + 2 * cc + 1],
                                 t[:, c0 + n - 30:c0 + n], ALU.add, ALU.mult, [rpa, rt, self.r_cst], [self.r_uctx])
                    else:
                        self.STT(tile["cconvf"][:, cc, (si - 1) * 16: si * 16], pa[:, c0:c0 + 16], c[:, self.C_B1 + 2 * cc: self.C_B1 + 2 * cc + 1],
                                 t[:, c0:c0 + 16], ALU.add, ALU.mult, [rpa, rt, self.r_cst], tile["r_cconvf"])

        self.proj_fm("w1", I["w1_ext"], 2 * D, 16, hn, r_hn, T, evac_w1)
        if P.dry:
            self.proj_fm("w2", I["conf_w2"], D, 16, hs, r_hs, T, None)
            return
        if kind == "halo":
            ranges = [(uo[0], 256, [(0, 0, 256)]), (uo[1], 3 * 46 + 16, None)]
        else:
            ranges = [(uo[0], segs[0][1], [(0, 0, segs[0][1])])]
        for cc in range(16):
            pss = [self.psum() for _ in ranges]
            for j in range(31):
                dg, rdg = self.gdg()
                self.TS("dve", dg, self.identb[:], c[:, self.C_DWW + j * 16 + cc: self.C_DWW + j * 16 + cc + 1], None, ALU.mult, None,
                        [self.r_identb, self.r_cst], [rdg])
                for (o, n, _), (ps, rp) in zip(ranges, pss):
                    self.MM(ps[:, 0:n], dg, u[:, cc, o + j:o + j + n], j == 0, j == 30, [rdg] + r_u[cc], [rp])
            bias = c[:, self.C_DWB + cc: self.C_DWB + cc + 1]
            ps, rp = pss[0]
            n0 = ranges[0][1]
            self.ACT(v[:, cc, 0:n0], ps[:, 0:n0], AF.Identity, [rp, self.r_cst], r_v[cc], bias=bias)
            if kind == "halo":
                ps, rp = pss[1]
                self.ACT(v[:, cc, 256:320].rearrange("p (s t) -> p s t", s=4),
                         ps[:, 0:184].rearrange("p (s w) -> p s w", w=46)[:, :, 0:16], AF.Identity, [rp, self.r_cst], r_v[cc], bias=bias)
        ps1, rp1 = self.psum()
        ps2, rp2 = self.psum()
        for cc in range(16):
            a, ra = self.gsqb()
            self.CP("act", a[:, 0:T], v[:, cc, 0:T], r_v[cc], [ra])
            self.MM(ps1[:, 0:T], self.onesb[:], a[:, 0:T], cc == 0, cc == 15, [self.r_onesb, ra], [rp1])
            b, rb = self.gsqb()
            self.ACT(b[:, 0:T], v[:, cc, 0:T], AF.Square, r_v[cc], [rb])
            self.MM(ps2[:, 0:T], self.onesb[:], b[:, 0:T], cc == 0, cc == 15, [self.r_onesb, rb], [rp2])
        mean, rmean = self.gtmp()
        self.TS("dve", mean[:, 0:T], ps1[:, 0:T], 1.0 / D, None, ALU.mult, None, [rp1], [rmean])
        msq, rmsq = self.gtmp()
        self.TT("dve", msq[:, 0:T], mean[:, 0:T], mean[:, 0:T], ALU.mult, [rmean], [rmsq])
        self.STT(msq[:, 0:T], ps2[:, 0:T], 1.0 / D, msq[:, 0:T], ALU.mult, ALU.subtract, [rp2, rmsq], [rmsq])
        self.ACT(msq[:, 0:T], msq[:, 0:T], AF.Sqrt, [rmsq], [rmsq], bias=EPS)
        self.RECIP(msq[:, 0:T], msq[:, 0:T], [rmsq], [rmsq])
        for cc in range(16):
            self.TT("dve", v[:, cc, 0:T], v[:, cc, 0:T], mean[:, 0:T], ALU.subtract, r_v[cc] + [rmean], r_v[cc])
            self.TT("pool", v[:, cc, 0:T], v[:, cc, 0:T], msq[:, 0:T], ALU.mult, r_v[cc] + [rmsq], r_v[cc])
            self.ACT(hs[:, cc, 0:T], v[:, cc, 0:T], AF.Silu, r_v[cc] + [self.r_cst], [r_hs[cc]],
                     scale=c[:, self.C_LNG + cc:self.C_LNG + cc + 1], bias=c[:, self.C_LNB + cc:self.C_LNB + cc + 1])

        def evac_w2(ci, ps, rp):
            for (c0, n, s) in segs:
                self.STT(self.xres[:, ci, c0:c0 + n], ps[:, c0:c0 + n], self.m_gt(1, 0, ci, s), self.xres[:, ci, c0:c0 + n],
                         ALU.mult, ALU.add, [rp, self.r_modT, self.r_xres[ci]], [self.r_xres[ci]])
                self.TS("dve", self.xres[:, ci, c0:c0 + n], self.xres[:, ci, c0:c0 + n], self.b2g[:, ci, s:s + 1], None, ALU.add, None,
                        [self.r_xres[ci], self.r_b2g], [self.r_xres[ci]])
            self.stat_chunk(ci, T)

        self.proj_fm("w2", I["conf_w2"], D, 16, hs, r_hs, T, evac_w2)

    def out_y(self, tile):
        P = self.P
        T, segs = tile["T"], tile["segs"]
        c = self.cst
        self.norm(T, segs, 0, 0, None, None, final=True)
        for (dst, c0, n) in tile["yout"]:
            pass
        nb = len(tile["yout"])
        stg = [self.av(i * 8192, [2048], F32) for i in range(4)]
        r_stg = [self.ares(i * 8192, 8192) for i in range(4)]
        for g in range(4):
            ts = []
            for q in range(4):
                cc = g * 4 + q
                t, rt = self.gtmp()
                self.STT(t[:, 0:T], self.xres[:, cc, 0:T], c[:, self.C_GFIN + cc:self.C_GFIN + cc + 1], self.rstd[:, 0:T], ALU.mult, ALU.mult,
                         [self.r_xres[cc], self.r_cst, self.r_rstd], [rt])
                ts.append((t, rt))
            for bi, (dst, c0, n) in enumerate(tile["yout"]):
                ps, rp = self.psum()
                for q in range(4):
                    t, rt = ts[q]
                    self.TR(ps[0:n, q * 128:(q + 1) * 128], t[:, c0:c0 + n], self.ident[:], [rt, self.r_ident], [rp])
                self.CP("act" if bi % 2 else "dve", stg[bi % 4][0:n, g * 512:(g + 1) * 512], ps[0:n, :], [rp], r_stg[bi % 4])
        for bi, (dst, c0, n) in enumerate(tile["yout"]):
            P.dma("sp", dst, stg[bi % 4][0:n, :], reads=r_stg[bi % 4])

    def out_fm_rows(self, src_fm, rd, nrows, dst):
        stg = self.av(32 * 1024, [2048], F32)
        r_stg = self.ares(32 * 1024, 8192)
        for g in range(4):
            ps, rp = self.psum()
            for q in range(4):
                cc = g * 4 + q
                self.TR(ps[0:nrows, q * 128:(q + 1) * 128], src_fm[:, cc, :], self.ident[:], rd + [self.r_ident], [rp])
            self.CP("dve", stg[0:nrows, g * 512:(g + 1) * 512], ps[0:nrows, :], [rp], r_stg)
        self.P.dma("sp", dst, stg[0:nrows, :], reads=r_stg)

    def out_state(self, hT, r_hT, dst):
        dv = dst.rearrange("(a p) n -> p a n", p=128)
        for g in range(2):
            stg, r_stg = self.gtmp()
            ps, rp = self.psum()
            for q in range(4):
                self.TR(ps[:, q * 128:(q + 1) * 128], hT[:, (g * 4 + q) * 128:(g * 4 + q + 1) * 128], self.ident[:], r_hT + [self.r_ident], [rp])
            self.CP("dve", stg[:, :], ps[:, :], [rp], [r_stg])
            self.P.dma("sp", dv[:, g * 4:(g + 1) * 4, :], stg[:, :].rearrange("p (a n) -> p a n", a=4), reads=[r_stg])

    def emit(self, P):
        self.P = P
        I, O = self.I, self.O
        self.ps_i = self.tmp_i = self.sqb_i = self.sm_i = self.dg_i = 0
        self.stat_pend = []
        self.stat_n = 0
        self.pt_i = 0
        if P.dry:
            self.wsched = []
        else:
            self.w_issued = self.w_consumed = 0
        self.emit_setup()
        KB = 1024
        KSTOP = int(os.environ.get("KSTOP", "9"))
        if KSTOP < 1:
            if not P.dry:
                P.finish()
            return
        nb = self.npre
        b0 = 0
        while b0 < nb:
            nblk = min(4, nb - b0)
            T = nblk * 128
            tile = dict(kind="pre", T=T, segs=[(0, T, 0)], xpw=3 + T, segoff=[0],
                        blocks=[(i * 128, 128, dict(seg=0)) for i in range(nblk)])
            self.load_x([(I["x_pre"][(b0 + i) * 128:(b0 + i + 1) * 128, :], i * 128, 128) for i in range(nblk)])
            self.l0_mixer(tile)
            b0 += nblk
            if int(os.environ.get("KSUB", "99")) < 99:
                break
        if KSTOP < 2:
            if not P.dry:
                P.finish()
            return
        T = 320
        segs = [(0, 256, 0)] + [(256 + 16 * s, 16, 1 + s) for s in range(4)]
        blocks = [(0, 128, dict(seg=0)), (128, 128, dict(seg=0))] + [(256 + 16 * s, 16, dict(seg=1 + s)) for s in range(4)]
        S = self.S
        tile = dict(kind="halo", T=T, segs=segs, blocks=blocks, rc0=0,
                    xpw=3 + 256 + 4 * 19, segoff=[0] + [259 + 19 * s for s in range(4)],
                    uw=30 + 256 + 4 * 46, uoff=[0] + [286 + 46 * s for s in range(4)])
        if not hasattr(self, "halo_bufs"):
            hb = self.halo_bufs = {}
            hb["Kown"] = S("Kown", [128, 4, 64], BF16)
            hb["Ksmp"] = S("Ksmp", [128, 4, 144], BF16)
            hb["Vcache"] = S("Vcache", [128, 256], BF16)
            hb["Vsmp"] = S("Vsmp", [16, 4, 256], BF16)
            hb["hTs"] = S("hTs", [128, 1024])
            hb["hTsb"] = S("hTsb", [128, 1024], BF16)
            hb["sconvf"] = S("sconvf", [128, 4, 16, 3])
            hb["cconvf"] = S("cconvf", [128, 16, 64])
            for k in list(hb.keys()):
                hb["r_" + k] = [Res(k)]
            hb["kso"] = self.av(53 * KB, [4, 256], F32); hb["r_kso"] = self.ares(53 * KB, 4096)
            hb["vso"] = self.av(57 * KB, [4, 256], F32); hb["r_vso"] = self.ares(57 * KB, 4096)
            hb["kpo"] = self.av(61 * KB, [256], F32); hb["r_kpo"] = self.ares(61 * KB, 1024)
            hb["vpo"] = self.av(62 * KB, [256], F32); hb["r_vpo"] = self.ares(62 * KB, 1024)
            print("SBUF bytes/partition:", self.sb_bytes)
        tile.update(self.halo_bufs)
        hb = self.halo_bufs
        self.load_x([(I["x_main"][0:128, :], 0, 128), (I["x_main"][128:256, :], 128, 128), (I["x_smp"], 256, 64)])
        self.l0_mixer(tile)
        self.mlp(tile, 0)
        self.l1_conf(tile)
        self.mlp(tile, 1)
        tile["yout"] = [(O["y_smp"], 256, 64)]
        self.out_y(tile)
        if not P.dry:
            for sq in range(4):
                self.out_fm_rows(hb["sconvf"][:, sq, :, :], hb["r_sconvf"], 3, O["sconv_s"][sq])
                P.dma("sp", O["cconv_s"][sq, 0:14, :], I["st_cconv"][sq, 16:30, :])
                self.out_fm_rows(hb["cconvf"][:, :, sq * 16:(sq + 1) * 16], hb["r_cconvf"], 16, O["cconv_s"][sq, 14:30, :])
            f = self.flags[:, 0:1]
            self.TS("dve", self.hT[:], self.hT[:], f, None, ALU.mult, None, [self.r_hT, self.r_flags], [self.r_hT])
            self.TS("dve", self.hTb[:], self.hTb[:], f, None, ALU.mult, None, [self.r_hTb, self.r_flags], [self.r_hTb])
            self.TS("dve", self.xpctx[:], self.xpctx[:], f, None, ALU.mult, None, [self.r_xpctx, self.r_flags], [self.r_xpctx])
            self.TS("dve", self.uctx[:], self.uctx[:], f, None, ALU.mult, None, [self.r_uctx, self.r_flags], [self.r_uctx])
        if KSTOP < 3:
            if not P.dry:
                P.finish()
            return
        for ti in range(self.nmain):
            T = 512
            tile = dict(kind="main", T=T, segs=[(0, 512, 0)], blocks=[(i * 128, 128, dict(seg=0)) for i in range(4)],
                        rc0=320 + 512 * ti, xpw=515, segoff=[0], uw=542, uoff=[0], first_main=(ti == 0), last=(ti == self.nmain - 1))
            tile.update(self.halo_bufs)
            base = HALO + ti * 512
            self.load_x([(I["x_main"][base + i * 128: base + (i + 1) * 128, :], i * 128, 128) for i in range(4)])
            self.l0_mixer(tile)
            self.mlp(tile, 0)
            self.l1_conf(tile)
            self.mlp(tile, 1)
            tile["yout"] = [(O["y_main"][ti * 512 + i * 128: ti * 512 + (i + 1) * 128, :], i * 128, 128) for i in range(4)]
            self.out_y(tile)
        if not P.dry:
            self.out_state(self.hT[:], [self.r_hT], O["hst_p"])
            self.out_fm_rows(self.xpctxf[:, :, :], [self.r_xpctxf], 3, O["sconv_p"])
            self.out_fm_rows(self.uctx[:, :, :], [self.r_uctx], 30, O["cconv_p"])
            P.finish()


def build_program(npre, nmain, dbg=()):
    nc = bass.Bass("TRN2", target_bir_lowering=False)
    st = ExitStack()
    K = Kern(nc, st, npre, nmain, dbg)
    dry = Prog(dry=True)
    K.emit(dry)
    P = Prog()
    K.emit(P)
    P.build(nc, st)
    st.close()
    return nc, K, P


def _prep_weights(inp):
    w_in = np.asarray(inp["w_in"][0], np.float32)
    q = w_in[:, 0:1024]
    k = w_in[:, 1024:1280]
    sw = np.concatenate([np.arange(32, 64), np.arange(0, 32)])
    cols = []
    for c in range(8):
        qc = q[:, c * 128:(c + 1) * 128]
        qs = np.concatenate([qc[:, 0:64][:, sw], qc[:, 64:128][:, sw]], axis=1)
        cols += [qc, qs]
    for j in range(4):
        kj = k[:, j * 64:(j + 1) * 64]
        cols += [kj, kj, kj[:, sw], kj[:, sw]]
    cols += [w_in[:, 1280:1536], w_in[:, 1536:2560], w_in[:, 2560:4608], w_in[:, 4608:4624]]
    w_in_ext = np.ascontiguousarray(np.concatenate(cols, axis=1))
    assert w_in_ext.shape[1] == WEXT
    w1 = np.asarray(inp["conf_w1"][0], np.float32)
    b1 = np.asarray(inp["conf_b1"][0], np.float32)
    c1, bb = [], []
    for c in range(16):
        c1 += [w1[:, c * 128:(c + 1) * 128], w1[:, 2048 + c * 128: 2048 + (c + 1) * 128]]
        bb += [b1[c * 128:(c + 1) * 128], b1[2048 + c * 128: 2048 + (c + 1) * 128]]
    w1_ext = np.ascontiguousarray(np.concatenate(c1, axis=1))
    b1_ext = np.ascontiguousarray(np.concatenate(bb))[None, :]
    f = lambda a: np.ascontiguousarray(np.asarray(a, np.float32))
    W = dict(
        w_mod=f(inp["w_mod"]), b_mod=f(inp["b_mod"]), g_mix=f(inp["g_mix"]), g_mlp=f(inp["g_mlp"]),
        w_in_ext=w_in_ext, w_out=f(inp["w_out"][0]), attn_sinks=f(inp["attn_sinks"]), ssm_a_log=f(inp["ssm_a_log"]),
        ssm_dt_bias=f(inp["ssm_dt_bias"]), ssm_d=f(inp["ssm_d"]), ssm_conv_w=f(inp["ssm_conv_w"][0]),
        ssm_conv_b=f(inp["ssm_conv_b"]), ssm_norm_g=f(inp["ssm_norm_g"]), w1_ext=w1_ext, b1_ext=b1_ext,
        conf_dw_w=f(inp["conf_dw_w"][0]), conf_dw_b=f(inp["conf_dw_b"]), conf_ln_g=f(inp["conf_ln_g"]),
        conf_ln_b=f(inp["conf_ln_b"]), conf_w2=f(inp["conf_w2"][0]), conf_b2=f(inp["conf_b2"]),
        mlp_w_up=f(inp["mlp_w_up"]), mlp_w_down=f(inp["mlp_w_down"]), g_final=f(inp["g_final"])[None, :],
    )
    return W


def _rope_tables(pos):
    p = np.arange(128)
    d = (p % 64) % 32
    inv = 10000.0 ** (-d.astype(np.float64) / 32.0)
    ang = inv[:, None] * pos[None, :].astype(np.float64)
    ang32 = (pos[None, :].astype(np.float32) * (10000.0 ** (-(d.astype(np.float32)) / 32.0)).astype(np.float32)[:, None]).astype(np.float32)
    cos = np.cos(ang32.astype(np.float64)).astype(np.float32)
    sin = np.sin(ang32.astype(np.float64)).astype(np.float32)
    sign = np.where((p % 64) < 32, -1.0, 1.0).astype(np.float32)[:, None]
    return np.ascontiguousarray(cos), np.ascontiguousarray(sin * sign)


def run(inp, seq=SEQ, dbg=(), trace=False):
    half = seq // 2
    nmain = half // 512
    npre = (half - HALO) // 128
    nc, K, P = build_program(npre, nmain, dbg)
    W = _prep_weights(inp)
    xp = np.asarray(inp["x_prompt"], np.float32)
    xs = np.asarray(inp["x_sample"], np.float32)
    in_maps = []
    for core in range(NCORES):
        b, hf = core // 2, core % 2
        m = dict(W)
        if hf == 1:
            m["x_pre"] = np.ascontiguousarray(xp[b, 0:max(npre, 1) * 128])
            m["x_main"] = np.ascontiguousarray(xp[b, half - HALO: seq])
            pos0 = half - HALO
            flags = np.tile(np.array([[1.0, 0.0]], np.float32), (128, 1))
        else:
            m["x_pre"] = np.zeros((max(npre, 1) * 128, D), np.float32)
            m["x_main"] = np.ascontiguousarray(np.concatenate([np.zeros((HALO, D), np.float32), xp[b, 0:half]], axis=0))
            pos0 = -HALO
            flags = np.tile(np.array([[0.0, NEG]], np.float32), (128, 1))
        sl = slice(core * 4, core * 4 + 4)
        m["x_smp"] = np.ascontiguousarray(xs[sl].reshape(64, D))
        m["c5"] = np.ascontiguousarray(np.concatenate([np.asarray(inp["c_prompt"], np.float32)[b:b + 1],
                                                       np.asarray(inp["c_sample"], np.float32)[sl]], axis=0))
        ck = np.asarray(inp["cache_swa_k"], np.float32)[0, sl]
        m["ck"] = np.ascontiguousarray(np.stack([ck, ck], axis=3).reshape(4, 128, 512))
        m["cv"] = np.ascontiguousarray(np.asarray(inp["cache_swa_v"], np.float32)[0, sl].reshape(4, 128, 256))
        m["st_ssm"] = np.ascontiguousarray(np.asarray(inp["state_ssm"], np.float32)[0, sl].reshape(4, 1024, 128))
        m["st_sconv"] = np.ascontiguousarray(np.asarray(inp["state_ssm_conv"], np.float32)[0, sl])
        m["st_cconv"] = np.ascontiguousarray(np.asarray(inp["state_conf_conv"], np.float32)[0, sl])
        pos = np.concatenate([pos0 + np.arange(HALO), np.tile(PAST_LEN + np.arange(16), 4),
                              pos0 + HALO + np.arange(half)]).astype(np.float64)
        m["rope_cos"], m["rope_sin"] = _rope_tables(pos)
        m["flags"] = flags
        in_maps.append(m)
    res = run_bass_kernel_spmd(nc, in_maps, core_ids=list(range(NCORES)), **({"trace": True} if trace else {}))
    R = res.results
    B = xp.shape[0]
    y_prompt = np.zeros((B, seq, D), np.float32)
    for core in range(NCORES):
        b, hf = core // 2, core % 2
        y_prompt[b, hf * half:(hf + 1) * half] = R[core]["y_main"]
    y_sample = np.concatenate([R[c]["y_smp"].reshape(4, 16, D) for c in range(NCORES)], axis=0)
    last = [R[2 * b + 1] for b in range(B)]
    swa_k_p = np.stack([r["k_p"].reshape(128, 4, 64) for r in last])[None]
    swa_v_p = np.stack([r["v_p"].reshape(128, 4, 64) for r in last])[None]
    swa_k_s = np.concatenate([R[c]["k_s"].reshape(4, 16, 4, 64) for c in range(NCORES)], axis=0)[None]
    swa_v_s = np.concatenate([R[c]["v_s"].reshape(4, 16, 4, 64) for c in range(NCORES)], axis=0)[None]
    st_p = np.stack([r["hst_p"].reshape(16, 64, 128) for r in last])[None]
    st_s = np.concatenate([R[c]["hst_s"].reshape(4, 16, 64, 128) for c in range(NCORES)], axis=0)[None]
    sc_p = np.stack([r["sconv_p"] for r in last])[None]
    sc_s = np.concatenate([R[c]["sconv_s"] for c in range(NCORES)], axis=0)[None]
    cc_p = np.stack([r["cconv_p"] for r in last])[None]
    cc_s = np.concatenate([R[c]["cconv_s"] for c in range(NCORES)], axis=0)[None]
    outs = (y_prompt, y_sample, swa_k_p, swa_v_p, swa_k_s, swa_v_s, st_p, st_s, sc_p, sc_s, cc_p, cc_s)
    outs = tuple(np.ascontiguousarray(o.astype(np.float32)) for o in outs)
    return outs, res, R


def kernel(**inputs):
    outs, _, _ = run(inputs)
    return outs
```

```python
import math, os, sys
from contextlib import ExitStack
KDEBUG = bool(os.environ.get("KDEBUG"))
import numpy as np
import concourse.bass as bass
import concourse.mybir as mybir
from concourse.bass_utils import run_bass_kernel_spmd

F32 = mybir.dt.float32
BF16 = mybir.dt.bfloat16
ALU = mybir.AluOpType
AF = mybir.ActivationFunctionType

D = 2048
KC = 16
DFF = 8192
EPS = 1e-6
SEQ = 8192
NCORES = 8
HALO = 256
PAST_LEN = 1024
WEXT = 3072 + 256 + 1024 + 2048 + 16
QK0, V0, Z0, XBC0, DT0 = 0, 3072, 3328, 4352, 6400
NEG = -30000.0


class Res:
    __slots__ = ("name", "w", "rs", "excl")

    def __init__(self, name="", excl=False):
        self.name = name
        self.w = None
        self.rs = []
        self.excl = excl


class Engine:
    def __init__(self, name):
        self.name = name
        self.ops = []
        self.count = 0
        self.known = {}
        self.is_pe = name == "pe"


class Prog:
    def __init__(self, dry=False):
        self.dry = dry
        self.sems = {}
        self.sem_names = []
        self.E = {n: Engine(n) for n in ("pe", "act", "dve", "pool", "sp")}
        for n in self.E:
            self.sem_names.append("eng_" + n)
        self.dma_pools = {}
        for q in ("sp", "pool"):
            keys = [f"dq_{q}_{i}" for i in range(8)]
            self.sem_names += keys
            self.dma_pools[q] = {"keys": keys, "cnt": [0] * len(keys), "i": 0}
        self.n_instr = 0

    def _need(self, eng, ev, same_ok=False):
        if ev is None:
            return
        key, val, ename = ev
        if same_ok and ename == eng.name:
            return
        if eng.known.get(key, 0) >= val:
            return
        eng.known[key] = val
        eng.ops.append(lambda e, key=key, val=val: e.wait_ge(self.sems[key], val))

    def _deps(self, eng, reads, writes):
        for r in reads:
            self._need(eng, r.w, same_ok=False)
            if r.excl:
                for ev in r.rs:
                    self._need(eng, ev, same_ok=True)
        for w in writes:
            self._need(eng, w.w, same_ok=eng.is_pe)
            for ev in w.rs:
                self._need(eng, ev, same_ok=(eng.name != "pool"))

    def _commit(self, ev, reads, writes):
        for r in reads:
            r.rs.append(ev)
            if len(r.rs) > 48:
                best = {}
                for e in r.rs:
                    if e[0] not in best or best[e[0]][1] < e[1]:
                        best[e[0]] = e
                r.rs = list(best.values())
        for w in writes:
            w.w = ev
            w.rs = []

    def op(self, engname, fn, reads=(), writes=()):
        if self.dry:
            return None
        if KDEBUG:
            fr = sys._getframe(2)
            lab = f"{fr.f_code.co_name}:{fr.f_lineno}"
            fn0 = fn
            fn = lambda e, fn0=fn0, lab=lab: fn0(e).annotate(lab)
        eng = self.E[engname]
        self._deps(eng, reads, writes)
        eng.count += 1
        key = "eng_" + engname
        ev = (key, eng.count, engname)
        eng.ops.append(lambda e, fn=fn, key=key: fn(e).then_inc(self.sems[key], 1))
        self._commit(ev, reads, writes)
        self.n_instr += 1
        return ev

    def dma(self, q, out_ap, in_ap, reads=(), writes=(), **kw):
        if self.dry:
            return None
        eng = self.E[q]
        pool = self.dma_pools[q]
        i = pool["i"]
        pool["i"] = (i + 1) % len(pool["keys"])
        key = pool["keys"][i]
        prev = pool["cnt"][i]
        pool["cnt"][i] = prev + 16
        if prev > 0:
            self._need(eng, (key, prev, "dma"))
        self._deps(eng, reads, writes)
        ev = (key, prev + 16, "dma")
        eng.ops.append(lambda e, key=key, o=out_ap, i_=in_ap, kw=kw:
                       e.dma_start(out=o, in_=i_, **kw).then_inc(self.sems[key], 16))
        self._commit(ev, reads, writes)
        self.n_instr += 1
        return ev

    def finish(self):
        for q, pool in self.dma_pools.items():
            for key, cnt in zip(pool["keys"], pool["cnt"]):
                if cnt:
                    self._need(self.E["sp"], (key, cnt, "dma"))

    def build(self, nc, stack):
        for key in self.sem_names:
            self.sems[key] = stack.enter_context(nc.semaphore(key))
        block = stack.enter_context(nc.Block())
        E = self.E

        @block.tensor
        def _(e):
            for f in E["pe"].ops:
                f(e)

        @block.scalar
        def _(e):
            for f in E["act"].ops:
                f(e)

        @block.vector
        def _(e):
            for f in E["dve"].ops:
                f(e)

        @block.gpsimd
        def _(e):
            for f in E["pool"].ops:
                f(e)

        @block.sync
        def _(e):
            for f in E["sp"].ops:
                f(e)


class Kern:
    def __init__(self, nc, st, npre, nmain, dbg=()):
        self.nc, self.st = nc, st
        self.npre, self.nmain = npre, nmain
        self.dbg = set(dbg)
        self.dbg_out = {}
        self.ncols = 320 + 512 * nmain
        self._decl()
        self._alloc()

    def _decl(self):
        nc = self.nc
        di = lambda n, s: nc.dram_tensor(n, list(s), F32, kind="ExternalInput").ap()
        do = lambda n, s: nc.dram_tensor(n, list(s), F32, kind="ExternalOutput").ap()
        I = self.I = {}
        O = self.O = {}
        I["x_pre"] = di("x_pre", (max(self.npre, 1) * 128, D))
        I["x_main"] = di("x_main", (HALO + 512 * self.nmain, D))
        I["x_smp"] = di("x_smp", (64, D))
        I["c5"] = di("c5", (5, D))
        I["ck"] = di("ck", (4, 128, 512))
        I["cv"] = di("cv", (4, 128, 256))
        I["st_ssm"] = di("st_ssm", (4, 1024, 128))
        I["st_sconv"] = di("st_sconv", (4, 3, D))
        I["st_cconv"] = di("st_cconv", (4, 30, D))
        I["rope_cos"] = di("rope_cos", (128, self.ncols))
        I["rope_sin"] = di("rope_sin", (128, self.ncols))
        I["flags"] = di("flags", (128, 2))
        I["w_mod"] = di("w_mod", (2, D, 6 * D))
        I["b_mod"] = di("b_mod", (2, 6 * D))
        I["g_mix"] = di("g_mix", (2, D))
        I["g_mlp"] = di("g_mlp", (2, D))
        I["w_in_ext"] = di("w_in_ext", (D, WEXT))
        I["w_out"] = di("w_out", (D, D))
        I["attn_sinks"] = di("attn_sinks", (1, 16))
        I["ssm_a_log"] = di("ssm_a_log", (1, 16))
        I["ssm_dt_bias"] = di("ssm_dt_bias", (1, 16))
        I["ssm_d"] = di("ssm_d", (1, 16))
        I["ssm_conv_w"] = di("ssm_conv_w", (4, D))
        I["ssm_conv_b"] = di("ssm_conv_b", (1, D))
        I["ssm_norm_g"] = di("ssm_norm_g", (1, 1024))
        I["w1_ext"] = di("w1_ext", (D, 2 * D))
        I["b1_ext"] = di("b1_ext", (1, 2 * D))
        I["conf_dw_w"] = di("conf_dw_w", (31, D))
        I["conf_dw_b"] = di("conf_dw_b", (1, D))
        I["conf_ln_g"] = di("conf_ln_g", (1, D))
        I["conf_ln_b"] = di("conf_ln_b", (1, D))
        I["conf_w2"] = di("conf_w2", (D, D))
        I["conf_b2"] = di("conf_b2", (1, D))
        I["mlp_w_up"] = di("mlp_w_up", (2, D, DFF))
        I["mlp_w_down"] = di("mlp_w_down", (2, DFF, D))
        I["g_final"] = di("g_final", (1, D))
        O["y_main"] = do("y_main", (512 * self.nmain, D))
        O["y_smp"] = do("y_smp", (64, D))
        O["k_p"] = do("k_p", (128, 256))
        O["v_p"] = do("v_p", (128, 256))
        O["k_s"] = do("k_s", (64, 256))
        O["v_s"] = do("v_s", (64, 256))
        O["hst_p"] = do("hst_p", (1024, 128))
        O["hst_s"] = do("hst_s", (4, 1024, 128))
        O["sconv_p"] = do("sconv_p", (3, D))
        O["sconv_s"] = do("sconv_s", (4, 3, D))
        O["cconv_p"] = do("cconv_p", (30, D))
        O["cconv_s"] = do("cconv_s", (4, 30, D))
        self.scr = {}
        self.scr_res = {}

    def _scratch(self, name, nunits):
        if name not in self.scr:
            self.scr[name] = self.nc.dram_tensor("scr_" + name, [nunits, 128, 4096], BF16, kind="Internal").ap()
        return self.scr[name]

    def _alloc(self):
        nc, st = self.nc, self.st
        self.sb_bytes = 0

        def S(name, shape, dt=F32):
            n = 1
            for s in shape[1:]:
                n *= s
            self.sb_bytes += n * (4 if dt == F32 else 2)
            return st.enter_context(nc.sbuf_tensor("s_" + name, list(shape), dt))

        self.S = S
        self.xres = S("xres", [128, 16, 512])
        self.r_xres = [Res(f"xres{c}") for c in range(16)]
        self.NW = 3
        self.wsl = [S(f"wsl{i}", [128, 4096], BF16) for i in range(self.NW)]
        self.r_wsl = [Res(f"wsl{i}") for i in range(self.NW)]
        self.AR = 72 * 1024
        self.arena = S("arena", [128, self.AR // 2], BF16)
        self.r_ar = [Res(f"ar{i}") for i in range(self.AR // 1024)]
        self.ident = S("ident", [128, 128]); self.r_ident = Res("ident")
        self.identb = S("identb", [128, 128], BF16); self.r_identb = Res("identb")
        self.tri = S("tri", [128, 128]); self.r_tri = Res("tri")
        self.onesf = S("onesf", [128, 128]); self.r_onesf = Res("onesf")
        self.onesb = S("onesb", [128, 128], BF16); self.r_onesb = Res("onesb")
        self.diag = S("diag", [128, 64, 128], BF16); self.r_diag = Res("diag")
        self.modT = S("modT", [128, 2, 96, 5]); self.r_modT = Res("modT")
        self.gs = S("gs", [128, 2, 2, 16, 5]); self.r_gs = Res("gs")
        self.b2g = S("b2g", [128, 16, 5]); self.r_b2g = Res("b2g")
        self.cst = S("cst", [128, 800]); self.r_cst = Res("cst")
        self.C_GMIX, self.C_GMLP, self.C_GFIN, self.C_CB, self.C_B1 = 0, 32, 64, 80, 96
        self.C_DWB, self.C_LNG, self.C_LNB, self.C_B2, self.C_NG, self.C_CW, self.C_DWW = 128, 144, 160, 176, 192, 200, 264
        self.rowc = S("rowc", [128, 5, 16]); self.r_rowc = Res("rowc")
        self.cbrow = S("cbrow", [1, 1536], BF16); self.r_cbrow = Res("cbrow")
        self.flags = S("flags", [128, 2]); self.r_flags = Res("flags")
        self.scT = S("scT", [128, 16, 5], BF16); self.r_scT = Res("scT")
        self.dgb = S("dgb", [128, 12, 128], BF16); self.r_dgb = [Res(f"dgb{i}") for i in range(12)]
        self.dg_i = 0
        self.Kbuf = S("Kbuf", [128, 4, 640], BF16); self.r_K = Res("Kbuf")
        self.Vbuf = S("Vbuf", [128, 5, 256], BF16); self.r_V = Res("Vbuf")
        self.hT = S("hT", [128, 1024]); self.r_hT = Res("hT")
        self.hTb = S("hTb", [128, 1024], BF16); self.r_hTb = Res("hTb")
        self.xpctx = S("xpctx", [128, 16, 3], BF16); self.r_xpctx = Res("xpctx")
        self.xpctxf = S("xpctxf", [128, 16, 3]); self.r_xpctxf = Res("xpctxf")
        self.uctx = S("uctx", [128, 16, 30]); self.r_uctx = Res("uctx")
        self.rstd = S("rstd", [128, 512]); self.r_rstd = Res("rstd")
        self.tmp = [S(f"tmp{i}", [128, 512]) for i in range(4)]
        self.r_tmp = [Res(f"tmp{i}") for i in range(4)]
        self.tmp_i = 0
        self.sqb = [S(f"sqb{i}", [128, 512], BF16) for i in range(4)]
        self.r_sqb = [Res(f"sqb{i}") for i in range(4)]
        self.sqb_i = 0
        self.sm = S("sm", [128, 4, 64]); self.r_sm = [Res(f"sm{i}") for i in range(4)]
        self.sm_i = 0
        self.ps = [st.enter_context(nc.psum_tensor(f"ps{i}", [128, 512], F32)) for i in range(8)]
        self.r_ps = [Res(f"ps{i}", excl=True) for i in range(8)]
        self.ps_i = 0

    def psum(self):
        i = self.ps_i
        self.ps_i = (i + 1) % 6
        return self.ps[i], self.r_ps[i]

    def gtmp(self):
        i = self.tmp_i
        self.tmp_i = (i + 1) % 4
        return self.tmp[i], self.r_tmp[i]

    def gsqb(self):
        i = self.sqb_i
        self.sqb_i = (i + 1) % 4
        return self.sqb[i], self.r_sqb[i]

    def gdg(self):
        i = self.dg_i
        self.dg_i = (i + 1) % 12
        return self.dgb[:, i, :], self.r_dgb[i]

    def gsm(self):
        i = self.sm_i
        self.sm_i = (i + 1) % 4
        return self.sm[:, i, :], self.r_sm[i]

    def av(self, off, shape, dt=BF16):
        n = 1
        for s in shape:
            n *= s
        esz = 4 if dt == F32 else 2
        assert off % 4 == 0 and off + n * esz <= self.AR, (off, shape)
        ap = self.arena[:, off // 2: off // 2 + n * esz // 2]
        if dt == F32:
            ap = ap.bitcast(F32)
        if len(shape) == 2:
            ap = ap.rearrange("p (a b) -> p a b", a=shape[0])
        elif len(shape) == 3:
            ap = ap.rearrange("p (a b c) -> p a b c", a=shape[0], b=shape[1])
        return ap

    def ares(self, off, nbytes):
        return self.r_ar[off // 1024: (off + nbytes + 1023) // 1024]

    def ACT(self, out, in_, func, rd, wr, **kw):
        self.P.op("act", lambda e: e.activation(out=out, in_=in_, func=func, **kw), rd, wr)

    def TT(self, eng, out, in0, in1, op, rd, wr):
        self.P.op(eng, lambda e: e.tensor_tensor(out=out, in0=in0, in1=in1, op=op), rd, wr)

    def TS(self, eng, out, in0, s1, s2, op0, op1, rd, wr):
        if s2 is None:
            self.P.op(eng, lambda e: e.tensor_scalar(out=out, in0=in0, scalar1=s1, scalar2=None, op0=op0), rd, wr)
        else:
            self.P.op(eng, lambda e: e.tensor_scalar(out=out, in0=in0, scalar1=s1, scalar2=s2, op0=op0, op1=op1), rd, wr)

    def STT(self, out, in0, scalar, in1, op0, op1, rd, wr):
        self.P.op("dve", lambda e: e.scalar_tensor_tensor(out=out, in0=in0, scalar=scalar, in1=in1, op0=op0, op1=op1), rd, wr)

    def CP(self, eng, out, in_, rd, wr):
        if eng == "act":
            self.P.op("act", lambda e: e.activation(out=out, in_=in_, func=AF.Identity), rd, wr)
        else:
            self.P.op(eng, lambda e: e.tensor_copy(out=out, in_=in_), rd, wr)

    def MM(self, out, lhsT, rhs, start, stop, rd, wr):
        self.P.op("pe", lambda e: e.matmul(out, lhsT=lhsT, rhs=rhs, start=start, stop=stop), rd, wr)

    def TR(self, out, in_, ident, rd, wr):
        self.P.op("pe", lambda e: e.transpose(out, in_, ident), rd, wr)

    def RECIP(self, out, in_, rd, wr):
        self.P.op("dve", lambda e: e.reciprocal(out=out, in_=in_), rd, wr)

    def MEMSET(self, eng, ap, val, wr):
        self.P.op(eng, lambda e: e.memset(ap, val), (), wr)

    def DBG(self, name, ap, rd, shape):
        if name not in self.dbg:
            return
        if name not in self.dbg_out:
            self.dbg_out[name] = self.nc.dram_tensor("dbg_" + name, list(shape), F32, kind="ExternalOutput").ap()
        self.P.dma("sp", self.dbg_out[name], ap, reads=rd)

    def wreq(self, name, uidx, nunits, src_ap, kc, ncols, once=False):
        P = self.P
        spec = (name, uidx, nunits, src_ap, kc, ncols, once)
        if P.dry:
            self.wsched.append(spec)
            return None, None
        i = self.w_consumed
        assert self.wsched[i][0] == name and self.wsched[i][1] == uidx, (self.wsched[i][:2], name, uidx)
        while self.w_issued < min(len(self.wsched), i + self.NW):
            self._wissue(self.w_issued)
            self.w_issued += 1
        self.w_consumed += 1
        s = i % self.NW
        view = self.wsl[s][:, 0:kc * ncols].rearrange("p (k n) -> p k n", k=kc)
        return view, self.r_wsl[s]

    def _wissue(self, i):
        name, uidx, nunits, src_ap, kc, ncols, once = self.wsched[i]
        s = i % self.NW
        P = self.P
        flat = self.wsl[s][:, 0:kc * ncols]
        view = flat.rearrange("p (k n) -> p k n", k=kc)
        key = (name, uidx)
        if once or key not in self.scr_res:
            P.dma("pool", view, src_ap.rearrange("(k p) n -> p k n", p=128), writes=[self.r_wsl[s]])
            if not once:
                if not hasattr(self, "w_seen"):
                    self.w_seen = set()
                if key in self.w_seen:
                    scr = self._scratch(name, nunits)
                    r = Res("scr")
                    self.scr_res[key] = r
                    P.dma("sp", scr[uidx, :, 0:kc * ncols], flat, reads=[self.r_wsl[s]], writes=[r])
                else:
                    self.w_seen.add(key)
        else:
            scr = self._scratch(name, nunits)
            P.dma("sp", flat, scr[uidx, :, 0:kc * ncols], reads=[self.scr_res[key]], writes=[self.r_wsl[s]])

    def load_vec_fm(self, src_rows_ap, nrows, dst_ap):
        t, rt = self.gtmp()
        self.P.dma("sp", t[0:nrows, 0:128], src_rows_ap, writes=[rt])
        ps, rp = self.psum()
        self.TR(ps[:, 0:nrows], t[0:nrows, 0:128], self.ident[0:nrows, 0:nrows], [rt, self.r_ident], [rp])
        self.CP("dve", dst_ap, ps[:, 0:nrows], [rp], [self.r_cst])

    def emit_setup(self):
        P, I = self.P, self.I
        self.MEMSET("pool", self.ident[:], 0.0, [self.r_ident])
        P.op("pool", lambda e: e.affine_select(out=self.ident[:], in_=self.ident[:], pattern=[[-1, 128]],
                                                compare_op=ALU.not_equal, fill=1.0, base=0, channel_multiplier=1),
             [self.r_ident], [self.r_ident])
        self.CP("dve", self.identb[:], self.ident[:], [self.r_ident], [self.r_identb])
        self.MEMSET("pool", self.tri[:], 1.0, [self.r_tri])
        P.op("pool", lambda e: e.affine_select(out=self.tri[:], in_=self.tri[:], pattern=[[1, 128]],
                                                compare_op=ALU.is_ge, fill=0.0, base=0, channel_multiplier=-1),
             [self.r_tri], [self.r_tri])
        self.MEMSET("pool", self.onesf[:], 1.0, [self.r_onesf])
        self.MEMSET("pool", self.onesb[:], 1.0, [self.r_onesb])
        self.MEMSET("pool", self.Kbuf[:], 0.0, [self.r_K])
        self.MEMSET("pool", self.Vbuf[:], 0.0, [self.r_V])
        self.MEMSET("pool", self.hT[:], 0.0, [self.r_hT])
        self.MEMSET("pool", self.hTb[:], 0.0, [self.r_hTb])
        self.MEMSET("pool", self.xpctx[:], 0.0, [self.r_xpctx])
        self.MEMSET("pool", self.xpctxf[:], 0.0, [self.r_xpctxf])
        self.MEMSET("pool", self.uctx[:], 0.0, [self.r_uctx])
        P.dma("sp", self.flags[:], I["flags"], writes=[self.r_flags])
        c = self.cst
        rows = lambda ap, n: ap.rearrange("a (r p) -> (a r) p", p=128)
        self.load_vec_fm(rows(I["g_mix"], 32), 32, c[:, self.C_GMIX:self.C_GMIX + 32])
        self.load_vec_fm(rows(I["g_mlp"], 32), 32, c[:, self.C_GMLP:self.C_GMLP + 32])
        self.load_vec_fm(rows(I["g_final"], 16), 16, c[:, self.C_GFIN:self.C_GFIN + 16])
        self.load_vec_fm(rows(I["ssm_conv_b"], 16), 16, c[:, self.C_CB:self.C_CB + 16])
        self.load_vec_fm(rows(I["b1_ext"], 32), 32, c[:, self.C_B1:self.C_B1 + 32])
        self.load_vec_fm(rows(I["conf_dw_b"], 16), 16, c[:, self.C_DWB:self.C_DWB + 16])
        self.load_vec_fm(rows(I["conf_ln_g"], 16), 16, c[:, self.C_LNG:self.C_LNG + 16])
        self.load_vec_fm(rows(I["conf_ln_b"], 16), 16, c[:, self.C_LNB:self.C_LNB + 16])
        self.load_vec_fm(rows(I["conf_b2"], 16), 16, c[:, self.C_B2:self.C_B2 + 16])
        self.load_vec_fm(rows(I["ssm_norm_g"], 8), 8, c[:, self.C_NG:self.C_NG + 8])
        self.load_vec_fm(rows(I["ssm_conv_w"], 64), 64, c[:, self.C_CW:self.C_CW + 64])
        dww = rows(I["conf_dw_w"], 496)
        for q in range(4):
            self.load_vec_fm(dww[q * 124:(q + 1) * 124, :], 124, c[:, self.C_DWW + q * 124: self.C_DWW + (q + 1) * 124])
        for i, nm in enumerate(["ssm_dt_bias", "ssm_a_log", "ssm_d", "attn_sinks"]):
            P.dma("sp", self.rowc[:, i, :], I[nm].broadcast_to([128, 16]), writes=[self.r_rowc])
        self.ACT(self.rowc[:, 1, :], self.rowc[:, 1, :], AF.Exp, [self.r_rowc], [self.r_rowc])
        self.TS("dve", self.rowc[:, 1, :], self.rowc[:, 1, :], -1.0, None, ALU.mult, None, [self.r_rowc], [self.r_rowc])
        self.ACT(self.rowc[:, 3, :], self.rowc[:, 3, :], AF.Exp, [self.r_rowc], [self.r_rowc])
        t, rt = self.gtmp()
        P.dma("sp", t[0:1, 0:512], I["ssm_conv_b"][:, 0:512], writes=[rt])
        self.CP("dve", self.cbrow[0:1, 0:512], t[0:1, 0:512], [rt], [self.r_cbrow])
        t, rt = self.gtmp()
        P.dma("sp", t[0:1, 0:512], I["ssm_conv_b"][:, 512:1024], writes=[rt])
        self.CP("dve", self.cbrow[0:1, 512:1024], t[0:1, 0:512], [rt], [self.r_cbrow])
        t, rt = self.gtmp()
        P.dma("sp", t[0:1, 0:512], I["ssm_conv_b"][:, 1024:1536], writes=[rt])
        self.CP("dve", self.cbrow[0:1, 1024:1536], t[0:1, 0:512], [rt], [self.r_cbrow])
        for j in range(4):
            for cc in range(16):
                self.TS("dve", self.diag[:, j * 16 + cc, :], self.identb[:], c[:, self.C_CW + j * 16 + cc: self.C_CW + j * 16 + cc + 1],
                        None, ALU.mult, None, [self.r_identb, self.r_cst], [self.r_diag])
        c5t = self.av(0, [2048], F32)
        r_c5 = self.ares(0, 8192)
        P.dma("sp", c5t[0:5, :], I["c5"], writes=r_c5)
        ps, rp = self.psum()
        for k in range(16):
            self.TR(ps[:, k * 5:(k + 1) * 5], c5t[0:5, k * 128:(k + 1) * 128], self.ident[0:5, 0:5], r_c5 + [self.r_ident], [rp])
        self.ACT(self.scT[:].rearrange("p k s -> p (k s)"), ps[:, 0:80], AF.Silu, [rp], [self.r_scT])
        bm = self.av(8192, [192], F32)
        r_bm = self.ares(8192, 768)
        for l in range(2):
            t, rt = self.gtmp()
            P.dma("sp", t[0:96, 0:128], I["b_mod"][l:l + 1, :].rearrange("a (r p) -> (a r) p", p=128), writes=[rt])
            ps, rp = self.psum()
            self.TR(ps[:, 0:96], t[0:96, 0:128], self.ident[0:96, 0:96], [rt, self.r_ident], [rp])
            self.CP("dve", bm[:, l * 96:(l + 1) * 96], ps[:, 0:96], [rp], r_bm)
        for l in range(2):
            for u in range(48):
                wv, rw = self.wreq("w_mod%d" % l, u, 48, I["w_mod"][l, :, u * 256:(u + 1) * 256], 16, 256, once=True)
                ps, rp = self.psum()
                if not P.dry:
                    for k in range(16):
                        self.MM(ps[0:5, 0:256], self.scT[:, k, :], wv[:, k, :], k == 0, k == 15, [self.r_scT, rw], [rp])
                    t, rt = self.gtmp()
                    self.CP("act", t[0:5, 0:256], ps[0:5, 0:256], [rp], [rt])
                    ps2, rp2 = self.psum()
                    for h in range(2):
                        self.TR(ps2[:, h * 5:(h + 1) * 5], t[0:5, h * 128:(h + 1) * 128], self.ident[0:5, 0:5], [rt, self.r_ident], [rp2])
                    for h in range(2):
                        ch = u * 2 + h
                        self.TS("dve", self.modT[:, l, ch, :], ps2[:, h * 5:(h + 1) * 5], bm[:, l * 96 + ch: l * 96 + ch + 1], None,
                                ALU.add, None, [rp2] + r_bm, [self.r_modT])
        for l in range(2):
            for cc in range(16):
                self.TS("dve", self.gs[:, l, 0, cc, :], self.modT[:, l, 16 + cc, :], 1.0, c[:, self.C_GMIX + l * 16 + cc: self.C_GMIX + l * 16 + cc + 1],
                        ALU.add, ALU.mult, [self.r_modT, self.r_cst], [self.r_gs])
                self.TS("dve", self.gs[:, l, 1, cc, :], self.modT[:, l, 64 + cc, :], 1.0, c[:, self.C_GMLP + l * 16 + cc: self.C_GMLP + l * 16 + cc + 1],
                        ALU.add, ALU.mult, [self.r_modT, self.r_cst], [self.r_gs])
        for cc in range(16):
            self.TS("dve", self.b2g[:, cc, :], self.modT[:, 1, 32 + cc, :], c[:, self.C_B2 + cc: self.C_B2 + cc + 1], None,
                    ALU.mult, None, [self.r_modT, self.r_cst], [self.r_b2g])

    def m_sh(self, l, which, cc, s):
        return self.modT[:, l, (0 if which == 0 else 48) + cc, s:s + 1]

    def m_gt(self, l, which, cc, s):
        return self.modT[:, l, (32 if which == 0 else 80) + cc, s:s + 1]

    def m_gs(self, l, which, cc, s):
        return self.gs[:, l, which, cc, s:s + 1]

    def load_x(self, blocks):
        P = self.P
        for bi, (src, c0, n) in enumerate(blocks):
            off = (bi % 2) * 8192
            xs = self.av(off, [2048], F32)
            rx = self.ares(off, 8192)
            P.dma("sp", xs[0:n, :], src, writes=rx)
            for g in range(4):
                ps, rp = self.psum()
                for q in range(4):
                    cc = g * 4 + q
                    self.TR(ps[:, q * 128: q * 128 + n], xs[0:n, cc * 128:(cc + 1) * 128], self.ident[0:n, 0:n], rx + [self.r_ident], [rp])
                for q in range(4):
                    cc = g * 4 + q
                    self.CP("act" if g % 2 else "dve", self.xres[:, cc, c0:c0 + n], ps[:, q * 128:q * 128 + n], [rp], [self.r_xres[cc]])
                    if bi == len(blocks) - 1:
                        self.stat_chunk(cc, c0 + n)

    def stat_chunk(self, cc, T):
        sq, rsq = self.gsqb()
        self.ACT(sq[:, 0:T], self.xres[:, cc, 0:T], AF.Square, [self.r_xres[cc]], [rsq])
        self.stat_pend.append((sq, rsq))
        if len(self.stat_pend) > 2:
            self._stat_mm(T)

    def _stat_mm(self, T):
        sq, rsq = self.stat_pend.pop(0)
        ps, rp = self.ps[7], self.r_ps[7]
        self.MM(ps[:, 0:T], self.onesb[:], sq[:, 0:T], self.stat_n == 0, self.stat_n == 15, [self.r_onesb, rsq], [rp])
        self.stat_n += 1

    def norm(self, T, segs, l, which, hn, r_hn, final=False):
        while self.stat_pend:
            self._stat_mm(T)
        assert self.P.dry or self.stat_n == 16, self.stat_n
        self.stat_n = 0
        ps, rp = self.ps[7], self.r_ps[7]
        self.ACT(self.rstd[:, 0:T], ps[:, 0:T], AF.Sqrt, [rp], [self.r_rstd], scale=1.0 / D, bias=EPS)
        self.RECIP(self.rstd[:, 0:T], self.rstd[:, 0:T], [self.r_rstd], [self.r_rstd])
        if final:
            return
        for cc in range(16):
            t, rt = self.gtmp()
            for (c0, n, s) in segs:
                self.STT(t[:, c0:c0 + n], self.xres[:, cc, c0:c0 + n], self.m_gs(l, which, cc, s), self.rstd[:, c0:c0 + n],
                         ALU.mult, ALU.mult, [self.r_xres[cc], self.r_gs, self.r_rstd], [rt])
                self.ACT(hn[:, cc, c0:c0 + n], t[:, c0:c0 + n], AF.Identity, [rt, self.r_modT], [r_hn[cc]],
                         bias=self.m_sh(l, which, cc, s))

    def proj_fm(self, name, src, ncols_total, kc, in_buf, r_in, T, evac, col0=0, ncols=None, kbase=0, nunits=None, ubase=0,
                unit_cols=256):
        P = self.P
        ncols = ncols_total - col0 if ncols is None else ncols
        nu = (ncols + unit_cols - 1) // unit_cols
        nunits = nu if nunits is None else nunits
        for u in range(nu):
            cw = min(unit_cols, ncols - u * unit_cols)
            sap = src[kbase * 128:(kbase + kc) * 128, col0 + u * unit_cols: col0 + u * unit_cols + cw]
            wv, rw = self.wreq(name, ubase + u, nunits, sap, kc, cw)
            if P.dry:
                continue
            for h in range(cw // 128):
                ps, rp = self.psum()
                for k in range(kc):
                    self.MM(ps[:, 0:T], wv[:, k, h * 128:(h + 1) * 128], in_buf[:, k, 0:T], k == 0, k == kc - 1,
                            [rw, r_in[k]], [rp])
                evac(u * (unit_cols // 128) + h, ps, rp)

    def l0_mixer(self, tile):
        P, I = self.P, self.I
        T, segs, blocks, kind = tile["T"], tile["segs"], tile["blocks"], tile["kind"]
        pre = kind == "pre"
        KB = 1024
        hn = self.av(0, [16, 512]); r_hn = [self.ares(c * KB, KB)[0] for c in range(16)]
        Q = self.av(16 * KB, [8, 512]); r_Q = [self.ares(16 * KB + c * KB, KB)[0] for c in range(8)]
        nslot = len(blocks)
        zb = self.av(24 * KB, [6, 1024]); r_z = [self.ares(24 * KB + b * 2048, 2048) for b in range(6)]
        XW = tile["xpw"]
        xp = self.av(36 * KB, [16, XW]); r_xp = [self.ares(36 * KB + c * XW * 2, XW * 2) for c in range(16)]
        xst = self.av(53 * KB, [6, 1024]); r_xst = [self.ares(53 * KB + b * 2048, 2048) for b in range(6)]
        Bt = self.av(65 * KB, [6, 512]); r_Bt = [self.ares(65 * KB + b * 1024, 1024) for b in range(6)]
        BCT = self.av(0, [8, 512]); r_BCT = [self.ares(c * KB, KB)[0] for c in range(8)]
        mix = self.av(36 * KB, [16, 512]); r_mix = [self.ares(36 * KB + c * KB, KB)[0] for c in range(16)]
        dtb = self.av(71 * KB, [6, 2, 16], F32); r_dtb = self.ares(71 * KB, 768)
        KSUB = int(os.environ.get("KSUB", "99"))
        if KSUB <= 1:
            return
        self.norm(T, segs, 0, 0, hn, r_hn)
        if KSUB <= 2:
            return
        ropec = self.av(63 * KB, [512], F32); ropes = self.av(65 * KB, [512], F32)
        r_rope = self.ares(63 * KB, 4096)
        if not pre:
            P.dma("sp", ropec[:, 0:T], I["rope_cos"][:, tile["rc0"]:tile["rc0"] + T], writes=r_rope)
            P.dma("sp", ropes[:, 0:T], I["rope_sin"][:, tile["rc0"]:tile["rc0"] + T], writes=r_rope)
        W = I["w_in_ext"]
        if not pre:
            kfull = {}

            def evac_qk(ci, ps, rp, st={}):
                u, h = ci // 2, ci % 2
                if h == 0:
                    t1, r1 = self.gtmp()
                    self.TT("dve", t1[:, 0:T], ps[:, 0:T], ropec[:, 0:T], ALU.mult, [rp] + r_rope, [r1])
                    st["t1"] = (t1, r1)
                else:
                    t1, r1 = st["t1"]
                    t2, r2 = self.gtmp()
                    self.TT("dve", t2[:, 0:T], ps[:, 0:T], ropes[:, 0:T], ALU.mult, [rp] + r_rope, [r2])
                    if u < 8:
                        self.TT("pool", Q[:, u, 0:T], t1[:, 0:T], t2[:, 0:T], ALU.add, [r1, r2], [r_Q[u]])
                    else:
                        j = u - 8
                        self.TT("pool", t1[:, 0:T], t1[:, 0:T], t2[:, 0:T], ALU.add, [r1, r2], [r1])
                        self.k_store(tile, j, t1, r1)

            self.proj_fm("w_in", W, WEXT, 16, hn, r_hn, T, evac_qk, col0=QK0, ncols=3072, nunits=29, ubase=0)
            wv, rw = self.wreq("w_in", 12, 29, W[:, V0:V0 + 256], 16, 256)
            if not P.dry:
                for bi, (c0, n, bk) in enumerate(blocks):
                    ps, rp = self.psum()
                    for k in range(16):
                        self.MM(ps[0:n, 0:256], hn[:, k, c0:c0 + n], wv[:, k, :], k == 0, k == 15, [r_hn[k], rw], [rp])
                    self.v_store(tile, bi, ps, rp)
        if not pre:
            for u in range(4):
                wv, rw = self.wreq("w_in", 13 + u, 29, W[:, Z0 + u * 256: Z0 + (u + 1) * 256], 16, 256)
                if P.dry:
                    continue
                for bi, (c0, n, bk) in enumerate(blocks):
                    ps, rp = self.psum()
                    for k in range(16):
                        self.MM(ps[0:n, 0:256], hn[:, k, c0:c0 + n], wv[:, k, :], k == 0, k == 15, [r_hn[k], rw], [rp])
                    self.ACT(zb[0:n, bi, u * 256:(u + 1) * 256], ps[0:n, 0:256], AF.Silu, [rp], r_z[bi])
        so = tile["segoff"]
        self.CP("pool", xp[:, :, so[0]:so[0] + 3], self.xpctx[:], [self.r_xpctx], sum(r_xp, []))

        def evac_xbc(ci, ps, rp):
            for si, (c0, n, s) in enumerate(segs):
                self.CP("act", xp[:, ci, so[si] + 3: so[si] + 3 + n], ps[:, c0:c0 + n], [rp], r_xp[ci])
                if si == 0 and not pre:
                    self.CP("dve", self.xpctxf[:, ci, :], ps[:, c0 + n - 3:c0 + n], [rp], [self.r_xpctxf])
                elif si > 0:
                    self.CP("dve", tile["sconvf"][:, si - 1, ci, :], ps[:, c0 + n - 3:c0 + n], [rp], tile["r_sconvf"])

        ncx = 1536 if pre else 2048
        self.proj_fm("w_in", W, WEXT, 16, hn, r_hn, T, evac_xbc, col0=XBC0, ncols=ncx, nunits=29, ubase=17)
        if KSUB <= 3:
            return
        wv, rw = self.wreq("w_in", 25, 29, W[:, DT0:DT0 + 16], 16, 16)
        if not P.dry:
            for bi, (c0, n, bk) in enumerate(blocks):
                ps, rp = self.psum()
                for k in range(16):
                    self.MM(ps[0:n, 0:16], hn[:, k, c0:c0 + n], wv[:, k, :], k == 0, k == 15, [r_hn[k], rw], [rp])
                self.TT("dve", dtb[0:n, bi, 0, :], ps[0:n, 0:16], self.rowc[0:n, 0, :], ALU.add, [rp, self.r_rowc], r_dtb)
                self.ACT(dtb[0:n, bi, 0, :], dtb[0:n, bi, 0, :], AF.Exp, r_dtb, r_dtb)
                self.ACT(dtb[0:n, bi, 0, :], dtb[0:n, bi, 0, :], AF.Ln, r_dtb, r_dtb, bias=1.0)
                self.TT("dve", dtb[0:n, bi, 1, :], dtb[0:n, bi, 0, :], self.rowc[0:n, 1, :], ALU.mult, r_dtb + [self.r_rowc], r_dtb)
        if P.dry:
            if not pre:
                self.proj_fm("w_out", I["w_out"], D, 16, mix, r_mix, T, None)
            return
        if KSUB <= 4:
            return
        if kind == "halo":
            P.dma("sp", self.O["k_s"].rearrange("(s t) c -> t s c", t=16), tile["kso"][0:16, :, :], reads=tile["r_kso"])
            P.dma("sp", self.O["v_s"].rearrange("(s t) c -> t s c", t=16), tile["vso"][0:16, :, :], reads=tile["r_vso"])
        if tile.get("last"):
            P.dma("sp", self.O["k_p"], tile["kpo"][:, :], reads=tile["r_kpo"])
            P.dma("sp", self.O["v_p"], tile["vpo"][:, :], reads=tile["r_vpo"])
        if kind == "halo":
            for s in range(4):
                self.load_ctx_fm(I["st_sconv"][s], 3, xp, r_xp, so[1 + s])
        c = self.cst
        for si, (c0, n, s) in enumerate(segs):
            if not pre:
                for cc in range(8, 16):
                    ps, rp = self.psum()
                    for j in range(4):
                        self.MM(ps[:, 0:n], self.diag[:, j * 16 + cc, :], xp[:, cc, so[si] + j: so[si] + j + n], j == 0, j == 3,
                                [self.r_diag] + r_xp[cc], [rp])
                    self.ACT(BCT[:, cc - 8, c0:c0 + n], ps[:, 0:n], AF.Silu, [rp, self.r_cst], [r_BCT[cc - 8]],
                             bias=c[:, self.C_CB + cc:self.C_CB + cc + 1])
        for bi, (c0, n, bk) in enumerate(blocks):
            si = bk["seg"]
            o = so[si] + (c0 - segs[si][0])
            for g3 in range(3):
                ps, rp = self.psum()
                for q in range(4):
                    cc = g3 * 4 + q
                    for j in range(4):
                        self.MM(ps[0:n, q * 128:(q + 1) * 128], xp[:, cc, o + j: o + j + n], self.diag[:, j * 16 + cc, :], j == 0, False,
                                r_xp[cc] + [self.r_diag], [rp])
                    self.MM(ps[0:n, q * 128:(q + 1) * 128], self.onesb[0:1, 0:n], self.cbrow[0:1, cc * 128:(cc + 1) * 128], False, True,
                            [self.r_onesb, self.r_cbrow], [rp])
                if g3 < 2:
                    self.ACT(xst[0:n, bi, g3 * 512:(g3 + 1) * 512], ps[0:n, :], AF.Silu, [rp], r_xst[bi])
                else:
                    self.ACT(Bt[0:n, bi, :], ps[0:n, :], AF.Silu, [rp], r_Bt[bi])
        n0 = segs[0][1]
        ncs = 12 if pre else 16
        self.CP("pool", self.xpctx[:, 0:ncs, :], xp[:, 0:ncs, so[0] + n0: so[0] + n0 + 3], sum(r_xp[0:ncs], []), [self.r_xpctx])
        if KSUB <= 5:
            return
        if not pre:
            self.attention(tile, Q, r_Q, mix, r_mix)
        for bi, (c0, n, bk) in enumerate(blocks):
            self.ssd_block(tile, bi, c0, n, bk, xst, r_xst, Bt, r_Bt, BCT, r_BCT, zb, r_z, dtb, r_dtb, mix, r_mix)
        if pre:
            return
        def evac_out(ci, ps, rp):
            for (c0, n, s) in segs:
                self.STT(self.xres[:, ci, c0:c0 + n], ps[:, c0:c0 + n], self.m_gt(0, 0, ci, s), self.xres[:, ci, c0:c0 + n],
                         ALU.mult, ALU.add, [rp, self.r_modT, self.r_xres[ci]], [self.r_xres[ci]])
            self.stat_chunk(ci, T)

        self.proj_fm("w_out", I["w_out"], D, 16, mix, r_mix, T, evac_out)

    def load_ctx_fm(self, src, nrows, dst, r_dst, coloff):
        for qq in range(4):
            stg, r_stg = self.gtmp()
            self.P.dma("sp", stg[0:nrows, :], src[:, qq * 512:(qq + 1) * 512], writes=[r_stg])
            ps, rp = self.psum()
            for q in range(4):
                self.TR(ps[:, q * 32: q * 32 + nrows], stg[0:nrows, q * 128:(q + 1) * 128], self.ident[0:nrows, 0:nrows],
                        [r_stg, self.r_ident], [rp])
            for q in range(4):
                cc = qq * 4 + q
                self.CP("dve", dst[:, cc, coloff:coloff + nrows], ps[:, q * 32:q * 32 + nrows], [rp], r_dst[cc])

    def k_store(self, tile, j, t1, r1):
        segs = tile["segs"]
        c0, n, s = segs[0]
        self.CP("act", self.Kbuf[:, j, 128:128 + n], t1[:, 0:n], [r1], [self.r_K])
        if tile["kind"] == "halo":
            self.CP("act", tile["Kown"][:, j, :], t1[:, 256:320], [r1], tile["r_Kown"])
            for si in range(1, 5):
                c0, n, s = segs[si]
                ps, rp = self.psum()
                self.TR(ps[0:16, 0:64], t1[0:64, c0:c0 + 16], self.ident[0:64, 0:64], [r1, self.r_ident], [rp])
                self.CP("dve", tile["kso"][0:16, si - 1, j * 64:(j + 1) * 64], ps[0:16, 0:64], [rp], tile["r_kso"])
        if tile.get("last"):
            ps, rp = self.psum()
            self.TR(ps[:, 0:64], t1[0:64, 384:512], self.ident[0:64, 0:64], [r1, self.r_ident], [rp])
            self.CP("dve", tile["kpo"][:, j * 64:(j + 1) * 64], ps[:, 0:64], [rp], tile["r_kpo"])

    def v_store(self, tile, bi, ps, rp):
        c0, n, bk = tile["blocks"][bi]
        if bk["seg"] == 0:
            self.CP("act", self.Vbuf[0:n, 1 + bi, :], ps[0:n, 0:256], [rp], [self.r_V])
            if tile.get("last") and bi == 3:
                self.CP("dve", tile["vpo"][:, :], ps[:, 0:256], [rp], tile["r_vpo"])
        else:
            s = bk["seg"] - 1
            self.CP("act", tile["Vsmp"][0:16, s, :], ps[0:16, 0:256], [rp], tile["r_Vsmp"])
            self.CP("dve", tile["vso"][0:16, s, :], ps[0:16, 0:256], [rp], tile["r_vso"])

    def attn_group(self, Q, r_Q, qc0, nq, ktiles, mix, r_mix, mc0):
        KBY = 1024
        PT = self.av(8 * KBY, [2, 4, 64]); r_PT = self.ares(8 * KBY, 1024)
        nkt = len(ktiles)
        for j in range(4):
            pss = [self.psum(), self.psum()]
            for kt, (Kfn, Vfn, nk, p0, kp, bias, rd) in enumerate(ktiles):
                for g in range(4):
                    h = 4 * j + g
                    half = h % 2
                    ps, rp = pss[half]
                    self.MM(ps[p0:p0 + nk, kt * 128 + (g // 2) * 64: kt * 128 + (g // 2) * 64 + nq], Kfn(j, half),
                            Q[half * 64:(half + 1) * 64, h // 2, qc0:qc0 + nq], True, True, rd + [r_Q[h // 2]], [rp])
            for kt, (Kfn, Vfn, nk, p0, kp, bias, rd) in enumerate(ktiles):
                if nk < kp:
                    self.MEMSET("pool", PT[:, kt, :, :], 0.0, r_PT)
                for half in range(2):
                    ps, rp = pss[half]
                    src = ps[p0:p0 + nk, kt * 128:(kt + 1) * 128].rearrange("p (g q) -> p g q", g=2)[:, :, 0:nq]
                    dst = PT[p0:p0 + nk, kt, :, :].rearrange("p (gg hh) q -> p gg hh q", hh=2)[:, :, half, 0:nq]
                    kw = {"scale": 0.125}
                    rdd = [rp]
                    if bias is not None:
                        kw["bias"] = bias[p0:p0 + nk, :]
                        rdd.append(self.r_flags)
                    self.ACT(dst, src, AF.Exp, rdd, r_PT, **kw)
            psd, rpd = self.psum()
            for kt, (Kfn, Vfn, nk, p0, kp, bias, rd) in enumerate(ktiles):
                self.MM(psd[:, 0:4 * nq].rearrange("p (g q) -> p g q", g=4), self.onesb[0:kp, :], PT[0:kp, kt, :, 0:nq],
                        kt == 0, kt == nkt - 1, [self.r_onesb] + r_PT, [rpd])
            pso, rpo = self.psum()
            for g in range(4):
                ph = (g % 2) * 64
                for kt, (Kfn, Vfn, nk, p0, kp, bias, rd) in enumerate(ktiles):
                    self.MM(pso[ph:ph + 64, (g // 2) * 64:(g // 2) * 64 + nq], Vfn(j), PT[0:kp, kt, g, 0:nq],
                            kt == 0, kt == nkt - 1, rd + r_PT, [rpo])
            den, rden = self.gtmp()
            self.TT("dve", den[:, 0:4 * nq].rearrange("p (g q) -> p g q", g=4), psd[:, 0:4 * nq].rearrange("p (g q) -> p g q", g=4),
                    self.rowc[:, 3, 4 * j:4 * j + 4].unsqueeze(2).to_broadcast([128, 4, nq]), ALU.add, [rpd, self.r_rowc], [rden])
            self.RECIP(den[:, 0:4 * nq], den[:, 0:4 * nq], [rden], [rden])
            for g in range(4):
                ph = (g % 2) * 64
                cc = 2 * j + g // 2
                self.TT("dve", mix[ph:ph + 64, cc, mc0:mc0 + nq], pso[ph:ph + 64, (g // 2) * 64:(g // 2) * 64 + nq],
                        den[ph:ph + 64, g * nq:(g + 1) * nq], ALU.mult, [rpo, rden], [r_mix[cc]])

    def attention(self, tile, Q, r_Q, mix, r_mix):
        P, I = self.P, self.I
        segs = tile["segs"]
        c0, n, s = segs[0]
        first_main = tile.get("first_main", False)
        for qi in range(n // 64):
            qc = qi * 64
            kts = []
            lo = qc - 128
            pieces = [(lo, 128), (lo + 128, 64)] if lo % 128 == 0 else [(lo, 64), (lo + 64, 128)]
            for (k0, nk) in pieces:
                blk = (k0 + 128) // 128
                p0 = (k0 + 128) % 128
                bias = self.flags[:, 1:2] if (first_main and k0 < 0) else None
                Kfn = (lambda j, half, k0=k0, nk=nk: self.Kbuf[half * 64:(half + 1) * 64, j, 128 + k0:128 + k0 + nk])
                Vfn = (lambda j, blk=blk: self.Vbuf[:, blk, j * 64:(j + 1) * 64])
                kts.append((Kfn, Vfn, nk, p0, 128, bias, [self.r_K, self.r_V]))
            self.attn_group(Q, r_Q, c0 + qc, 64, kts, mix, r_mix, c0 + qc)
        if tile["kind"] == "halo":
            Ks, Vc, Vo, Kown = tile["Ksmp"], tile["Vcache"], tile["Vsmp"], tile["Kown"]
            for si in range(1, 5):
                c0, n, s = segs[si]
                sq = si - 1
                stg, r_stg = self.gtmp()
                P.dma("sp", stg[:, :], I["ck"][sq], writes=[r_stg])
                ps, rp = self.psum()
                for j in range(4):
                    self.TR(ps[:, j * 128:(j + 1) * 128], stg[:, j * 128:(j + 1) * 128], self.ident[:], [r_stg, self.r_ident], [rp])
                self.CP("dve", Ks[:, :, 0:128], ps[:, :].rearrange("p (j t) -> p j t", j=4), [rp], tile["r_Ksmp"])
                self.CP("pool", Ks[:, :, 128:144], Kown[:, :, sq * 16:(sq + 1) * 16], tile["r_Kown"], tile["r_Ksmp"])
                stg2, r_stg2 = self.gtmp()
                P.dma("sp", stg2[:, 0:256], I["cv"][sq], writes=[r_stg2])
                self.CP("dve", Vc[:, :], stg2[:, 0:256], [r_stg2], tile["r_Vcache"])
                kts = [
                    ((lambda j, half: Ks[half * 64:(half + 1) * 64, j, 0:128]),
                     (lambda j: Vc[:, j * 64:(j + 1) * 64]), 128, 0, 128, None, tile["r_Ksmp"] + tile["r_Vcache"]),
                    ((lambda j, half: Ks[half * 64:(half + 1) * 64, j, 128:144]),
                     (lambda j, sq=sq: Vo[0:16, sq, j * 64:(j + 1) * 64]), 16, 0, 16, None, tile["r_Ksmp"] + tile["r_Vsmp"]),
                ]
                self.attn_group(Q, r_Q, c0, 16, kts, mix, r_mix, c0)
        c0, n, s = segs[0]
        self.CP("pool", self.Kbuf[:, :, 0:128], self.Kbuf[:, :, n:n + 128], [self.r_K], [self.r_K])
        self.CP("pool", self.Vbuf[:, 0, :], self.Vbuf[:, n // 128, :], [self.r_V], [self.r_V])

    def ssd_block(self, tile, bi, c0, L, bk, xst, r_xst, Bt, r_Bt, BCT, r_BCT, zb, r_z, dtb, r_dtb, mix, r_mix):
        pre = tile["kind"] == "pre"
        KBY = 1024
        base = 8 * KBY
        if bk["seg"] == 0:
            hT, r_hT, hTb, r_hTb = self.hT, [self.r_hT], self.hTb, [self.r_hTb]
        else:
            sq = bk["seg"] - 1
            hT, r_hT = tile["hTs"], tile["r_hTs"]
            hTb, r_hTb = tile["hTsb"], tile["r_hTsb"]
            for g2 in range(2):
                stg3, r_stg3 = self.gtmp()
                self.P.dma("sp", stg3[:, :].rearrange("p (a n) -> p a n", a=4),
                           self.I["st_ssm"][sq].rearrange("(a p) n -> p a n", p=128)[:, g2 * 4:(g2 + 1) * 4, :], writes=[r_stg3])
                ps, rp = self.psum()
                for q in range(4):
                    self.TR(ps[:, q * 128:(q + 1) * 128], stg3[:, q * 128:(q + 1) * 128], self.ident[:], [r_stg3, self.r_ident], [rp])
                self.CP("dve", hT[:, g2 * 512:(g2 + 1) * 512], ps[:, :], [rp], r_hT)
                self.CP("act", hTb[:, g2 * 512:(g2 + 1) * 512], ps[:, :], [rp], r_hTb)
        bufA = self.av(base + 1 * KBY, [4, 128], F32); rA = self.ares(base + 1 * KBY, 2048)
        bufB = self.av(base + 3 * KBY, [4, 128], F32); rB = self.ares(base + 3 * KBY, 2048)
        CBm = self.av(base + 5 * KBY, [128], F32); rCB = self.ares(base + 5 * KBY, 512)
        GT = self.av(base + 6 * KBY, [4, 128]); rG = self.ares(base + 6 * KBY, 1024)
        CsT = self.av(base + 7 * KBY, [4, 128]); rCs = self.ares(base + 7 * KBY, 1024)
        xdt = self.av(base + 8 * KBY, [4, 64]); rxdt = self.ares(base + 8 * KBY, 512)
        xdte = self.av(base + 8 * KBY + 512, [4, 64]); rxdte = self.ares(base + 8 * KBY + 512, 512)
        ysb = self.av(base + 9 * KBY, [1024], F32); rys = self.ares(base + 9 * KBY, 4096)
        ynb = self.av(base + 13 * KBY, [1024]); ryn = self.ares(base + 13 * KBY, 2048)
        dt = dtb[0:L, bi, 0, :]
        dA = dtb[0:L, bi, 1, :]
        acum, racum = self.gsm()
        ps, rp = self.psum()
        self.MM(ps[0:L, 0:16], self.tri[0:L, 0:L], dA, True, True, [self.r_tri] + r_dtb, [rp])
        self.MM(ps[:, 16:32], self.onesf[0:L, :], dA, True, True, [self.r_onesf] + r_dtb, [rp])
        self.CP("act", acum[0:L, 0:16], ps[0:L, 0:16], [rp], [racum])
        tot = acum[:, 16:32]
        self.CP("act", tot, ps[:, 16:32], [rp], [racum])
        te = acum[:, 32:48]
        self.TT("dve", te[0:L, :], tot[0:L, :], acum[0:L, 0:16], ALU.subtract, [racum], [racum])
        self.ACT(te[0:L, :], te[0:L, :], AF.Exp, [racum], [racum])
        self.TT("dve", te[0:L, :], te[0:L, :], dt, ALU.mult, [racum] + r_dtb, [racum])
        dec = acum[:, 48:64]
        self.ACT(dec, tot, AF.Exp, [racum], [racum])
        psy = [None, None]
        for g in range(4):
            xs_g = xst[0:L, bi, g * 256:(g + 1) * 256].rearrange("p (r d) -> p r d", r=4)
            if not pre:
                self.TT("dve", bufA[0:L, :, 0:L], self.tri[0:L, 0:L].unsqueeze(1).to_broadcast([L, 4, L]),
                        dA[:, 4 * g:4 * g + 4].unsqueeze(2).to_broadcast([L, 4, L]), ALU.mult, [self.r_tri] + r_dtb, rA)
                ps, rp = self.psum()
                for r in range(4):
                    self.MM(ps[:, r * L:(r + 1) * L], self.onesf[0:L, :], bufA[0:L, r, 0:L], True, True, [self.r_onesf] + rA, [rp])
                self.CP("act", bufB[:, :, 0:L], ps[:, 0:4 * L].rearrange("p (r l) -> p r l", r=4), [rp], rB)
                self.TT("dve", bufA[0:L, :, 0:L], bufB[0:L, :, 0:L], acum[0:L, 4 * g:4 * g + 4].unsqueeze(2).to_broadcast([L, 4, L]),
                        ALU.subtract, rB + [racum], rA)
                self.TS("dve", bufA[0:L, :, 0:L], bufA[0:L, :, 0:L], 0.0, None, ALU.min, None, rA, rA)
                self.ACT(bufA[0:L, :, 0:L], bufA[0:L, :, 0:L], AF.Exp, rA, rA)
                ps2, rp2 = self.psum()
                self.MM(ps2[0:L, 0:L], BCT[:, g, c0:c0 + L], BCT[:, 4 + g, c0:c0 + L], True, True, [r_BCT[g], r_BCT[4 + g]], [rp2])
                self.TT("dve", CBm[0:L, 0:L], ps2[0:L, 0:L], self.tri[0:L, 0:L], ALU.mult, [rp2, self.r_tri], rCB)
                self.TT("dve", GT[0:L, :, 0:L], bufA[0:L, :, 0:L], CBm[0:L, 0:L].unsqueeze(1).to_broadcast([L, 4, L]), ALU.mult,
                        rA + rCB, rG)
                self.ACT(bufB[:, :, 0:L], bufB[:, :, 0:L], AF.Exp, rB, rB)
                self.TT("dve", CsT[:, :, 0:L], bufB[:, :, 0:L], BCT[:, 4 + g, c0:c0 + L].unsqueeze(1).to_broadcast([128, 4, L]), ALU.mult,
                        rB + [r_BCT[4 + g]], rCs)
                self.TT("dve", xdt[0:L, :, :], xs_g, dt[:, 4 * g:4 * g + 4].unsqueeze(2).to_broadcast([L, 4, 64]), ALU.mult,
                        r_xst[bi] + r_dtb, rxdt)
                if g % 2 == 0:
                    psy[g // 2] = (self.ps[6 + g // 2], self.r_ps[6 + g // 2])
                py, rpy = psy[g // 2]
                for r in range(4):
                    h = 4 * g + r
                    col = (h % 8) * 64
                    self.MM(py[0:L, col:col + 64], GT[0:L, r, 0:L], xdt[0:L, r, :], True, False, rG + rxdt, [rpy])
                    self.MM(py[0:L, col:col + 64], CsT[:, r, 0:L], hTb[:, h * 64:(h + 1) * 64], False, True, rCs + r_hTb, [rpy])
            self.TT("dve", xdte[0:L, :, :], xs_g, te[0:L, 4 * g:4 * g + 4].unsqueeze(2).to_broadcast([L, 4, 64]), ALU.mult,
                    r_xst[bi] + [racum], rxdte)
            ps3, rp3 = self.psum()
            self.MM(ps3[:, 0:256], Bt[0:L, bi, g * 128:(g + 1) * 128], xdte[0:L, :, :].rearrange("p r d -> p (r d)"), True, True,
                    r_Bt[bi] + rxdte, [rp3])
            hg = hT[:, g * 256:(g + 1) * 256]
            self.TT("dve", hg.rearrange("p (r d) -> p r d", r=4), hg.rearrange("p (r d) -> p r d", r=4),
                    dec[:, 4 * g:4 * g + 4].unsqueeze(2).to_broadcast([128, 4, 64]), ALU.mult, r_hT + [racum], r_hT)
            self.TT("dve", hg, hg, ps3[:, 0:256], ALU.add, r_hT + [rp3], r_hT)
            self.CP("act", hTb[:, g * 256:(g + 1) * 256], hg, r_hT, r_hTb)
        if bk["seg"] > 0:
            self.out_state(hT, r_hT, self.O["hst_s"][bk["seg"] - 1])
        if pre:
            return
        Dx = self.av(base + 1 * KBY, [1024], F32); rDx = self.ares(base + 1 * KBY, 4096)
        self.TT("dve", Dx[0:L, :].rearrange("p (h d) -> p h d", h=16), xst[0:L, bi, :].rearrange("p (h d) -> p h d", h=16),
                self.rowc[0:L, 2, :].unsqueeze(2).to_broadcast([L, 16, 64]), ALU.mult, r_xst[bi] + [self.r_rowc], rDx)
        for hh in range(2):
            py, rpy = psy[hh]
            self.TT("dve", ysb[0:L, hh * 512:(hh + 1) * 512], py[0:L, :], Dx[0:L, hh * 512:(hh + 1) * 512], ALU.add, [rpy] + rDx, rys)
        self.TT("dve", ysb[0:L, :], ysb[0:L, :], zb[0:L, bi, :], ALU.mult, rys + r_z[bi], rys)
        ssq, rssq = self.gsm()
        self.MEMSET("pool", ssq[:, 0:4], 0.0, [rssq])
        for g in range(4):
            self.ACT(Dx[0:L, g * 256:(g + 1) * 256], ysb[0:L, g * 256:(g + 1) * 256], AF.Square, rys + [rssq], rDx + [rssq],
                     accum_out=ssq[0:L, g:g + 1])
        self.P.op("act", lambda e: e.activation(out=ssq[0:L, 0:4], in_=ssq[0:L, 0:4], func=AF.Sqrt, scale=1.0 / 256, bias=EPS),
                  rDx + [rssq], [rssq])
        self.RECIP(ssq[0:L, 0:4], ssq[0:L, 0:4], [rssq], [rssq])
        self.TT("dve", ynb[0:L, :].rearrange("p (g d) -> p g d", g=4), ysb[0:L, :].rearrange("p (g d) -> p g d", g=4),
                ssq[0:L, 0:4].unsqueeze(2).to_broadcast([L, 4, 256]), ALU.mult, rys + [rssq], ryn)
        c = self.cst
        for hh in range(2):
            ps, rp = self.psum()
            pb = ps[:].bitcast(BF16)
            for q in range(4):
                cc = hh * 4 + q
                self.TR(pb[:, q * 128:q * 128 + L], ynb[0:L, cc * 128:(cc + 1) * 128], self.identb[0:L, 0:L], ryn + [self.r_identb], [rp])
            for q in range(4):
                cc = hh * 4 + q
                self.ACT(mix[:, 8 + cc, c0:c0 + L], pb[:, q * 128:q * 128 + L], AF.Identity, [rp, self.r_cst], [r_mix[8 + cc]],
                         scale=c[:, self.C_NG + cc:self.C_NG + cc + 1])

    def mlp(self, tile, l):
        P, I = self.P, self.I
        T, segs = tile["T"], tile["segs"]
        KB = 1024
        hn = self.av(0, [16, 512]); r_hn = [self.ares(c * KB, KB)[0] for c in range(16)]
        hid = self.av(16 * KB, [32, 512]); r_hid = [self.ares(16 * KB + c * KB, KB)[0] for c in range(32)]
        self.norm(T, segs, l, 1, hn, r_hn)
        for half in range(2):
            def evac_up(ci, ps, rp):
                t, rt = self.gtmp()
                self.ACT(t[:, 0:T], ps[:, 0:T], AF.Relu, [rp], [rt])
                self.TT("pool" if ci % 2 else "dve", hid[:, ci, 0:T], t[:, 0:T], t[:, 0:T], ALU.mult, [rt], [r_hid[ci]])

            self.proj_fm("w_up%d" % l, I["mlp_w_up"][l], DFF, 16, hn, r_hn, T, evac_up, col0=half * 4096, ncols=4096,
                         nunits=32, ubase=half * 16)

            def evac_dn(ci, ps, rp, half=half):
                for (c0, n, s) in segs:
                    self.STT(self.xres[:, ci, c0:c0 + n], ps[:, c0:c0 + n], self.m_gt(l, 1, ci, s), self.xres[:, ci, c0:c0 + n],
                             ALU.mult, ALU.add, [rp, self.r_modT, self.r_xres[ci]], [self.r_xres[ci]])
                if half == 1:
                    self.stat_chunk(ci, T)

            self.proj_fm("w_dn%d" % l, I["mlp_w_down"][l], D, 32, hid, r_hid, T, evac_dn, kbase=half * 32, nunits=32,
                         ubase=half * 16, unit_cols=128)

    def l1_conf(self, tile):
        P, I = self.P, self.I
        T, segs, kind = tile["T"], tile["segs"], tile["kind"]
        KB = 1024
        c = self.cst
        hn = self.av(0, [16, 512]); r_hn = [self.ares(cc * KB, KB)[0] for cc in range(16)]
        UW = tile["uw"]
        uo = tile["uoff"]
        UB = 16 * KB
        u = self.av(UB, [16, UW]); r_u = [self.ares(UB + cc * UW * 2, UW * 2) for cc in range(16)]
        VB = 34 * KB
        v = self.av(VB, [16, 512], F32); r_v = [self.ares(VB + cc * 2 * KB, 2 * KB) for cc in range(16)]
        hs = hn; r_hs = r_hn
        self.norm(T, segs, 1, 0, hn, r_hn)
        self.CP("pool", u[:, :, uo[0]:uo[0] + 30], self.uctx[:], [self.r_uctx], sum(r_u, []))
        if kind == "halo":
            for s in range(4):
                self.load_ctx_fm(I["st_cconv"][s], 30, u, r_u, uo[1 + s])
        st = {}

        def evac_w1(ci, ps, rp):
            cc, h = ci // 2, ci % 2
            if h == 0:
                st["a"] = (ps, rp)
            else:
                pa, rpa = st["a"]
                t, rt = self.gtmp()
                self.ACT(t[:, 0:T], ps[:, 0:T], AF.Sigmoid, [rp, self.r_cst], [rt], bias=c[:, self.C_B1 + 2 * cc + 1: self.C_B1 + 2 * cc + 2])
                for si, (c0, n, s) in enumerate(segs):
                    self.STT(u[:, cc, uo[si] + 30: uo[si] + 30 + n], pa[:, c0:c0 + n], c[:, self.C_B1 + 2 * cc: self.C_B1 + 2 * cc + 1],
                             t[:, c0:c0 + n], ALU.add, ALU.mult, [rpa, rt, self.r_cst], r_u[cc])
                    if si == 0:
                        self.STT(self.uctx[:, cc, :], pa[:, c0 + n - 30:c0 + n], c[:, self.C_B1 + 2 * cc: self.C_B1 + 2 * cc + 1],
                                 t[:, c0 + n - 30:c0 + n], ALU.add, ALU.mult, [rpa, rt, self.r_cst], [self.r_uctx])
                    else:
                        self.STT(tile["cconvf"][:, cc, (si - 1) * 16: si * 16], pa[:, c0:c0 + 16], c[:, self.C_B1 + 2 * cc: self.C_B1 + 2 * cc + 1],
                                 t[:, c0:c0 + 16], ALU.add, ALU.mult, [rpa, rt, self.r_cst], tile["r_cconvf"])

        self.proj_fm("w1", I["w1_ext"], 2 * D, 16, hn, r_hn, T, evac_w1)
        if P.dry:
            self.proj_fm("w2", I["conf_w2"], D, 16, hs, r_hs, T, None)
            return
        if kind == "halo":
            ranges = [(uo[0], 256, [(0, 0, 256)]), (uo[1], 3 * 46 + 16, None)]
        else:
            ranges = [(uo[0], segs[0][1], [(0, 0, segs[0][1])])]
        for cc in range(16):
            pss = [self.psum() for _ in ranges]
            for j in range(31):
                dg, rdg = self.gdg()
                self.TS("dve", dg, self.identb[:], c[:, self.C_DWW + j * 16 + cc: self.C_DWW + j * 16 + cc + 1], None, ALU.mult, None,
                        [self.r_identb, self.r_cst], [rdg])
                for (o, n, _), (ps, rp) in zip(ranges, pss):
                    self.MM(ps[:, 0:n], dg, u[:, cc, o + j:o + j + n], j == 0, j == 30, [rdg] + r_u[cc], [rp])
            bias = c[:, self.C_DWB + cc: self.C_DWB + cc + 1]
            ps, rp = pss[0]
            n0 = ranges[0][1]
            self.ACT(v[:, cc, 0:n0], ps[:, 0:n0], AF.Identity, [rp, self.r_cst], r_v[cc], bias=bias)
            if kind == "halo":
                ps, rp = pss[1]
                self.ACT(v[:, cc, 256:320].rearrange("p (s t) -> p s t", s=4),
                         ps[:, 0:184].rearrange("p (s w) -> p s w", w=46)[:, :, 0:16], AF.Identity, [rp, self.r_cst], r_v[cc], bias=bias)
        ps1, rp1 = self.psum()
        ps2, rp2 = self.psum()
        for cc in range(16):
            a, ra = self.gsqb()
            self.CP("act", a[:, 0:T], v[:, cc, 0:T], r_v[cc], [ra])
            self.MM(ps1[:, 0:T], self.onesb[:], a[:, 0:T], cc == 0, cc == 15, [self.r_onesb, ra], [rp1])
            b, rb = self.gsqb()
            self.ACT(b[:, 0:T], v[:, cc, 0:T], AF.Square, r_v[cc], [rb])
            self.MM(ps2[:, 0:T], self.onesb[:], b[:, 0:T], cc == 0, cc == 15, [self.r_onesb, rb], [rp2])
        mean, rmean = self.gtmp()
        self.TS("dve", mean[:, 0:T], ps1[:, 0:T], 1.0 / D, None, ALU.mult, None, [rp1], [rmean])
        msq, rmsq = self.gtmp()
        self.TT("dve", msq[:, 0:T], mean[:, 0:T], mean[:, 0:T], ALU.mult, [rmean], [rmsq])
        self.STT(msq[:, 0:T], ps2[:, 0:T], 1.0 / D, msq[:, 0:T], ALU.mult, ALU.subtract, [rp2, rmsq], [rmsq])
        self.ACT(msq[:, 0:T], msq[:, 0:T], AF.Sqrt, [rmsq], [rmsq], bias=EPS)
        self.RECIP(msq[:, 0:T], msq[:, 0:T], [rmsq], [rmsq])
        for cc in range(16):
            self.TT("dve", v[:, cc, 0:T], v[:, cc, 0:T], mean[:, 0:T], ALU.subtract, r_v[cc] + [rmean], r_v[cc])
            self.TT("pool", v[:, cc, 0:T], v[:, cc, 0:T], msq[:, 0:T], ALU.mult, r_v[cc] + [rmsq], r_v[cc])
            self.ACT(hs[:, cc, 0:T], v[:, cc, 0:T], AF.Silu, r_v[cc] + [self.r_cst], [r_hs[cc]],
                     scale=c[:, self.C_LNG + cc:self.C_LNG + cc + 1], bias=c[:, self.C_LNB + cc:self.C_LNB + cc + 1])

        def evac_w2(ci, ps, rp):
            for (c0, n, s) in segs:
                self.STT(self.xres[:, ci, c0:c0 + n], ps[:, c0:c0 + n], self.m_gt(1, 0, ci, s), self.xres[:, ci, c0:c0 + n],
                         ALU.mult, ALU.add, [rp, self.r_modT, self.r_xres[ci]], [self.r_xres[ci]])
                self.TS("dve", self.xres[:, ci, c0:c0 + n], self.xres[:, ci, c0:c0 + n], self.b2g[:, ci, s:s + 1], None, ALU.add, None,
                        [self.r_xres[ci], self.r_b2g], [self.r_xres[ci]])
            self.stat_chunk(ci, T)

        self.proj_fm("w2", I["conf_w2"], D, 16, hs, r_hs, T, evac_w2)

    def out_y(self, tile):
        P = self.P
        T, segs = tile["T"], tile["segs"]
        c = self.cst
        self.norm(T, segs, 0, 0, None, None, final=True)
        for (dst, c0, n) in tile["yout"]:
            pass
        nb = len(tile["yout"])
        stg = [self.av(i * 8192, [2048], F32) for i in range(4)]
        r_stg = [self.ares(i * 8192, 8192) for i in range(4)]
        for g in range(4):
            ts = []
            for q in range(4):
                cc = g * 4 + q
                t, rt = self.gtmp()
                self.STT(t[:, 0:T], self.xres[:, cc, 0:T], c[:, self.C_GFIN + cc:self.C_GFIN + cc + 1], self.rstd[:, 0:T], ALU.mult, ALU.mult,
                         [self.r_xres[cc], self.r_cst, self.r_rstd], [rt])
                ts.append((t, rt))
            for bi, (dst, c0, n) in enumerate(tile["yout"]):
                ps, rp = self.psum()
                for q in range(4):
                    t, rt = ts[q]
                    self.TR(ps[0:n, q * 128:(q + 1) * 128], t[:, c0:c0 + n], self.ident[:], [rt, self.r_ident], [rp])
                self.CP("act" if bi % 2 else "dve", stg[bi % 4][0:n, g * 512:(g + 1) * 512], ps[0:n, :], [rp], r_stg[bi % 4])
        for bi, (dst, c0, n) in enumerate(tile["yout"]):
            P.dma("sp", dst, stg[bi % 4][0:n, :], reads=r_stg[bi % 4])

    def out_fm_rows(self, src_fm, rd, nrows, dst):
        stg = self.av(32 * 1024, [2048], F32)
        r_stg = self.ares(32 * 1024, 8192)
        for g in range(4):
            ps, rp = self.psum()
            for q in range(4):
                cc = g * 4 + q
                self.TR(ps[0:nrows, q * 128:(q + 1) * 128], src_fm[:, cc, :], self.ident[:], rd + [self.r_ident], [rp])
            self.CP("dve", stg[0:nrows, g * 512:(g + 1) * 512], ps[0:nrows, :], [rp], r_stg)
        self.P.dma("sp", dst, stg[0:nrows, :], reads=r_stg)

    def out_state(self, hT, r_hT, dst):
        dv = dst.rearrange("(a p) n -> p a n", p=128)
        for g in range(2):
            stg, r_stg = self.gtmp()
            ps, rp = self.psum()
            for q in range(4):
                self.TR(ps[:, q * 128:(q + 1) * 128], hT[:, (g * 4 + q) * 128:(g * 4 + q + 1) * 128], self.ident[:], r_hT + [self.r_ident], [rp])
            self.CP("dve", stg[:, :], ps[:, :], [rp], [r_stg])
            self.P.dma("sp", dv[:, g * 4:(g + 1) * 4, :], stg[:, :].rearrange("p (a n) -> p a n", a=4), reads=[r_stg])

    def emit(self, P):
        self.P = P
        I, O = self.I, self.O
        self.ps_i = self.tmp_i = self.sqb_i = self.sm_i = self.dg_i = 0
        self.stat_pend = []
        self.stat_n = 0
        self.pt_i = 0
        if P.dry:
            self.wsched = []
        else:
            self.w_issued = self.w_consumed = 0
        self.emit_setup()
        KB = 1024
        KSTOP = int(os.environ.get("KSTOP", "9"))
        if KSTOP < 1:
            if not P.dry:
                P.finish()
            return
        nb = self.npre
        b0 = 0
        while b0 < nb:
            nblk = min(4, nb - b0)
            T = nblk * 128
            tile = dict(kind="pre", T=T, segs=[(0, T, 0)], xpw=3 + T, segoff=[0],
                        blocks=[(i * 128, 128, dict(seg=0)) for i in range(nblk)])
            self.load_x([(I["x_pre"][(b0 + i) * 128:(b0 + i + 1) * 128, :], i * 128, 128) for i in range(nblk)])
            self.l0_mixer(tile)
            b0 += nblk
            if int(os.environ.get("KSUB", "99")) < 99:
                break
        if KSTOP < 2:
            if not P.dry:
                P.finish()
            return
        T = 320
        segs = [(0, 256, 0)] + [(256 + 16 * s, 16, 1 + s) for s in range(4)]
        blocks = [(0, 128, dict(seg=0)), (128, 128, dict(seg=0))] + [(256 + 16 * s, 16, dict(seg=1 + s)) for s in range(4)]
        S = self.S
        tile = dict(kind="halo", T=T, segs=segs, blocks=blocks, rc0=0,
                    xpw=3 + 256 + 4 * 19, segoff=[0] + [259 + 19 * s for s in range(4)],
                    uw=30 + 256 + 4 * 46, uoff=[0] + [286 + 46 * s for s in range(4)])
        if not hasattr(self, "halo_bufs"):
            hb = self.halo_bufs = {}
            hb["Kown"] = S("Kown", [128, 4, 64], BF16)
            hb["Ksmp"] = S("Ksmp", [128, 4, 144], BF16)
            hb["Vcache"] = S("Vcache", [128, 256], BF16)
            hb["Vsmp"] = S("Vsmp", [16, 4, 256], BF16)
            hb["hTs"] = S("hTs", [128, 1024])
            hb["hTsb"] = S("hTsb", [128, 1024], BF16)
            hb["sconvf"] = S("sconvf", [128, 4, 16, 3])
            hb["cconvf"] = S("cconvf", [128, 16, 64])
            for k in list(hb.keys()):
                hb["r_" + k] = [Res(k)]
            hb["kso"] = self.av(53 * KB, [4, 256], F32); hb["r_kso"] = self.ares(53 * KB, 4096)
            hb["vso"] = self.av(57 * KB, [4, 256], F32); hb["r_vso"] = self.ares(57 * KB, 4096)
            hb["kpo"] = self.av(61 * KB, [256], F32); hb["r_kpo"] = self.ares(61 * KB, 1024)
            hb["vpo"] = self.av(62 * KB, [256], F32); hb["r_vpo"] = self.ares(62 * KB, 1024)
            print("SBUF bytes/partition:", self.sb_bytes)
        tile.update(self.halo_bufs)
        hb = self.halo_bufs
        self.load_x([(I["x_main"][0:128, :], 0, 128), (I["x_main"][128:256, :], 128, 128), (I["x_smp"], 256, 64)])
        self.l0_mixer(tile)
        self.mlp(tile, 0)
        self.l1_conf(tile)
        self.mlp(tile, 1)
        tile["yout"] = [(O["y_smp"], 256, 64)]
        self.out_y(tile)
        if not P.dry:
            for sq in range(4):
                self.out_fm_rows(hb["sconvf"][:, sq, :, :], hb["r_sconvf"], 3, O["sconv_s"][sq])
                P.dma("sp", O["cconv_s"][sq, 0:14, :], I["st_cconv"][sq, 16:30, :])
                self.out_fm_rows(hb["cconvf"][:, :, sq * 16:(sq + 1) * 16], hb["r_cconvf"], 16, O["cconv_s"][sq, 14:30, :])
            f = self.flags[:, 0:1]
            self.TS("dve", self.hT[:], self.hT[:], f, None, ALU.mult, None, [self.r_hT, self.r_flags], [self.r_hT])
            self.TS("dve", self.hTb[:], self.hTb[:], f, None, ALU.mult, None, [self.r_hTb, self.r_flags], [self.r_hTb])
            self.TS("dve", self.xpctx[:], self.xpctx[:], f, None, ALU.mult, None, [self.r_xpctx, self.r_flags], [self.r_xpctx])
            self.TS("dve", self.uctx[:], self.uctx[:], f, None, ALU.mult, None, [self.r_uctx, self.r_flags], [self.r_uctx])
        if KSTOP < 3:
            if not P.dry:
                P.finish()
            return
        for ti in range(self.nmain):
            T = 512
            tile = dict(kind="main", T=T, segs=[(0, 512, 0)], blocks=[(i * 128, 128, dict(seg=0)) for i in range(4)],
                        rc0=320 + 512 * ti, xpw=515, segoff=[0], uw=542, uoff=[0], first_main=(ti == 0), last=(ti == self.nmain - 1))
            tile.update(self.halo_bufs)
            base = HALO + ti * 512
            self.load_x([(I["x_main"][base + i * 128: base + (i + 1) * 128, :], i * 128, 128) for i in range(4)])
            self.l0_mixer(tile)
            self.mlp(tile, 0)
            self.l1_conf(tile)
            self.mlp(tile, 1)
            tile["yout"] = [(O["y_main"][ti * 512 + i * 128: ti * 512 + (i + 1) * 128, :], i * 128, 128) for i in range(4)]
            self.out_y(tile)
        if not P.dry:
            self.out_state(self.hT[:], [self.r_hT], O["hst_p"])
            self.out_fm_rows(self.xpctxf[:, :, :], [self.r_xpctxf], 3, O["sconv_p"])
            self.out_fm_rows(self.uctx[:, :, :], [self.r_uctx], 30, O["cconv_p"])
            P.finish()


def build_program(npre, nmain, dbg=()):
    nc = bass.Bass("TRN2", target_bir_lowering=False)
    st = ExitStack()
    K = Kern(nc, st, npre, nmain, dbg)
    dry = Prog(dry=True)
    K.emit(dry)
    P = Prog()
    K.emit(P)
    P.build(nc, st)
    st.close()
    return nc, K, P


def _prep_weights(inp):
    w_in = np.asarray(inp["w_in"][0], np.float32)
    q = w_in[:, 0:1024]
    k = w_in[:, 1024:1280]
    sw = np.concatenate([np.arange(32, 64), np.arange(0, 32)])
    cols = []
    for c in range(8):
        qc = q[:, c * 128:(c + 1) * 128]
        qs = np.concatenate([qc[:, 0:64][:, sw], qc[:, 64:128][:, sw]], axis=1)
        cols += [qc, qs]
    for j in range(4):
        kj = k[:, j * 64:(j + 1) * 64]
        cols += [kj, kj, kj[:, sw], kj[:, sw]]
    cols += [w_in[:, 1280:1536], w_in[:, 1536:2560], w_in[:, 2560:4608], w_in[:, 4608:4624]]
    w_in_ext = np.ascontiguousarray(np.concatenate(cols, axis=1))
    assert w_in_ext.shape[1] == WEXT
    w1 = np.asarray(inp["conf_w1"][0], np.float32)
    b1 = np.asarray(inp["conf_b1"][0], np.float32)
    c1, bb = [], []
    for c in range(16):
        c1 += [w1[:, c * 128:(c + 1) * 128], w1[:, 2048 + c * 128: 2048 + (c + 1) * 128]]
        bb += [b1[c * 128:(c + 1) * 128], b1[2048 + c * 128: 2048 + (c + 1) * 128]]
    w1_ext = np.ascontiguousarray(np.concatenate(c1, axis=1))
    b1_ext = np.ascontiguousarray(np.concatenate(bb))[None, :]
    f = lambda a: np.ascontiguousarray(np.asarray(a, np.float32))
    W = dict(
        w_mod=f(inp["w_mod"]), b_mod=f(inp["b_mod"]), g_mix=f(inp["g_mix"]), g_mlp=f(inp["g_mlp"]),
        w_in_ext=w_in_ext, w_out=f(inp["w_out"][0]), attn_sinks=f(inp["attn_sinks"]), ssm_a_log=f(inp["ssm_a_log"]),
        ssm_dt_bias=f(inp["ssm_dt_bias"]), ssm_d=f(inp["ssm_d"]), ssm_conv_w=f(inp["ssm_conv_w"][0]),
        ssm_conv_b=f(inp["ssm_conv_b"]), ssm_norm_g=f(inp["ssm_norm_g"]), w1_ext=w1_ext, b1_ext=b1_ext,
        conf_dw_w=f(inp["conf_dw_w"][0]), conf_dw_b=f(inp["conf_dw_b"]), conf_ln_g=f(inp["conf_ln_g"]),
        conf_ln_b=f(inp["conf_ln_b"]), conf_w2=f(inp["conf_w2"][0]), conf_b2=f(inp["conf_b2"]),
        mlp_w_up=f(inp["mlp_w_up"]), mlp_w_down=f(inp["mlp_w_down"]), g_final=f(inp["g_final"])[None, :],
    )
    return W


def _rope_tables(pos):
    p = np.arange(128)
    d = (p % 64) % 32
    inv = 10000.0 ** (-d.astype(np.float64) / 32.0)
    ang = inv[:, None] * pos[None, :].astype(np.float64)
    ang32 = (pos[None, :].astype(np.float32) * (10000.0 ** (-(d.astype(np.float32)) / 32.0)).astype(np.float32)[:, None]).astype(np.float32)
    cos = np.cos(ang32.astype(np.float64)).astype(np.float32)
    sin = np.sin(ang32.astype(np.float64)).astype(np.float32)
    sign = np.where((p % 64) < 32, -1.0, 1.0).astype(np.float32)[:, None]
    return np.ascontiguousarray(cos), np.ascontiguousarray(sin * sign)


def run(inp, seq=SEQ, dbg=(), trace=False):
    half = seq // 2
    nmain = half // 512
    npre = (half - HALO) // 128
    nc, K, P = build_program(npre, nmain, dbg)
    W = _prep_weights(inp)
    xp = np.asarray(inp["x_prompt"], np.float32)
    xs = np.asarray(inp["x_sample"], np.float32)
    in_maps = []
    for core in range(NCORES):
        b, hf = core // 2, core % 2
        m = dict(W)
        if hf == 1:
            m["x_pre"] = np.ascontiguousarray(xp[b, 0:max(npre, 1) * 128])
            m["x_main"] = np.ascontiguousarray(xp[b, half - HALO: seq])
            pos0 = half - HALO
            flags = np.tile(np.array([[1.0, 0.0]], np.float32), (128, 1))
        else:
            m["x_pre"] = np.zeros((max(npre, 1) * 128, D), np.float32)
            m["x_main"] = np.ascontiguousarray(np.concatenate([np.zeros((HALO, D), np.float32), xp[b, 0:half]], axis=0))
            pos0 = -HALO
            flags = np.tile(np.array([[0.0, NEG]], np.float32), (128, 1))
        sl = slice(core * 4, core * 4 + 4)
        m["x_smp"] = np.ascontiguousarray(xs[sl].reshape(64, D))
        m["c5"] = np.ascontiguousarray(np.concatenate([np.asarray(inp["c_prompt"], np.float32)[b:b + 1],
                                                       np.asarray(inp["c_sample"], np.float32)[sl]], axis=0))
        ck = np.asarray(inp["cache_swa_k"], np.float32)[0, sl]
        m["ck"] = np.ascontiguousarray(np.stack([ck, ck], axis=3).reshape(4, 128, 512))
        m["cv"] = np.ascontiguousarray(np.asarray(inp["cache_swa_v"], np.float32)[0, sl].reshape(4, 128, 256))
        m["st_ssm"] = np.ascontiguousarray(np.asarray(inp["state_ssm"], np.float32)[0, sl].reshape(4, 1024, 128))
        m["st_sconv"] = np.ascontiguousarray(np.asarray(inp["state_ssm_conv"], np.float32)[0, sl])
        m["st_cconv"] = np.ascontiguousarray(np.asarray(inp["state_conf_conv"], np.float32)[0, sl])
        pos = np.concatenate([pos0 + np.arange(HALO), np.tile(PAST_LEN + np.arange(16), 4),
                              pos0 + HALO + np.arange(half)]).astype(np.float64)
        m["rope_cos"], m["rope_sin"] = _rope_tables(pos)
        m["flags"] = flags
        in_maps.append(m)
    res = run_bass_kernel_spmd(nc, in_maps, core_ids=list(range(NCORES)), **({"trace": True} if trace else {}))
    R = res.results
    B = xp.shape[0]
    y_prompt = np.zeros((B, seq, D), np.float32)
    for core in range(NCORES):
        b, hf = core // 2, core % 2
        y_prompt[b, hf * half:(hf + 1) * half] = R[core]["y_main"]
    y_sample = np.concatenate([R[c]["y_smp"].reshape(4, 16, D) for c in range(NCORES)], axis=0)
    last = [R[2 * b + 1] for b in range(B)]
    swa_k_p = np.stack([r["k_p"].reshape(128, 4, 64) for r in last])[None]
    swa_v_p = np.stack([r["v_p"].reshape(128, 4, 64) for r in last])[None]
    swa_k_s = np.concatenate([R[c]["k_s"].reshape(4, 16, 4, 64) for c in range(NCORES)], axis=0)[None]
    swa_v_s = np.concatenate([R[c]["v_s"].reshape(4, 16, 4, 64) for c in range(NCORES)], axis=0)[None]
    st_p = np.stack([r["hst_p"].reshape(16, 64, 128) for r in last])[None]
    st_s = np.concatenate([R[c]["hst_s"].reshape(4, 16, 64, 128) for c in range(NCORES)], axis=0)[None]
    sc_p = np.stack([r["sconv_p"] for r in last])[None]
    sc_s = np.concatenate([R[c]["sconv_s"] for c in range(NCORES)], axis=0)[None]
    cc_p = np.stack([r["cconv_p"] for r in last])[None]
    cc_s = np.concatenate([R[c]["cconv_s"] for c in range(NCORES)], axis=0)[None]
    outs = (y_prompt, y_sample, swa_k_p, swa_v_p, swa_k_s, swa_v_s, st_p, st_s, sc_p, sc_s, cc_p, cc_s)
    outs = tuple(np.ascontiguousarray(o.astype(np.float32)) for o in outs)
    return outs, res, R


def kernel(**inputs):
    outs, _, _ = run(inputs)
    return outs
```
